# Optimizing a Trainium2 kernel written in Bass

```python
import jax
import jax.numpy as jnp
from jax import lax
import numpy as np

D_MODEL = 1024
BATCH = 4
SEQ = 8192
DEPTH = 2

MEM_LEN = 256
EPS = 1e-6
N_BRANCH = 3

SSD_WIDTH = 2 * D_MODEL
SSD_HEAD_DIM = 64
SSD_HEADS = SSD_WIDTH // SSD_HEAD_DIM
SSD_GROUPS = 8
SSD_STATE = 128
SSD_CONV = 5
SSD_CHUNK = 128
SSD_CONV_DIM = SSD_WIDTH + 2 * SSD_GROUPS * SSD_STATE

GMLP_WIDTH = D_MODEL
GMLP_GROUPS = 8
GMLP_CHUNK = 128

XATTN_HEADS = 4
XATTN_HEAD_DIM = D_MODEL // XATTN_HEADS
XATTN_WIDTH = XATTN_HEADS * XATTN_HEAD_DIM

IN_COLS = SSD_WIDTH + SSD_CONV_DIM + 2 * SSD_HEADS + 3 * GMLP_WIDTH + 2 * XATTN_WIDTH + N_BRANCH * D_MODEL

kernel_name = 'hybrid_ssd_gmlp_memxattn_encoder'


def _rmsnorm(x, g):
    xf = x.astype(jnp.float32)
    y = xf * lax.rsqrt(jnp.mean(xf * xf, axis=-1, keepdims=True) + EPS)
    return (y * g.astype(jnp.float32)).astype(x.dtype)


def _group_rmsnorm(x, g, groups):
    b, L, w = x.shape
    xf = x.astype(jnp.float32).reshape(b, L, groups, w // groups)
    y = xf * lax.rsqrt(jnp.mean(xf * xf, axis=-1, keepdims=True) + EPS)
    return (y.reshape(b, L, w) * g.astype(jnp.float32)).astype(x.dtype)


def _layernorm(x, g, bias):
    xf = x.astype(jnp.float32)
    mu = jnp.mean(xf, axis=-1, keepdims=True)
    xc = xf - mu
    y = xc * lax.rsqrt(jnp.mean(xc * xc, axis=-1, keepdims=True) + EPS)
    return (y * g.astype(jnp.float32) + bias.astype(jnp.float32)).astype(x.dtype)


def _dwconv_centred(x, w, bias):
    c = x.shape[-1]
    y = lax.conv_general_dilated(
        x, w[:, None, :].astype(x.dtype), window_strides=(1,),
        padding=[(SSD_CONV // 2, SSD_CONV // 2)],
        dimension_numbers=('NWC', 'WIO', 'NWC'), feature_group_count=c)
    return y + bias.astype(x.dtype)


def _ssd_scan(xh, dt, a, bm, cm):
    b, L, H, P = xh.shape
    G, N = bm.shape[2], bm.shape[3]
    R = H // G
    Q = SSD_CHUNK
    nc = L // Q
    xdt = (xh.astype(jnp.float32) * dt[..., None]).reshape(b, nc, Q, G, R, P)
    bc = bm.astype(jnp.float32).reshape(b, nc, Q, G, N)
    cc = cm.astype(jnp.float32).reshape(b, nc, Q, G, N)
    cs = jnp.cumsum((dt * a).reshape(b, nc, Q, G, R), axis=2)
    lower = jnp.tril(jnp.ones((Q, Q), dtype=bool))[:, :, None, None]
    seg = cs[:, :, :, None] - cs[:, :, None, :]
    decay = jnp.exp(jnp.where(lower, seg, -jnp.inf))
    scores = jnp.einsum('bctgn,bcsgn->bctsg', cc, bc)
    y_diag = jnp.einsum('bctsgr,bcsgrp->bctgrp', scores[..., None] * decay, xdt)
    to_end = jnp.exp(cs[:, :, -1:] - cs)
    states = jnp.einsum('bcsgn,bcsgrp->bcgrpn', bc, xdt * to_end[..., None])
    chunk_decay = jnp.exp(cs[:, :, -1])

    def step(h, inp):
        s, d = inp
        return h * d[..., None, None] + s, h

    h0 = jnp.zeros((b, G, R, P, N), jnp.float32)
    _, h_prev = lax.scan(step, h0, (jnp.moveaxis(states, 1, 0), jnp.moveaxis(chunk_decay, 1, 0)))
    h_prev = jnp.moveaxis(h_prev, 0, 1)
    y_off = jnp.einsum('bctgn,bcgrpn->bctgrp', cc, h_prev) * jnp.exp(cs)[..., None]
    return (y_diag + y_off).reshape(b, L, H, P)


def _ssd_branch(z, xbc, dt_raw, conv_w, conv_b, dt_bias, a_log, d_skip, norm_g):
    b, L, _ = z.shape
    xbc = jax.nn.silu(_dwconv_centred(xbc, conv_w, conv_b))
    xs, bm, cm = jnp.split(xbc, [SSD_WIDTH, SSD_WIDTH + SSD_GROUPS * SSD_STATE], axis=-1)
    xh = xs.reshape(b, L, SSD_HEADS, SSD_HEAD_DIM)
    bm = bm.reshape(b, L, SSD_GROUPS, SSD_STATE)
    cm = cm.reshape(b, L, SSD_GROUPS, SSD_STATE)
    dt = jax.nn.softplus(dt_raw.astype(jnp.float32).reshape(b, L, 2, SSD_HEADS) + dt_bias.astype(jnp.float32))
    a = -jnp.exp(a_log.astype(jnp.float32))
    y_fwd = _ssd_scan(xh, dt[:, :, 0], a[0], bm, cm)
    y_bwd = jnp.flip(_ssd_scan(jnp.flip(xh, 1), jnp.flip(dt[:, :, 1], 1), a[1],
                               jnp.flip(bm, 1), jnp.flip(cm, 1)), 1)
    y = y_fwd + y_bwd + d_skip.astype(jnp.float32)[:, None] * xh.astype(jnp.float32)
    y = y.reshape(b, L, SSD_WIDTH).astype(z.dtype)
    return _group_rmsnorm(y * jax.nn.silu(z), norm_g, SSD_GROUPS)


def _gmlp_branch(gate, uv, ln_g, ln_b, w_s, b_s):
    b, L, _ = gate.shape
    u, v = jnp.split(jax.nn.gelu(uv), 2, axis=-1)
    v = _layernorm(v, ln_g, ln_b)
    vc = v.reshape(b, L // GMLP_CHUNK, GMLP_CHUNK, GMLP_GROUPS, GMLP_WIDTH // GMLP_GROUPS)
    sv = jnp.einsum('gts,bcsgd->bctgd', w_s, vc) + b_s.T[:, :, None]
    return u * sv.reshape(b, L, GMLP_WIDTH) * jax.nn.silu(gate)


def _xattn_branch(q, gate, mem_n, w_kv):
    b, L, _ = q.shape
    k, v = jnp.split(jnp.einsum('bmd,de->bme', mem_n, w_kv), 2, axis=-1)
    qh = q.reshape(b, L, XATTN_HEADS, XATTN_HEAD_DIM)
    kh = k.reshape(b, MEM_LEN, XATTN_HEADS, XATTN_HEAD_DIM)
    vh = v.reshape(b, MEM_LEN, XATTN_HEADS, XATTN_HEAD_DIM)
    s = jnp.einsum('bqhd,bkhd->bhqk', qh, kh).astype(jnp.float32) * (XATTN_HEAD_DIM ** -0.5)
    p = jax.nn.softmax(s, axis=-1).astype(vh.dtype)
    o = jnp.einsum('bhqk,bkhd->bqhd', p, vh).reshape(b, L, XATTN_WIDTH)
    return o * jax.nn.silu(gate)


def setup_inputs(seed: int = 0) -> dict:
    key = jax.random.key(seed)
    ks = jax.random.split(key, 24)
    f32 = jnp.float32

    def nrm(k, shape, scale):
        return jax.random.normal(k, shape, f32) * scale

    def gain(k, shape):
        return 1.0 + 0.02 * jax.random.normal(k, shape, f32)

    dt0 = jnp.exp(jax.random.uniform(ks[6], (DEPTH, 2, SSD_HEADS), f32, np.log(1e-3), np.log(1e-1)))
    dt_bias = dt0 + jnp.log(-jnp.expm1(-dt0))
    a_log = jnp.log(jax.random.uniform(ks[7], (DEPTH, 2, SSD_HEADS), f32, 1.0, 16.0))
    return {
        'x': nrm(ks[0], (BATCH, SEQ, D_MODEL), 1.0),
        'mem': nrm(ks[1], (BATCH, MEM_LEN, D_MODEL), 1.0),
        'norm_pre_g': gain(ks[2], (DEPTH, D_MODEL)),
        'w_in': nrm(ks[3], (DEPTH, D_MODEL, IN_COLS), D_MODEL ** -0.5),
        'conv_w': nrm(ks[4], (DEPTH, SSD_CONV, SSD_CONV_DIM), SSD_CONV ** -0.5),
        'conv_b': nrm(ks[5], (DEPTH, SSD_CONV_DIM), 0.01),
        'dt_bias': dt_bias,
        'a_log': a_log,
        'd_skip': gain(ks[8], (DEPTH, SSD_HEADS)),
        'ssd_norm_g': gain(ks[9], (DEPTH, SSD_WIDTH)),
        'gmlp_ln_g': gain(ks[10], (DEPTH, GMLP_WIDTH)),
        'gmlp_ln_b': nrm(ks[11], (DEPTH, GMLP_WIDTH), 0.01),
        'w_spatial': nrm(ks[12], (DEPTH, GMLP_GROUPS, GMLP_CHUNK, GMLP_CHUNK), GMLP_CHUNK ** -0.5),
        'b_spatial': gain(ks[13], (DEPTH, GMLP_GROUPS, GMLP_CHUNK)),
        'mem_norm_g': gain(ks[14], (DEPTH, D_MODEL)),
        'w_kv': nrm(ks[15], (DEPTH, D_MODEL, 2 * XATTN_WIDTH), D_MODEL ** -0.5),
        'w_br_ssd': nrm(ks[16], (DEPTH, SSD_WIDTH, D_MODEL), SSD_WIDTH ** -0.5),
        'w_br_gmlp': nrm(ks[17], (DEPTH, GMLP_WIDTH, D_MODEL), GMLP_WIDTH ** -0.5),
        'w_br_xattn': nrm(ks[18], (DEPTH, XATTN_WIDTH, D_MODEL), XATTN_WIDTH ** -0.5),
        'w_out': nrm(ks[19], (DEPTH, D_MODEL, D_MODEL), D_MODEL ** -0.5),
        'norm_post_g': gain(ks[20], (DEPTH, D_MODEL)),
    }


def reference(x, mem, norm_pre_g, w_in, conv_w, conv_b, dt_bias, a_log, d_skip,
              ssd_norm_g, gmlp_ln_g, gmlp_ln_b, w_spatial, b_spatial, mem_norm_g,
              w_kv, w_br_ssd, w_br_gmlp, w_br_xattn, w_out, norm_post_g):
    b, L, _ = x.shape
    sizes = [SSD_WIDTH, SSD_CONV_DIM, 2 * SSD_HEADS, GMLP_WIDTH, 2 * GMLP_WIDTH,
             XATTN_WIDTH, XATTN_WIDTH, N_BRANCH * D_MODEL]
    split_points = np.cumsum(sizes)[:-1].tolist()
    for l in range(DEPTH):
        h = _rmsnorm(x, norm_pre_g[l])
        proj = jnp.einsum('bsd,de->bse', h, w_in[l])
        z, xbc, dt_raw, g_gate, g_uv, xq, x_gate, merge = jnp.split(proj, split_points, axis=-1)
        y_ssd = _ssd_branch(z, xbc, dt_raw, conv_w[l], conv_b[l], dt_bias[l], a_log[l],
                            d_skip[l], ssd_norm_g[l])
        y_gmlp = _gmlp_branch(g_gate, g_uv, gmlp_ln_g[l], gmlp_ln_b[l], w_spatial[l], b_spatial[l])
        y_xattn = _xattn_branch(xq, x_gate, _rmsnorm(mem, mem_norm_g[l]), w_kv[l])
        gates = jax.nn.sigmoid(merge.astype(jnp.float32)).astype(x.dtype).reshape(b, L, N_BRANCH, D_MODEL)
        merged = (gates[:, :, 0] * jnp.einsum('bse,ed->bsd', y_ssd, w_br_ssd[l])
                  + gates[:, :, 1] * jnp.einsum('bse,ed->bsd', y_gmlp, w_br_gmlp[l])
                  + gates[:, :, 2] * jnp.einsum('bse,ed->bsd', y_xattn, w_br_xattn[l]))
        out = jnp.einsum('bsd,de->bse', merged, w_out[l])
        x = x + _rmsnorm(out, norm_post_g[l])
    return x
```

```python
import contextlib
import numpy as np
import concourse.bass as bass
import concourse.mybir as mybir
from concourse.bass_utils import run_bass_kernel_spmd

F32 = mybir.dt.float32
BF16 = mybir.dt.bfloat16
AF = mybir.ActivationFunctionType
ALU = mybir.AluOpType

D = 1024
NCORES = 4
EPS = 1e-6
SAME_ENGINE_WAITS = True
DMA_CAST = True

C_Z, C_XBC, C_DT, C_GG, C_GUV, C_XQ, C_XG, C_MG = 0, 2048, 6144, 6208, 7232, 9280, 10304, 11328
PJ_Z, PJ_GG, PJ_GUV, PJ_XG, PJ_MG, PJ_W = 0, 2048, 3072, 5120, 6144, 9216


class Buf:
    def __init__(self, t, name):
        self.t = t
        self.name = name
        self.w = None
        self.r = {}
        self.dsem = None

    def __getitem__(self, idx):
        return self.t[idx]


class Sched:
    def __init__(self, nc, es):
        self.nc = nc
        self.es = es
        self.eng = {"pe": nc.tensor, "dve": nc.vector, "act": nc.scalar, "pool": nc.gpsimd, "sp": nc.sync}
        self.esem = {}
        self.cnt = {}
        self.seen = {e: {} for e in self.eng}
        self.nsem = 0
        for e in ("pe", "dve", "act", "pool"):
            self.esem[e] = self.new_sem("e_" + e)
            self.cnt[e] = 0
        self.free_dsems = []
        self.all_dsems = []

    def new_sem(self, name):
        self.nsem += 1
        return self.es.enter_context(self.nc.semaphore(name + str(self.nsem)))

    def get_dsem(self):
        if self.free_dsems:
            return self.free_dsems.pop()
        s = [self.new_sem("d"), 0]
        self.all_dsems.append(s)
        return s

    def release(self, bufs):
        for b in bufs:
            if b.dsem is not None:
                self.free_dsems.append(b.dsem)
                b.dsem = None

    def _wait(self, e, tok):
        sem, val, seng = tok
        if seng == e and (e == "pe" or not SAME_ENGINE_WAITS):
            return
        k = id(sem)
        if self.seen[e].get(k, 0) >= val:
            return
        self.eng[e].wait_ge(sem, val)
        self.seen[e][k] = val

    def _deps(self, e, reads, writes):
        for b in reads:
            if b.w is not None:
                self._wait(e, b.w)
        for b in writes:
            if b.w is not None:
                self._wait(e, b.w)
            for tok in b.r.values():
                self._wait(e, tok)

    def op(self, e, fn, reads=(), writes=()):
        self._deps(e, reads, writes)
        if self.cnt[e] >= 30000:
            self.esem[e] = self.new_sem("e_" + e)
            self.cnt[e] = 0
        ins = fn(self.eng[e])
        self.cnt[e] += 1
        ins.then_inc(self.esem[e], 1)
        tok = (self.esem[e], self.cnt[e], e)
        for b in reads:
            b.r[e] = tok
        for b in writes:
            b.w = tok
            b.r = {}
        return ins

    def dma(self, out, in_, sb, reads=(), writes=(), q="sp"):
        self._deps(q, reads, writes)
        if sb.dsem is None:
            sb.dsem = self.get_dsem()
        ins = self.eng[q].dma_start(out=out, in_=in_)
        sb.dsem[1] += 16
        ins.then_inc(sb.dsem[0], 16)
        tok = (sb.dsem[0], sb.dsem[1], "dma")
        for b in reads:
            b.r[id(sb.dsem[0])] = tok
        for b in writes:
            b.w = tok
            b.r = {}

    def barrier(self):
        toks = [(self.esem[e], self.cnt[e], e) for e in self.esem if self.cnt[e] > 0]
        toks += [(s[0], s[1], "dma") for s in self.all_dsems if s[1] > 0]
        for e in self.eng:
            for tok in toks:
                if tok[2] == e:
                    continue
                self._wait(e, tok)


class Ring:
    def __init__(self, bufs):
        self.bufs = bufs
        self.i = 0

    def next(self):
        b = self.bufs[self.i % len(self.bufs)]
        self.i += 1
        return b


def bc3(ap, n):
    p, a = ap.shape
    return ap.unsqueeze(2).broadcast_to([p, a, n])


def build(L, depth, debug=False):
    NCH = L // 128
    HALF = min(L, 2048)
    NHALF = L // HALF
    HCH = HALF // 128
    NTB = HALF // 512
    nc = bass.Bass("TRN2", target_bir_lowering=False)

    def din(name, shape, dt=F32):
        return nc.dram_tensor(name, list(shape), dt, kind="ExternalInput").ap()

    skind = "ExternalOutput" if debug else "Internal"

    def dscr(name, shape, dt):
        return nc.dram_tensor(name, list(shape), dt, kind=skind).ap()

    x_in = din("x", [L, D])
    mem_in = din("mem", [256, D])
    w_in = din("w_in", [depth, D, 14400])
    w_kv = din("w_kv", [depth, D, 2048])
    w_bs = din("w_br_ssd", [depth, 2048, D])
    w_bg = din("w_br_gmlp", [depth, D, D])
    w_bx = din("w_br_xattn", [depth, D, D])
    w_o = din("w_out", [depth, D, D])
    p_row = din("p_row", [depth, 1, 9 * 1024])
    p_convw = din("p_convw", [depth, 128, 32 * 5])
    p_convb = din("p_convb", [depth, 128, 32])
    p_wsT = din("p_wsT", [depth, 128, 8 * 128])
    p_bsT = din("p_bsT", [depth, 128, 8])
    c_ident = din("c_ident", [128, 128])
    c_tri0 = din("c_tri0", [128, 128])
    c_tri1 = din("c_tri1", [128, 128])
    c_esel = din("c_esel", [96, 32 * 128])
    c_nm = din("c_nm", [128, 2 * 512])
    out_d = nc.dram_tensor("out", [L, D], F32, kind="ExternalOutput").ap()

    X1 = dscr("s_x1", [L, D], F32)
    PJ = dscr("s_pj", [L, PJ_W], BF16)
    XBC = dscr("s_xbc", [NCH, 128, 32, 128], BF16)
    XQ = dscr("s_xq", [NCH, 128, 8, 128], BF16)
    DTS = dscr("s_dts", [L, 384], F32)
    CS3 = dscr("s_cs3", [L, 192], BF16)
    XS = dscr("s_xs", [L, 2048], BF16)
    BTM = dscr("s_btm", [L, 1024], BF16)
    Y0 = dscr("s_y0", [L, 2048], F32)
    M0 = dscr("s_m0", [L, D], F32)

    with contextlib.ExitStack() as es:
        S = Sched(nc, es)

        uid = [0]

        def sb(st, name, shape, dt):
            uid[0] += 1
            return Buf(st.enter_context(nc.sbuf_tensor(f"{name}_u{uid[0]}", list(shape), dt)), name)

        def ps(st, name, shape, dt):
            uid[0] += 1
            return Buf(st.enter_context(nc.psum_tensor(f"{name}_u{uid[0]}", list(shape), dt)), name)

        ident = sb(es, "ident", [128, 128], BF16)
        tri0 = sb(es, "tri0", [128, 128], F32)
        tri1 = sb(es, "tri1", [128, 128], F32)
        onesf = sb(es, "onesf", [128, 128], F32)
        onesb = sb(es, "onesb", [128, 2], BF16)
        esel = sb(es, "esel", [96, 32 * 128], BF16)
        negm = sb(es, "negm", [128, 2 * 512], BF16)
        prow = sb(es, "prow", [128, 9 * 1024], F32)
        convw = sb(es, "convw", [128, 160], F32)
        convb = sb(es, "convb", [128, 32], F32)
        wsT = sb(es, "wsT", [128, 1024], BF16)
        bsT = sb(es, "bsT", [128, 8], F32)
        a_bc = sb(es, "a_bc", [128, 64], F32)
        G_PRE, G_SSD, G_LNG, G_LNB, G_MEM, G_POST, G_MISC = 0, 1024, 3072, 4096, 5120, 6144, 7168

        with contextlib.ExitStack() as st:
            tmpf = sb(st, "c_tmpf", [128, 32 * 128], F32)
            S.dma(tmpf[:, 0:128], c_ident[:, :], tmpf, writes=[tmpf])
            S.op("dve", lambda e: e.tensor_copy(out=ident[:], in_=tmpf[:, 0:128]), [tmpf], [ident])
            S.dma(tri0[:], c_tri0[:, :], tri0, writes=[tri0])
            S.dma(tri1[:], c_tri1[:, :], tri1, writes=[tri1])
            S.dma(tmpf[0:96, :], c_esel[:, :], tmpf, writes=[tmpf])
            S.op("dve", lambda e: e.tensor_copy(out=esel[:], in_=tmpf[0:96, :]), [tmpf], [esel])
            S.dma(tmpf[:, 0:1024], c_nm[:, :], tmpf, writes=[tmpf])
            S.op("dve", lambda e: e.tensor_copy(out=negm[:], in_=tmpf[:, 0:1024]), [tmpf], [negm])
            S.op("pool", lambda e: e.memset(onesf[:], 1.0), [], [onesf])
            S.op("pool", lambda e: e.memset(onesb[:], 1.0), [], [onesb])
            S.barrier()
            S.release([tmpf, tri0, tri1])

        for l in range(depth):
            Xsrc = x_in if l == 0 else X1
            Xdst = out_d if l == depth - 1 else X1
            if depth == 1:
                Xdst = out_d

            with contextlib.ExitStack() as st:
                tmpw = sb(st, "p_tmpw", [128, 1024], F32)
                S.dma(prow[:], p_row[l, 0:1, :].partition_broadcast(128), prow, writes=[prow])
                S.dma(convw[:], p_convw[l, :, :], convw, writes=[convw])
                S.dma(convb[:], p_convb[l, :, :], convb, writes=[convb])
                S.dma(bsT[:], p_bsT[l, :, :], bsT, writes=[bsT])
                S.dma(tmpw[:], p_wsT[l, :, :], tmpw, writes=[tmpw])
                S.op("dve", lambda e: e.tensor_copy(out=wsT[:], in_=tmpw[:]), [tmpw], [wsT])
                S.op("act", lambda e: e.activation(out=a_bc[:], in_=prow[:, G_MISC + 64:G_MISC + 128], func=AF.Exp),
                     [prow], [a_bc])
                S.op("dve", lambda e: e.tensor_scalar(out=a_bc[:], in0=a_bc[:], scalar1=-1.0, scalar2=None,
                                                       op0=ALU.mult), [a_bc], [a_bc])
                S.barrier()
                S.release([tmpw, prow, convw, convb, bsT])
            dtb_bc = prow[:, G_MISC:G_MISC + 64]
            dsk_bc = prow[:, G_MISC + 128:G_MISC + 160]

            for hf in range(NHALF):
                hs = hf * HALF
                with contextlib.ExitStack() as st:
                    hT = sb(st, "hT", [128, 8, HALF + 4], BF16)
                    with contextlib.ExitStack() as st1:
                        xin_r = Ring([sb(st1, f"xin{i}", [128, 1024], F32) for i in range(9)])
                        jk_r = Ring([sb(st1, f"jk{i}", [128, 1024], BF16) for i in range(4)])
                        junk = sb(st1, "junk", [128, 1024], BF16)
                        hb_r = Ring([sb(st1, f"hb{i}", [128, 1024], BF16) for i in range(5)])
                        ss_r = Ring([sb(st1, f"ss{i}", [128, 4], F32) for i in range(5)])
                        pT_r = Ring([ps(st1, f"pT{i}", [128, 1024], BF16) for i in range(2)])

                        def norm_load(r0, nr):
                            xin = xin_r.next()
                            S.dma(xin[0:nr, :], Xsrc[r0:r0 + nr, :], xin, writes=[xin])
                            return xin

                        def norm_rows(r0, nr, col0, xin=None):
                            hb = hb_r.next(); ss = ss_r.next(); pT = pT_r.next()
                            if xin is None:
                                xin = norm_load(r0, nr)
                            S.op("act", lambda e: e.activation(out=junk[0:nr, :], in_=xin[0:nr, :], func=AF.Square,
                                                                accum_out=ss[0:nr, 0:1]), [xin], [junk, ss])
                            S.op("dve", lambda e: e.tensor_scalar(out=ss[0:nr, 1:2], in0=ss[0:nr, 0:1], scalar1=1.0 / D,
                                                                   scalar2=EPS, op0=ALU.mult, op1=ALU.add), [ss], [ss])
                            S.op("act", lambda e: e.activation(out=ss[0:nr, 2:3], in_=ss[0:nr, 1:2], func=AF.Ln), [ss], [ss])
                            S.op("act", lambda e: e.activation(out=ss[0:nr, 3:4], in_=ss[0:nr, 2:3], func=AF.Exp, scale=-0.5), [ss], [ss])
                            S.op("dve", lambda e: e.scalar_tensor_tensor(out=hb[0:nr, :], in0=xin[0:nr, :], scalar=ss[0:nr, 3:4],
                                                                          in1=prow[0:nr, G_PRE:G_PRE + 1024], op0=ALU.mult,
                                                                          op1=ALU.mult), [xin, ss, prow], [hb])
                            for k in range(8):
                                S.op("pe", lambda e, k=k: e.transpose(out=pT[:, k * 128:k * 128 + nr],
                                                                      in_=hb[0:nr, k * 128:(k + 1) * 128],
                                                                      identity=ident[0:nr, 0:nr]), [hb, ident], [pT])
                            src = pT[:, :].rearrange("p (k t) -> p k t", t=128)[:, :, 0:nr]
                            S.op("act", lambda e: e.activation(out=hT[:, :, col0:col0 + nr], in_=src, func=AF.Copy),
                                 [pT], [hT])

                        if hs - 2 >= 0:
                            norm_rows(hs - 2, 2, 0)
                        else:
                            S.op("pool", lambda e: e.memset(hT[:, :, 0:2], 0.0), [], [hT])
                        if hs + HALF + 2 <= L:
                            norm_rows(hs + HALF, 2, HALF + 2)
                        else:
                            S.op("pool", lambda e: e.memset(hT[:, :, HALF + 2:HALF + 4], 0.0), [], [hT])
                        NB1 = 4
                        loads = {}

                        def ensure_load(c):
                            if c < HCH and c not in loads:
                                loads[c] = norm_load(hs + c * 128, 128)

                        for c in range(min(NB1, HCH)):
                            ensure_load(c)
                        for b0 in range(0, HCH, NB1):
                            batch = list(range(b0, min(b0 + NB1, HCH)))
                            for c in batch:
                                ensure_load(c + NB1)
                            ctx = {c: dict(xin=loads.pop(c), hb=hb_r.next(), ss=ss_r.next(), jk=jk_r.next()) for c in batch}
                            for c in batch:
                                X = ctx[c]
                                S.op("act", lambda e, X=X: e.activation(out=X["jk"][:], in_=X["xin"][:], func=AF.Square,
                                                                        accum_out=X["ss"][:, 0:1]), [X["xin"]], [X["jk"], X["ss"]])
                            for c in batch:
                                X = ctx[c]
                                S.op("dve", lambda e, X=X: e.tensor_scalar(out=X["ss"][:, 1:2], in0=X["ss"][:, 0:1], scalar1=1.0 / D,
                                                                           scalar2=EPS, op0=ALU.mult, op1=ALU.add), [X["ss"]], [X["ss"]])
                            for c in batch:
                                X = ctx[c]
                                S.op("act", lambda e, X=X: e.activation(out=X["ss"][:, 2:3], in_=X["ss"][:, 1:2], func=AF.Ln), [X["ss"]], [X["ss"]])
                            for c in batch:
                                X = ctx[c]
                                S.op("act", lambda e, X=X: e.activation(out=X["ss"][:, 3:4], in_=X["ss"][:, 2:3], func=AF.Exp, scale=-0.5), [X["ss"]], [X["ss"]])
                            for c in batch:
                                X = ctx[c]
                                S.op("dve", lambda e, X=X: e.scalar_tensor_tensor(out=X["hb"][:], in0=X["xin"][:], scalar=X["ss"][:, 3:4],
                                                                                  in1=prow[:, G_PRE:G_PRE + 1024], op0=ALU.mult,
                                                                                  op1=ALU.mult), [X["xin"], X["ss"], prow], [X["hb"]])
                            for c in batch:
                                X = ctx[c]
                                pT = pT_r.next()
                                col0 = 2 + c * 128
                                for k in range(8):
                                    S.op("pe", lambda e, k=k, X=X, pT=pT: e.transpose(out=pT[:, k * 128:(k + 1) * 128],
                                                                                      in_=X["hb"][:, k * 128:(k + 1) * 128],
                                                                                      identity=ident[:]), [X["hb"], ident], [pT])
                                S.op("act", lambda e, pT=pT, col0=col0: e.activation(out=hT[:, :, col0:col0 + 128],
                                                                                     in_=pT[:, :].rearrange("p (k t) -> p k t", t=128),
                                                                                     func=AF.Copy), [pT], [hT])
                        S.barrier()
                        S.release(xin_r.bufs)

                    with contextlib.ExitStack() as st2:
                        wf_r = Ring([sb(st2, f"wf{i}", [128, 8, 512], F32) for i in range(0 if DMA_CAST else 2)])
                        wb_r = Ring([sb(st2, f"wb{i}", [128, 8, 512], BF16) for i in range(3)])
                        pre_r = Ring([sb(st2, f"pre{i}", [128, HALF + 4], BF16) for i in range(2)])
                        acc_r = Ring([sb(st2, f"acc{i}", [128, HALF], F32) for i in range(2)])
                        xc_r = Ring([sb(st2, f"xc{i}", [128, HALF], BF16) for i in range(2)])
                        stg_r = Ring([sb(st2, f"stg{i}", [128, 512], BF16) for i in range(4)])
                        pm_r = Ring([ps(st2, f"pm{i}", [128, 512], F32) for i in range(4)])
                        pd_r = Ring([ps(st2, f"pd{i}", [128, 256], F32) for i in range(2)])
                        dt_r = Ring([sb(st2, f"dtw{i}", [128, 384 + 384], F32) for i in range(5)])
                        c3_r = Ring([sb(st2, f"c3w{i}", [128, 192], BF16) for i in range(5)])

                        def load_w(col0, ncol):
                            wb = wb_r.next()
                            src = w_in[l, :, col0:col0 + ncol].rearrange("(k p) c -> p k c", p=128)
                            if DMA_CAST:
                                S.dma(wb[:, :, 0:ncol], src, wb, writes=[wb], q="pool")
                            else:
                                wf = wf_r.next()
                                S.dma(wf[:, :, 0:ncol], src, wf, writes=[wf])
                                S.op("act", lambda e: e.activation(out=wb[:, :, 0:ncol], in_=wf[:, :, 0:ncol], func=AF.Copy), [wf], [wb])
                            return wb

                        segs = [(C_Z, PJ_Z, 2048, AF.Silu), (C_GG, PJ_GG, 1024, AF.Silu),
                                (C_GUV, PJ_GUV, 2048, AF.Gelu_apprx_tanh), (C_XG, PJ_XG, 1024, AF.Silu),
                                (C_MG, PJ_MG, 3072, AF.Sigmoid)]
                        wlist = ([(C_XBC + bi * 512, 512) for bi in range(8)] + [(C_XQ + bi * 512, 512) for bi in range(2)]
                                 + [(wc + b0, 512) for (wc, pc, width, fn) in segs for b0 in range(0, width, 512)] + [(C_DT, 64)])
                        wq = []
                        widx = [0]

                        def issue_w():
                            if widx[0] < len(wlist):
                                wq.append(load_w(*wlist[widx[0]]))
                                widx[0] += 1

                        def next_w():
                            if not wq:
                                issue_w()
                            wb_ = wq.pop(0)
                            issue_w()
                            return wb_

                        pending = []

                        def flush_pending():
                            while pending:
                                pending.pop(0)()

                        def store_tile(dst, xc, m):
                            for q in range(0, HCH, 8):
                                nq = min(8, HCH - q)
                                o = dst[hs // 128 + q:hs // 128 + q + nq, :, m, :].rearrange("c p t -> p c t")
                                i_ = xc[:, q * 128:(q + nq) * 128].rearrange("p (c t) -> p c t", t=128)
                                S.dma(o, i_, xc, reads=[xc])

                        def feat_block(col0, kind, mbase):
                            wb = next_w()
                            if kind == "xq":
                                flush_pending()
                            for mi in range(4):
                                m = mbase + mi
                                if kind == "xbc":
                                    pre = pre_r.next()
                                    blocks = [(i * 512, 512) for i in range(NTB)] + [(HALF, 4)]
                                else:
                                    pre = xc_r.next()
                                    blocks = [(i * 512, 512) for i in range(NTB)]
                                for (c0, n) in blocks:
                                    pm = pm_r.next()
                                    hc0 = c0 if kind == "xbc" else c0 + 2
                                    for k in range(8):
                                        S.op("pe", lambda e, k=k: e.matmul(pm[:, 0:n], lhsT=wb[:, k, mi * 128:(mi + 1) * 128],
                                                                           rhs=hT[:, k, hc0:hc0 + n], start=(k == 0), stop=(k == 7)),
                                             [wb, hT], [pm])
                                    S.op("act", lambda e: e.activation(out=pre[:, c0:c0 + n], in_=pm[:, 0:n], func=AF.Copy),
                                         [pm], [pre])
                                if kind == "xbc":
                                    acc = acc_r.next(); xc = xc_r.next()
                                    S.op("act", lambda e: e.activation(out=acc[:], in_=pre[:, 0:HALF], func=AF.Copy,
                                                                        scale=convw[:, m * 5:m * 5 + 1]), [pre, convw], [acc])
                                    flush_pending()
                                    for k in range(1, 5):
                                        S.op("dve", lambda e, k=k: e.scalar_tensor_tensor(
                                            out=acc[:], in0=pre[:, k:k + HALF], scalar=convw[:, m * 5 + k:m * 5 + k + 1],
                                            in1=acc[:], op0=ALU.mult, op1=ALU.add), [pre, convw, acc], [acc])

                                    def fin(acc=acc, xc=xc, m=m):
                                        S.op("act", lambda e: e.activation(out=xc[:], in_=acc[:], func=AF.Silu,
                                                                            bias=convb[:, m:m + 1]), [acc, convb], [xc])
                                        store_tile(XBC, xc, m)
                                    pending.append(fin)
                                else:
                                    store_tile(XQ, pre, m)

                        for bi in range(8):
                            feat_block(C_XBC + bi * 512, "xbc", bi * 4)
                        for bi in range(2):
                            feat_block(C_XQ + bi * 512, "xq", bi * 4)
                        flush_pending()

                        for (wc, pc, width, fn) in segs:
                            for b0 in range(0, width, 512):
                                wb = next_w()
                                for c in range(HCH):
                                    pm = pm_r.next(); stg = stg_r.next()
                                    for k in range(8):
                                        S.op("pe", lambda e, k=k: e.matmul(pm[:, :], lhsT=hT[:, k, 2 + c * 128:2 + (c + 1) * 128],
                                                                           rhs=wb[:, k, :], start=(k == 0), stop=(k == 7)),
                                             [wb, hT], [pm])
                                    S.op("act", lambda e: e.activation(out=stg[:], in_=pm[:, :], func=fn), [pm], [stg])
                                    r0 = hs + c * 128
                                    S.dma(PJ[r0:r0 + 128, pc + b0:pc + b0 + 512], stg[:], stg, reads=[stg])

                        wb = next_w()
                        pdq = Ring(pm_r.bufs + pd_r.bufs)

                        def dt_chunk(c):
                            pd = pdq.next(); dw = dt_r.next(); c3 = c3_r.next()
                            r0 = hs + c * 128
                            for k in range(8):
                                S.op("pe", lambda e, k=k: e.matmul(pd[:, 0:64], lhsT=hT[:, k, 2 + c * 128:2 + (c + 1) * 128],
                                                                   rhs=wb[:, k, 0:64], start=(k == 0), stop=(k == 7)), [wb, hT], [pd])
                            V_, A_, E_, R_ = 384, 448, 512, 576
                            S.op("dve", lambda e: e.tensor_tensor(out=dw[:, V_:V_ + 64], in0=pd[:, 0:64], in1=dtb_bc, op=ALU.add),
                                 [pd, prow], [dw])
                            yield
                            S.op("act", lambda e: e.activation(out=dw[:, A_:A_ + 64], in_=dw[:, V_:V_ + 64], func=AF.Abs), [dw], [dw])
                            yield
                            S.op("act", lambda e: e.activation(out=dw[:, E_:E_ + 64], in_=dw[:, A_:A_ + 64], func=AF.Exp, scale=-1.0),
                                 [dw], [dw])
                            yield
                            S.op("act", lambda e: e.activation(out=dw[:, E_:E_ + 64], in_=dw[:, E_:E_ + 64], func=AF.Ln, bias=1.0),
                                 [dw], [dw])
                            yield
                            S.op("dve", lambda e: e.tensor_scalar(out=dw[:, R_:R_ + 64], in0=dw[:, V_:V_ + 64], scalar1=0.0,
                                                                   scalar2=None, op0=ALU.max), [dw], [dw])
                            yield
                            S.op("dve", lambda e: e.tensor_tensor(out=dw[:, 0:64], in0=dw[:, R_:R_ + 64], in1=dw[:, E_:E_ + 64],
                                                                   op=ALU.add), [dw], [dw])
                            yield
                            S.op("dve", lambda e: e.tensor_tensor(out=dw[:, V_:V_ + 64], in0=dw[:, 0:64], in1=a_bc[:], op=ALU.mult),
                                 [dw, a_bc], [dw])
                            yield
                            S.op("pe", lambda e: e.matmul(pd[:, 64:96], lhsT=tri0[:], rhs=dw[:, V_:V_ + 32], start=True, stop=True),
                                 [tri0, dw], [pd])
                            yield
                            S.op("pe", lambda e: e.matmul(pd[:, 96:128], lhsT=tri1[:], rhs=dw[:, V_ + 32:V_ + 64], start=True, stop=True),
                                 [tri1, dw], [pd])
                            yield
                            S.op("pe", lambda e: e.matmul(pd[:, 128:192], lhsT=onesf[:], rhs=dw[:, V_:V_ + 64], start=True, stop=True),
                                 [onesf, dw], [pd])
                            yield
                            T_ = 640
                            S.op("dve", lambda e: e.tensor_copy(out=dw[:, 320:384], in_=pd[:, 64:128]), [pd], [dw])
                            yield
                            S.op("dve", lambda e: e.tensor_copy(out=dw[:, T_:T_ + 64], in_=pd[:, 128:192]), [pd], [dw])
                            yield
                            L_ = 704
                            S.op("act", lambda e: e.activation(out=dw[:, L_:L_ + 64], in_=dw[:, 0:64], func=AF.Ln), [dw], [dw])
                            yield
                            S.op("dve", lambda e: e.tensor_tensor(out=dw[:, 64:128], in0=dw[:, L_:L_ + 64], in1=dw[:, 320:384],
                                                                   op=ALU.subtract), [dw], [dw])
                            yield
                            S.op("act", lambda e: e.activation(out=dw[:, 128:192], in_=dw[:, 320:384], func=AF.Exp), [dw], [dw])
                            yield
                            S.op("act", lambda e: e.activation(out=dw[:, 256:320], in_=dw[:, T_:T_ + 64], func=AF.Exp), [dw], [dw])
                            yield
                            S.op("dve", lambda e: e.tensor_tensor(out=dw[:, A_:A_ + 64], in0=dw[:, T_:T_ + 64], in1=dw[:, 320:384],
                                                                   op=ALU.subtract), [dw], [dw])
                            yield
                            S.op("act", lambda e: e.activation(out=dw[:, A_:A_ + 64], in_=dw[:, A_:A_ + 64], func=AF.Exp), [dw], [dw])
                            yield
                            S.op("dve", lambda e: e.tensor_tensor(out=dw[:, 192:256], in0=dw[:, A_:A_ + 64], in1=dw[:, 0:64],
                                                                   op=ALU.mult), [dw], [dw])
                            yield
                            c3v = c3[:, :].rearrange("p (d j h) -> p d j h", d=2, j=3)
                            csv = dw[:, 320:384].rearrange("p (d h) -> p d h", d=2)
                            r1 = dw[:, E_:E_ + 64].rearrange("p (d h) -> p d h", d=2)
                            r2 = dw[:, R_:R_ + 64].rearrange("p (d h) -> p d h", d=2)
                            S.op("dve", lambda e: e.tensor_copy(out=c3v[:, :, 0, :], in_=csv), [dw], [c3])
                            yield
                            S.op("dve", lambda e: e.tensor_tensor(out=r1, in0=csv, in1=c3v[:, :, 0, :], op=ALU.subtract), [dw, c3], [dw])
                            yield
                            S.op("dve", lambda e: e.tensor_copy(out=c3v[:, :, 1, :], in_=r1), [dw], [c3])
                            yield
                            S.op("dve", lambda e: e.tensor_tensor(out=r2, in0=r1, in1=c3v[:, :, 1, :], op=ALU.subtract), [dw, c3], [dw])
                            yield
                            S.op("dve", lambda e: e.tensor_copy(out=c3v[:, :, 2, :], in_=r2), [dw], [c3])
                            yield
                            S.dma(DTS[r0:r0 + 128, :], dw[:, 0:384], dw, reads=[dw])
                            yield
                            S.dma(CS3[r0:r0 + 128, :], c3[:], c3, reads=[c3])
                            yield

                        NBD = 4
                        for b0 in range(0, HCH, NBD):
                            gens = [dt_chunk(c) for c in range(b0, min(b0 + NBD, HCH))]
                            while gens:
                                for g_ in list(gens):
                                    try:
                                        next(g_)
                                    except StopIteration:
                                        gens.remove(g_)
                        S.barrier()
                        S.release(wf_r.bufs + wb_r.bufs + xc_r.bufs + stg_r.bufs + dt_r.bufs + c3_r.bufs)

            for d in range(2):
                with contextlib.ExitStack() as st:
                    if d == 1:
                        wbs = sb(st, "wbs", [128, 16, 1024], BF16)
                        with contextlib.ExitStack() as stw:
                            wtmp_r = Ring([sb(stw, f"wtmp{i}", [128, 4, 1024], F32) for i in range(2)])
                            for q in range(4):
                                wt = wtmp_r.next()
                                S.dma(wt[:], w_bs[l, q * 512:(q + 1) * 512, :].rearrange("(k p) c -> p k c", p=128), wt, writes=[wt])
                                S.op("pool", lambda e: e.tensor_copy(out=wbs[:, q * 4:(q + 1) * 4, :], in_=wt[:]), [wt], [wbs])
                            S.barrier()
                            S.release(wtmp_r.bufs)
                    xbw_r = Ring([sb(st, f"xbw{i}", [128, 32, 128], BF16) for i in range(2)])
                    dts_r = Ring([sb(st, f"dts{i}", [128, 384], F32) for i in range(2)])
                    cs3_r = Ring([sb(st, f"cs3{i}", [128, 192], BF16) for i in range(2)])
                    xs_r = Ring([sb(st, f"xs{i}", [128, 2048], BF16) for i in range(2)])
                    bt_r = Ring([sb(st, f"bt{i}", [128, 1024], BF16) for i in range(2)])
                    csT_r = Ring([sb(st, f"csT{i}", [96, 128], BF16) for i in range(2)])
                    scs_r = Ring([sb(st, f"scs{i}", [128, 1024], BF16) for i in range(2)])
                    xde_r = Ring([sb(st, f"xde{i}", [128, 2048], BF16) for i in range(2)])
                    dec_r = Ring([sb(st, f"dec{i}", [128, 4, 128], BF16) for i in range(3)])
                    M_r = Ring([sb(st, f"M{i}", [128, 4, 128], BF16) for i in range(3)])
                    tmp_r = Ring([sb(st, f"tmp{i}", [128, 256], F32) for i in range(3)])
                    yacc_r = Ring([sb(st, f"yacc{i}", [128, 2048], F32) for i in range(2 + d)])
                    for yb in yacc_r.bufs:
                        yb.views = [yb] + [Buf(yb.t, yb.name + f"_v{i}") for i in range(1, 8)]
                    Hf = sb(st, "Hf", [128, 2048], F32)
                    Hb = sb(st, "Hb", [128, 2048], BF16)
                    Hfv = [Hf] + [Buf(Hf.t, f"Hf_v{i}") for i in range(1, 8)]
                    Hbv = [Hb] + [Buf(Hb.t, f"Hb_v{i}") for i in range(1, 8)]
                    p_sc = ps(st, "p_sc", [128, 1024], F32)
                    p_cb_r = Ring([ps(st, f"p_cb{i}", [128, 512], F32) for i in range(2)])
                    pbk = [ps(st, f"pbk{i}", [128, 512], F32) for i in range(3)]
                    p_T = ps(st, "p_T", [128, 1024], BF16)
                    if d == 1:
                        zs_r = Ring([sb(st, f"zs{i}", [128, 2048], BF16) for i in range(3)])
                        g0_r = Ring([sb(st, f"g0{i}", [128, 1024], BF16) for i in range(3)])
                        ynb = sb(st, "ynb", [128, 2048], BF16)
                        ynT = sb(st, "ynT", [128, 16, 128], BF16)
                        gs = sb(st, "gs", [128, 32], F32)
                        m0_r = Ring([sb(st, f"m0{i}", [128, 1024], F32) for i in range(2)])
                    S.op("pool", lambda e: e.memset(Hf[:], 0.0), [], Hfv)
                    S.op("pool", lambda e: e.memset(Hb[:], 0.0), [], Hbv)

                    order = list(range(NCH)) if d == 0 else list(range(NCH - 1, -1, -1))
                    def sweep_loads(c):
                        r0 = c * 128
                        Bn = dict(xbw=xbw_r.next(), dts=dts_r.next(), cs3=cs3_r.next(), xs=xs_r.next(), bt=bt_r.next(), yacc=yacc_r.next())
                        S.dma(Bn["dts"][:], DTS[r0:r0 + 128, :], Bn["dts"], writes=[Bn["dts"]])
                        S.dma(Bn["cs3"][:], CS3[r0:r0 + 128, :], Bn["cs3"], writes=[Bn["cs3"]])
                        if d == 0:
                            S.dma(Bn["xbw"][:], XBC[c, :, :, :], Bn["xbw"], writes=[Bn["xbw"]])
                        else:
                            Bn["zs"] = zs_r.next(); Bn["g0"] = g0_r.next()
                            S.dma(Bn["xbw"][:, 16:32, :], XBC[c, :, 16:32, :], Bn["xbw"], writes=[Bn["xbw"]])
                            S.dma(Bn["xs"][:], XS[r0:r0 + 128, :], Bn["xs"], writes=[Bn["xs"]])
                            S.dma(Bn["bt"][:], BTM[r0:r0 + 128, :], Bn["bt"], writes=[Bn["bt"]])
                            S.dma(Bn["yacc"][:], Y0[r0:r0 + 128, :], Bn["yacc"], writes=Bn["yacc"].views)
                            S.dma(Bn["zs"][:], PJ[r0:r0 + 128, PJ_Z:PJ_Z + 2048], Bn["zs"], writes=[Bn["zs"]])
                            S.dma(Bn["g0"][:], PJ[r0:r0 + 128, PJ_MG:PJ_MG + 1024], Bn["g0"], writes=[Bn["g0"]])
                        return Bn

                    for db in dec_r.bufs:
                        db.views = [db] + [Buf(db.t, db.name + f"_v{i}") for i in range(1, 4)]

                    def prep_a(c, Bn):
                        r0 = c * 128
                        xbw = Bn["xbw"]; dts = Bn["dts"]; cs3 = Bn["cs3"]; xs = Bn["xs"]; bt = Bn["bt"]; yacc = Bn["yacc"]
                        Bn["csT"] = csT_r.next(); Bn["xde"] = xde_r.next()
                        csT = Bn["csT"]; xde = Bn["xde"]
                        if d == 0:
                            for half in range(3):
                                for j in range(8):
                                    S.op("pe", lambda e, j=j: e.transpose(out=p_T[:, j * 128:(j + 1) * 128],
                                                                          in_=xbw[:, half * 8 + j, :], identity=ident[:]),
                                         [xbw, ident], [p_T])
                                dstb = xs[:, half * 1024:(half + 1) * 1024] if half < 2 else bt[:]
                                dbuf = xs if half < 2 else bt
                                S.op("act", lambda e: e.activation(out=dstb, in_=p_T[:, :], func=AF.Copy), [p_T], [dbuf])
                            S.dma(XS[r0:r0 + 128, :], xs[:], xs, reads=[xs])
                            S.dma(BTM[r0:r0 + 128, :], bt[:], bt, reads=[bt])
                        S.op("pe", lambda e: e.transpose(out=p_T[0:96, 0:128], in_=cs3[:, d * 96:(d + 1) * 96], identity=ident[:]),
                             [cs3, ident], [p_T])
                        S.op("act", lambda e: e.activation(out=csT[:], in_=p_T[0:96, 0:128], func=AF.Copy), [p_T], [csT])
                        dtd = dts[:, d * 32:(d + 1) * 32]
                        wend = dts[:, 192 + d * 32:192 + (d + 1) * 32]
                        xs3 = xs[:, :].rearrange("p (h q) -> p h q", q=64)
                        S.op("dve", lambda e: e.tensor_tensor(out=xde[:, :].rearrange("p (h q) -> p h q", q=64), in0=xs3,
                                                               in1=bc3(wend, 64), op=ALU.mult), [xs, dts], [xde])
                        if d == 0:
                            S.op("dve", lambda e: e.tensor_tensor(out=yacc[:, :].rearrange("p (h q) -> p h q", q=64), in0=xs3,
                                                                   in1=bc3(dsk_bc, 64), op=ALU.mult), [xs, prow], yacc.views)

                    def prep_b(c, Bn):
                        xbw = Bn["xbw"]
                        Bn["scs"] = scs_r.next()
                        scs_ = Bn["scs"]
                        for g in range(8):
                            S.op("pe", lambda e, g=g: e.matmul(p_sc[:, g * 128:(g + 1) * 128], lhsT=xbw[:, 16 + g, :],
                                                               rhs=xbw[:, 24 + g, :], start=True, stop=True), [xbw], [p_sc])
                        S.op("act", lambda e: e.activation(out=scs_[:], in_=p_sc[:, :], func=AF.Copy), [p_sc], [scs_])

                    pending_post = []
                    pend = [sweep_loads(order[0])]
                    prep_a(order[0], pend[0])
                    prep_b(order[0], pend[0])
                    for ci, c in enumerate(order):
                        r0 = c * 128
                        Bn = pend.pop(0)
                        xbw = Bn["xbw"]; dts = Bn["dts"]; cs3 = Bn["cs3"]; xs = Bn["xs"]; bt = Bn["bt"]; yacc = Bn["yacc"]
                        csT = Bn["csT"]; xde = Bn["xde"]; scs = Bn["scs"]
                        if d == 1:
                            zs = Bn["zs"]; g0 = Bn["g0"]
                        GB = [None] * 8

                        def stage_a(g):
                            G_ = dict(p_cb=p_cb_r.next(), dec=dec_r.next(), M=M_r.next(), tmp=tmp_r.next())
                            GB[g] = G_
                            p_cb = G_["p_cb"]; dec = G_["dec"]; tmp = G_["tmp"]
                            bkA = pbk[g % 2]
                            p_yo = bkA.t[:, 0:256]; p_st = bkA.t[:, 256:512]
                            gc = slice(g * 256, (g + 1) * 256)
                            S.op("pe", lambda e: e.matmul(p_yo[:, :], lhsT=xbw[:, 24 + g, :], rhs=Hb[:, gc], start=True, stop=True),
                                 [xbw, Hbv[g]], [bkA])
                            S.op("pe", lambda e: e.matmul(p_st[:, :], lhsT=bt[:, g * 128:(g + 1) * 128], rhs=xde[:, gc], start=True, stop=True),
                                 [bt, xde], [bkA])
                            S.op("pe", lambda e: e.matmul(p_cb[:, :], lhsT=ident[:], rhs=negm[:, d * 512:(d + 1) * 512], start=True, stop=False),
                                 [ident, negm], [p_cb])
                            for r in range(4):
                                h = g * 4 + r
                                S.op("pe", lambda e, r=r, h=h: e.matmul(p_cb[:, r * 128:(r + 1) * 128], lhsT=esel[:, h * 128:(h + 1) * 128],
                                                                        rhs=csT[:], start=False, stop=(r == 3)), [esel, csT], [p_cb])
                            for r in range(4):
                                h = g * 4 + r
                                S.op("act", lambda e, r=r, h=h: e.activation(out=dec[:, r, :], in_=p_cb[:, r * 128:(r + 1) * 128],
                                                                             func=AF.Exp, bias=dts[:, 64 + d * 32 + h:64 + d * 32 + h + 1]),
                                     [p_cb, dts], [dec.views[r]])
                            cdec = dts[:, 256 + d * 32 + g * 4:256 + d * 32 + g * 4 + 4]
                            S.op("dve", lambda e: e.tensor_tensor(out=Hf[:, gc].rearrange("p (r q) -> p r q", q=64),
                                                                   in0=Hf[:, gc].rearrange("p (r q) -> p r q", q=64),
                                                                   in1=bc3(cdec, 64), op=ALU.mult), [Hfv[g], dts], [Hfv[g]])
                            ecs = dts[:, 128 + d * 32 + g * 4:128 + d * 32 + g * 4 + 4]
                            S.op("dve", lambda e: e.tensor_tensor(out=tmp[:, :].rearrange("p (r q) -> p r q", q=64),
                                                                   in0=p_yo[:, :].rearrange("p (r q) -> p r q", q=64),
                                                                   in1=bc3(ecs, 64), op=ALU.mult), [bkA, dts], [tmp])
                            S.op("dve", lambda e: e.tensor_tensor(out=Hf[:, gc], in0=p_st[:, :], in1=Hf[:, gc], op=ALU.add),
                                 [bkA, Hfv[g]], [Hfv[g]])
                            S.op("act", lambda e: e.activation(out=Hb[:, gc], in_=Hf[:, gc], func=AF.Copy), [Hfv[g]], [Hbv[g]])
                            S.op("dve", lambda e: e.tensor_tensor(out=yacc[:, gc], in0=yacc[:, gc], in1=tmp[:], op=ALU.add),
                                 [yacc.views[g], tmp], [yacc.views[g]])

                        def stage_b(g):
                            G_ = GB[g]
                            dec = G_["dec"]; M = G_["M"]
                            bky = pbk[2]
                            p_y = bky.t[:, (g % 2) * 256:(g % 2 + 1) * 256]
                            scg = scs[:, g * 128:(g + 1) * 128].unsqueeze(1).broadcast_to([128, 4, 128])
                            S.op("dve", lambda e: e.tensor_tensor(out=M[:], in0=dec[:], in1=scg, op=ALU.mult), dec.views + [scs], [M])
                            for r in range(4):
                                h = g * 4 + r
                                S.op("pe", lambda e, r=r, h=h: e.matmul(p_y[:, r * 64:(r + 1) * 64], lhsT=M[:, r, :],
                                                                        rhs=xs[:, h * 64:(h + 1) * 64], start=True, stop=True), [M, xs], [bky])

                        def stage_c(g):
                            bky = pbk[2]
                            p_y = bky.t[:, (g % 2) * 256:(g % 2 + 1) * 256]
                            gc = slice(g * 256, (g + 1) * 256)
                            S.op("dve", lambda e: e.tensor_tensor(out=yacc[:, gc], in0=p_y[:, :], in1=yacc[:, gc], op=ALU.add),
                                 [bky, yacc.views[g]], [yacc.views[g]])

                        for step in range(10):
                            if step < 8:
                                stage_a(step)
                            if 2 <= step <= 9:
                                stage_c(step - 2)
                            if 1 <= step <= 8:
                                stage_b(step - 1)
                            if step == 1 and ci + 1 < len(order):
                                pend.append(sweep_loads(order[ci + 1]))
                            if step >= 1:
                                for pg in list(pending_post):
                                    try:
                                        next(pg)
                                    except StopIteration:
                                        pending_post.remove(pg)
                            if step == 5 and ci + 1 < len(order):
                                prep_a(order[ci + 1], pend[0])
                        if d == 0:
                            S.dma(Y0[r0:r0 + 128, :], yacc[:], yacc, reads=yacc.views)
                        else:
                            def post(r0=r0, yacc=yacc, zs=zs, g0=g0):
                                m0 = m0_r.next()
                                yz = yacc
                                S.op("dve", lambda e: e.tensor_tensor(out=yz[:], in0=yacc[:], in1=zs[:], op=ALU.mult), yacc.views + [zs], yacc.views)
                                yield
                                for g in range(8):
                                    S.op("act", lambda e, g=g: e.activation(out=ynb[:, g * 256:(g + 1) * 256], in_=yz[:, g * 256:(g + 1) * 256],
                                                                            func=AF.Square, accum_out=gs[:, g:g + 1]), [yz], [ynb, gs])
                                yield
                                S.op("dve", lambda e: e.tensor_scalar(out=gs[:, 8:16], in0=gs[:, 0:8], scalar1=1.0 / 256, scalar2=EPS,
                                                                       op0=ALU.mult, op1=ALU.add), [gs], [gs])
                                S.op("act", lambda e: e.activation(out=gs[:, 16:24], in_=gs[:, 8:16], func=AF.Ln), [gs], [gs])
                                S.op("act", lambda e: e.activation(out=gs[:, 24:32], in_=gs[:, 16:24], func=AF.Exp, scale=-0.5), [gs], [gs])
                                yield
                                for g in range(8):
                                    S.op("dve", lambda e, g=g: e.scalar_tensor_tensor(
                                        out=ynb[:, g * 256:(g + 1) * 256], in0=yz[:, g * 256:(g + 1) * 256], scalar=gs[:, 24 + g:25 + g],
                                        in1=prow[:, G_SSD + g * 256:G_SSD + (g + 1) * 256], op0=ALU.mult, op1=ALU.mult), [yz, gs, prow], [ynb])
                                for half in range(2):
                                    yield
                                    for j in range(8):
                                        S.op("pe", lambda e, j=j: e.transpose(out=p_T[:, j * 128:(j + 1) * 128],
                                                                              in_=ynb[:, (half * 8 + j) * 128:(half * 8 + j + 1) * 128],
                                                                              identity=ident[:]), [ynb, ident], [p_T])
                                    S.op("act", lambda e: e.activation(out=ynT[:, half * 8:(half + 1) * 8, :],
                                                                        in_=p_T[:, :].rearrange("p (k t) -> p k t", t=128), func=AF.Copy),
                                         [p_T], [ynT])
                                for nb in range(2):
                                    yield
                                    for k in range(16):
                                        S.op("pe", lambda e, k=k: e.matmul(p_sc[:, nb * 512:(nb + 1) * 512], lhsT=ynT[:, k, :],
                                                                           rhs=wbs[:, k, nb * 512:(nb + 1) * 512], start=(k == 0), stop=(k == 15)),
                                             [ynT, wbs], [p_sc])
                                S.op("dve", lambda e: e.tensor_tensor(out=m0[:], in0=p_sc[:, :], in1=g0[:], op=ALU.mult), [p_sc, g0], [m0])
                                S.dma(M0[r0:r0 + 128, :], m0[:], m0, reads=[m0])
                            pending_post.append(post())
                        if ci + 1 < len(order):
                            for pg in list(pending_post):
                                if pg is not pending_post[-1] or d == 0:
                                    for _ in pg:
                                        pass
                                    pending_post.remove(pg)
                            prep_b(order[ci + 1], pend[0])
                    for pg in pending_post:
                        for _ in pg:
                            pass
                    S.barrier()
                    rel = xbw_r.bufs + dts_r.bufs + cs3_r.bufs + xs_r.bufs + bt_r.bufs + yacc_r.bufs
                    if d == 1:
                        rel += zs_r.bufs + g0_r.bufs + m0_r.bufs
                    S.release(rel)

            with contextlib.ExitStack() as st:
                wbg = sb(st, "wbg", [128, 8, 1024], BF16)
                wbx = sb(st, "wbx", [128, 8, 1024], BF16)
                wo = sb(st, "wo", [128, 8, 1024], BF16)
                kT = sb(st, "kT", [128, 8, 256], BF16)
                Vv = sb(st, "Vv", [128, 2, 1024], BF16)
                pA = ps(st, "pA", [128, 1024], F32)
                pB = ps(st, "pB", [128, 1024], F32)
                pC = ps(st, "pC", [128, 1024], F32)
                p_T = ps(st, "p5_T", [128, 1024], BF16)
                p_den = ps(st, "p_den", [128, 8], F32)
                with contextlib.ExitStack() as stw:
                    wtmp_r = Ring([sb(stw, f"w5tmp{i}", [128, 4, 1024], F32) for i in range(2)])
                    for (wsrc, wdst) in ((w_bg, wbg), (w_bx, wbx), (w_o, wo)):
                        for q in range(2):
                            wt = wtmp_r.next()
                            S.dma(wt[:], wsrc[l, q * 512:(q + 1) * 512, :].rearrange("(k p) c -> p k c", p=128), wt, writes=[wt])
                            S.op("pool", lambda e, q=q, wdst=wdst, wt=wt: e.tensor_copy(out=wdst[:, q * 4:(q + 1) * 4, :], in_=wt[:]),
                                 [wt], [wdst])
                    memT = sb(stw, "memT", [128, 8, 256], BF16)
                    mx = sb(stw, "mx", [128, 1024], F32)
                    mjunk = sb(stw, "mjunk", [128, 1024], BF16)
                    mh = sb(stw, "mh", [128, 1024], BF16)
                    mss = sb(stw, "mss", [128, 4], F32)
                    wkb = sb(stw, "wkb", [128, 8, 2048], BF16)
                    for q in range(4):
                        wt = wtmp_r.next()
                        S.dma(wt[:, :, 0:512].rearrange("p k c -> p k c"),
                              w_kv[l, :, q * 512:(q + 1) * 512].rearrange("(k p) c -> p k c", p=128)[:, 0:4, :], wt, writes=[wt])
                        S.dma(wt[:, :, 512:1024],
                              w_kv[l, :, q * 512:(q + 1) * 512].rearrange("(k p) c -> p k c", p=128)[:, 4:8, :], wt, writes=[wt])
                        S.op("pool", lambda e, q=q, wt=wt: e.tensor_copy(out=wkb[:, 0:4, q * 512:(q + 1) * 512], in_=wt[:, :, 0:512]),
                             [wt], [wkb])
                        S.op("pool", lambda e, q=q, wt=wt: e.tensor_copy(out=wkb[:, 4:8, q * 512:(q + 1) * 512], in_=wt[:, :, 512:1024]),
                             [wt], [wkb])
                    for mt in range(2):
                        S.dma(mx[:], mem_in[mt * 128:(mt + 1) * 128, :], mx, writes=[mx])
                        S.op("act", lambda e: e.activation(out=mjunk[:], in_=mx[:], func=AF.Square, accum_out=mss[:, 0:1]), [mx], [mjunk, mss])
                        S.op("dve", lambda e: e.tensor_scalar(out=mss[:, 1:2], in0=mss[:, 0:1], scalar1=1.0 / D, scalar2=EPS,
                                                               op0=ALU.mult, op1=ALU.add), [mss], [mss])
                        S.op("act", lambda e: e.activation(out=mss[:, 2:3], in_=mss[:, 1:2], func=AF.Ln), [mss], [mss])
                        S.op("act", lambda e: e.activation(out=mss[:, 3:4], in_=mss[:, 2:3], func=AF.Exp, scale=-0.5), [mss], [mss])
                        S.op("dve", lambda e: e.scalar_tensor_tensor(out=mh[:], in0=mx[:], scalar=mss[:, 3:4],
                                                                      in1=prow[:, G_MEM:G_MEM + 1024], op0=ALU.mult, op1=ALU.mult),
                             [mx, mss, prow], [mh])
                        for k in range(8):
                            S.op("pe", lambda e, k=k: e.transpose(out=p_T[:, k * 128:(k + 1) * 128], in_=mh[:, k * 128:(k + 1) * 128],
                                                                  identity=ident[:]), [mh, ident], [p_T])
                        S.op("act", lambda e, mt=mt: e.activation(out=memT[:, :, mt * 128:(mt + 1) * 128],
                                                                  in_=p_T[:, :].rearrange("p (k t) -> p k t", t=128), func=AF.Copy),
                             [p_T], [memT])
                    for j in range(8):
                        for k in range(8):
                            S.op("pe", lambda e, k=k, j=j: e.matmul(pA[:, 0:256], lhsT=wkb[:, k, j * 128:(j + 1) * 128], rhs=memT[:, k, :],
                                                                    start=(k == 0), stop=(k == 7)), [wkb, memT], [pA])
                        S.op("act", lambda e, j=j: e.activation(out=kT[:, j, :], in_=pA[:, 0:256], func=AF.Copy), [pA], [kT])
                    for mt in range(2):
                        for nb in range(2):
                            for k in range(8):
                                S.op("pe", lambda e, k=k, mt=mt, nb=nb: e.matmul(pB[:, 0:512], lhsT=memT[:, k, mt * 128:(mt + 1) * 128],
                                                                                 rhs=wkb[:, k, 1024 + nb * 512:1024 + (nb + 1) * 512],
                                                                                 start=(k == 0), stop=(k == 7)), [wkb, memT], [pB])
                            S.op("act", lambda e, mt=mt, nb=nb: e.activation(out=Vv[:, mt, nb * 512:(nb + 1) * 512], in_=pB[:, 0:512],
                                                                             func=AF.Copy), [pB], [Vv])
                    S.barrier()
                    S.release(wtmp_r.bufs + [mx])

                pj_r = Ring([sb(st, f"pj{i}", [128, PJ_W - 2048], BF16) for i in range(2)])
                xq_r = Ring([sb(st, f"xq{i}", [128, 8, 128], BF16) for i in range(2)])
                m0_r = Ring([sb(st, f"m05{i}", [128, 1024], F32) for i in range(2)])
                xi_r = Ring([sb(st, f"xi5{i}", [128, 1024], F32) for i in range(2)])
                xo_r = Ring([sb(st, f"xo5{i}", [128, 1024], F32) for i in range(2)])
                bst = sb(st, "bst", [128, 16], F32)
                vt = sb(st, "vt", [128, 1024], F32)
                vn = sb(st, "vn", [128, 1024], BF16)
                svt = sb(st, "svt", [128, 1024], F32)
                yg = sb(st, "yg", [128, 1024], BF16)
                tT = sb(st, "tT", [128, 8, 128], BF16)
                Eb = sb(st, "Eb", [128, 8, 128], BF16)
                rden = sb(st, "rden", [128, 8], F32)
                ot = sb(st, "ot", [128, 1024], F32)
                yx = sb(st, "yx", [128, 1024], BF16)
                macc = sb(st, "macc", [128, 1024], F32)
                mb = sb(st, "mb", [128, 1024], BF16)
                pjunk = sb(st, "pjunk", [128, 1024], BF16)
                pss = sb(st, "pss", [128, 4], F32)

                def transpose8(src, srcbuf):
                    for k in range(8):
                        S.op("pe", lambda e, k=k: e.transpose(out=p_T[:, k * 128:(k + 1) * 128], in_=src[:, k * 128:(k + 1) * 128],
                                                              identity=ident[:]), [srcbuf, ident], [p_T])
                    S.op("act", lambda e: e.activation(out=tT[:], in_=p_T[:, :].rearrange("p (k t) -> p k t", t=128), func=AF.Copy),
                         [p_T], [tT])

                def proj(pdst, wsb):
                    for nb in range(2):
                        for k in range(8):
                            S.op("pe", lambda e, k=k, nb=nb: e.matmul(pdst[:, nb * 512:(nb + 1) * 512], lhsT=tT[:, k, :],
                                                                      rhs=wsb[:, k, nb * 512:(nb + 1) * 512], start=(k == 0), stop=(k == 7)),
                                 [tT, wsb], [pdst])

                def p5_loads(c):
                    r0 = c * 128
                    Bn = dict(pj=pj_r.next(), xq=xq_r.next(), m0=m0_r.next(), xi=xi_r.next())
                    S.dma(Bn["pj"][:], PJ[r0:r0 + 128, 2048:PJ_W], Bn["pj"], writes=[Bn["pj"]])
                    S.dma(Bn["xq"][:], XQ[c, :, :, :], Bn["xq"], writes=[Bn["xq"]])
                    S.dma(Bn["m0"][:], M0[r0:r0 + 128, :], Bn["m0"], writes=[Bn["m0"]])
                    S.dma(Bn["xi"][:], Xsrc[r0:r0 + 128, :], Bn["xi"], writes=[Bn["xi"]])
                    return Bn

                def views(Bn):
                    pj = Bn["pj"]
                    o = -2048
                    return dict(pj=pj, xq=Bn["xq"], m0=Bn["m0"], xi=Bn["xi"],
                                sgg=pj[:, PJ_GG + o:PJ_GG + o + 1024], uu=pj[:, PJ_GUV + o:PJ_GUV + o + 1024],
                                vv=pj[:, PJ_GUV + o + 1024:PJ_GUV + o + 2048], sxg=pj[:, PJ_XG + o:PJ_XG + o + 1024],
                                g1=pj[:, PJ_MG + o + 1024:PJ_MG + o + 2048], g2=pj[:, PJ_MG + o + 2048:PJ_MG + o + 3072])

                def head(c, Bn):
                    V_ = views(Bn)
                    pj = V_["pj"]; xq = V_["xq"]; vv = V_["vv"]
                    for h in range(4):
                        for mt in range(2):
                            for dc in range(2):
                                S.op("pe", lambda e, h=h, mt=mt, dc=dc: e.matmul(
                                    pA[:, (h * 2 + mt) * 128:(h * 2 + mt + 1) * 128], lhsT=kT[:, h * 2 + dc, mt * 128:(mt + 1) * 128],
                                    rhs=xq[:, h * 2 + dc, :], start=(dc == 0), stop=(dc == 1)), [kT, xq], [pA])
                    yield
                    S.op("act", lambda e: e.activation(out=Eb[:], in_=pA[:, :].rearrange("p (j t) -> p j t", t=128), func=AF.Exp,
                                                        scale=1.0 / 16.0), [pA], [Eb])
                    yield
                    for i in range(2):
                        S.op("dve", lambda e, i=i: e.bn_stats(out=bst[:, i * 6:(i + 1) * 6], in_=vv[:, i * 512:(i + 1) * 512]), [pj], [bst])
                    S.op("dve", lambda e: e.bn_aggr(out=bst[:, 12:14], in_=bst[:, 0:12]), [bst], [bst])
                    S.op("dve", lambda e: e.tensor_scalar(out=bst[:, 14:15], in0=bst[:, 13:14], scalar1=EPS, scalar2=None, op0=ALU.add),
                         [bst], [bst])
                    S.op("act", lambda e: e.activation(out=bst[:, 14:15], in_=bst[:, 14:15], func=AF.Ln), [bst], [bst])
                    S.op("act", lambda e: e.activation(out=bst[:, 15:16], in_=bst[:, 14:15], func=AF.Exp, scale=-0.5), [bst], [bst])
                    yield
                    S.op("dve", lambda e: e.tensor_scalar(out=vt[:], in0=vv, scalar1=bst[:, 12:13], scalar2=bst[:, 15:16],
                                                           op0=ALU.subtract, op1=ALU.mult), [pj, bst], [vt])
                    S.op("dve", lambda e: e.tensor_tensor(out=vt[:], in0=vt[:], in1=prow[:, G_LNG:G_LNG + 1024], op=ALU.mult), [vt, prow], [vt])
                    S.op("dve", lambda e: e.tensor_tensor(out=vn[:], in0=vt[:], in1=prow[:, G_LNB:G_LNB + 1024], op=ALU.add), [vt, prow], [vn])
                    yield
                    for g in range(8):
                        S.op("pe", lambda e, g=g: e.matmul(pB[:, g * 128:(g + 1) * 128], lhsT=wsT[:, g * 128:(g + 1) * 128],
                                                           rhs=vn[:, g * 128:(g + 1) * 128], start=True, stop=True), [wsT, vn], [pB])

                def mid(c, Bn):
                    V_ = views(Bn)
                    pj = V_["pj"]; m0 = V_["m0"]
                    for h in range(4):
                        for mt in range(2):
                            S.op("pe", lambda e, h=h, mt=mt: e.matmul(pC[:, h * 256:(h + 1) * 256], lhsT=Eb[:, h * 2 + mt, :],
                                                                      rhs=Vv[:, mt, h * 256:(h + 1) * 256], start=(mt == 0), stop=(mt == 1)),
                                 [Eb, Vv], [pC])
                    for h in range(4):
                        for mt in range(2):
                            S.op("pe", lambda e, h=h, mt=mt: e.matmul(p_den[:, h * 2:h * 2 + 2], lhsT=Eb[:, h * 2 + mt, :], rhs=onesb[:],
                                                                      start=(mt == 0), stop=(mt == 1)), [Eb, onesb], [p_den])
                    yield
                    S.op("dve", lambda e: e.tensor_tensor(out=svt[:, :].rearrange("p (g q) -> p g q", q=128),
                                                           in0=pB[:, :].rearrange("p (g q) -> p g q", q=128),
                                                           in1=bc3(bsT[:, 0:8], 128), op=ALU.add), [pB, bsT], [svt])
                    S.op("dve", lambda e: e.tensor_tensor(out=svt[:], in0=svt[:], in1=V_["uu"], op=ALU.mult), [svt, pj], [svt])
                    S.op("dve", lambda e: e.tensor_tensor(out=yg[:], in0=svt[:], in1=V_["sgg"], op=ALU.mult), [svt, pj], [yg])
                    S.op("dve", lambda e: e.reciprocal(out=rden[:], in_=p_den[:, :]), [p_den], [rden])
                    rd4 = rden[:, :].rearrange("p (h two) -> p h two", two=2)[:, :, 0:1].broadcast_to([128, 4, 256])
                    S.op("dve", lambda e: e.tensor_tensor(out=ot[:, :].rearrange("p (h q) -> p h q", q=256),
                                                           in0=pC[:, :].rearrange("p (h q) -> p h q", q=256), in1=rd4, op=ALU.mult),
                         [pC, rden], [ot])
                    S.op("dve", lambda e: e.tensor_tensor(out=yx[:], in0=ot[:], in1=V_["sxg"], op=ALU.mult), [ot, pj], [yx])
                    yield
                    transpose8(yg, yg)
                    yield
                    proj(pA, wbg)
                    yield
                    S.op("dve", lambda e: e.tensor_tensor(out=macc[:], in0=pA[:, :], in1=V_["g1"], op=ALU.mult), [pA, pj], [macc])
                    S.op("dve", lambda e: e.tensor_tensor(out=macc[:], in0=macc[:], in1=m0[:], op=ALU.add), [macc, m0], [macc])
                    transpose8(yx, yx)
                    proj(pB, wbx)
                    S.op("dve", lambda e: e.tensor_tensor(out=ot[:], in0=pB[:, :], in1=V_["g2"], op=ALU.mult), [pB, pj], [ot])
                    S.op("dve", lambda e: e.tensor_tensor(out=mb[:], in0=ot[:], in1=macc[:], op=ALU.add), [ot, macc], [mb])

                def tail(c, Bn):
                    r0 = c * 128
                    xi = Bn["xi"]; xo = xo_r.next()
                    transpose8(mb, mb)
                    proj(pC, wo)
                    S.op("act", lambda e: e.activation(out=pjunk[:], in_=pC[:, :], func=AF.Square, accum_out=pss[:, 0:1]), [pC], [pjunk, pss])
                    S.op("dve", lambda e: e.tensor_scalar(out=pss[:, 1:2], in0=pss[:, 0:1], scalar1=1.0 / D, scalar2=EPS,
                                                           op0=ALU.mult, op1=ALU.add), [pss], [pss])
                    S.op("act", lambda e: e.activation(out=pss[:, 2:3], in_=pss[:, 1:2], func=AF.Ln), [pss], [pss])
                    S.op("act", lambda e: e.activation(out=pss[:, 3:4], in_=pss[:, 2:3], func=AF.Exp, scale=-0.5), [pss], [pss])
                    S.op("dve", lambda e: e.scalar_tensor_tensor(out=xo[:], in0=pC[:, :], scalar=pss[:, 3:4],
                                                                  in1=prow[:, G_POST:G_POST + 1024], op0=ALU.mult, op1=ALU.mult),
                         [pC, pss, prow], [xo])
                    S.op("dve", lambda e: e.tensor_tensor(out=xo[:], in0=xo[:], in1=xi[:], op=ALU.add), [xo, xi], [xo])
                    S.dma(Xdst[r0:r0 + 128, :], xo[:], xo, reads=[xo])

                def drain(gen):
                    for _ in gen:
                        pass

                cur = p5_loads(0)
                drain(head(0, cur))
                for c in range(NCH):
                    nxt = p5_loads(c + 1) if c + 1 < NCH else None
                    gm = mid(c, cur)
                    gh = head(c + 1, nxt) if nxt is not None else iter(())
                    for _ in range(4):
                        next(gm, None)
                        next(gh, None)
                    drain(gm)
                    drain(gh)
                    tail(c, cur)
                    cur = nxt
                S.barrier()
                S.release(pj_r.bufs + xq_r.bufs + m0_r.bufs + xi_r.bufs + xo_r.bufs)
        S.barrier()
    return nc


def host_consts():
    ident = np.eye(128, dtype=np.float32)
    k = np.arange(128)[:, None]
    t = np.arange(128)[None, :]
    tri0 = (k <= t).astype(np.float32)
    tri1 = (k >= t).astype(np.float32)
    esel = np.zeros((96, 32, 128), np.float32)
    for h in range(32):
        for j in range(3):
            esel[j * 32 + h, h, :] = 1.0
    nm0 = np.where(k > t, -30000.0, 0.0).astype(np.float32)
    nm1 = np.where(k < t, -30000.0, 0.0).astype(np.float32)
    c_nm = np.concatenate([np.tile(nm0, (1, 4)), np.tile(nm1, (1, 4))], axis=1)
    return {"c_ident": ident, "c_tri0": tri0, "c_tri1": tri1, "c_esel": esel.reshape(96, 32 * 128), "c_nm": c_nm}


def host_params(inp, depth):
    f = lambda a: np.asarray(a, dtype=np.float32)
    p_row = np.zeros((depth, 1, 9 * 1024), np.float32)
    p_row[:, 0, 0:1024] = f(inp["norm_pre_g"])[:depth]
    p_row[:, 0, 1024:3072] = f(inp["ssd_norm_g"])[:depth]
    p_row[:, 0, 3072:4096] = f(inp["gmlp_ln_g"])[:depth]
    p_row[:, 0, 4096:5120] = f(inp["gmlp_ln_b"])[:depth]
    p_row[:, 0, 5120:6144] = f(inp["mem_norm_g"])[:depth]
    p_row[:, 0, 6144:7168] = f(inp["norm_post_g"])[:depth]
    p_row[:, 0, 7168:7232] = f(inp["dt_bias"])[:depth].reshape(depth, 64)
    p_row[:, 0, 7232:7296] = f(inp["a_log"])[:depth].reshape(depth, 64)
    p_row[:, 0, 7296:7328] = f(inp["d_skip"])[:depth]
    cw = f(inp["conv_w"])[:depth]
    p_convw = np.ascontiguousarray(cw.reshape(depth, 5, 32, 128).transpose(0, 3, 2, 1)).reshape(depth, 128, 160)
    cb = f(inp["conv_b"])[:depth]
    p_convb = np.ascontiguousarray(cb.reshape(depth, 32, 128).transpose(0, 2, 1))
    ws = f(inp["w_spatial"])[:depth]
    p_wsT = np.ascontiguousarray(ws.transpose(0, 3, 1, 2)).reshape(depth, 128, 1024)
    bs = f(inp["b_spatial"])[:depth]
    p_bsT = np.ascontiguousarray(bs.transpose(0, 2, 1))
    return {"p_row": p_row, "p_convw": p_convw, "p_convb": p_convb, "p_wsT": p_wsT, "p_bsT": p_bsT}


_NC_CACHE = {}


def kernel(**inputs):
    x = np.asarray(inputs["x"], dtype=np.float32)
    B, L, _ = x.shape
    depth = inputs["w_in"].shape[0]
    key = (L, depth)
    if key not in _NC_CACHE:
        _NC_CACHE[key] = build(L, depth)
    nc = _NC_CACHE[key]
    shared = {}
    shared.update(host_consts())
    shared.update(host_params(inputs, depth))
    for n in ("w_in", "w_kv", "w_br_ssd", "w_br_gmlp", "w_br_xattn", "w_out"):
        shared[n] = np.ascontiguousarray(np.asarray(inputs[n], dtype=np.float32))
    mem = np.asarray(inputs["mem"], dtype=np.float32)
    in_maps = []
    for b in range(B):
        m = dict(shared)
        m["x"] = np.ascontiguousarray(x[b])
        m["mem"] = np.ascontiguousarray(mem[b])
        in_maps.append(m)
    res = run_bass_kernel_spmd(nc, in_maps, core_ids=list(range(B)))
    return np.stack([np.asarray(res.results[b]["out"], dtype=np.float32) for b in range(B)], axis=0)
```

```python
import contextlib
import numpy as np
import concourse.bass as bass
import concourse.mybir as mybir
from concourse.bass_utils import run_bass_kernel_spmd

F32 = mybir.dt.float32
BF16 = mybir.dt.bfloat16
AF = mybir.ActivationFunctionType
ALU = mybir.AluOpType

D = 1024
NCORES = 4
EPS = 1e-6
SAME_ENGINE_WAITS = True
DMA_CAST = True

C_Z, C_XBC, C_DT, C_GG, C_GUV, C_XQ, C_XG, C_MG = 0, 2048, 6144, 6208, 7232, 9280, 10304, 11328
PJ_Z, PJ_GG, PJ_GUV, PJ_XG, PJ_MG, PJ_W = 0, 2048, 3072, 5120, 6144, 9216


class Buf:
    def __init__(self, t, name):
        self.t = t
        self.name = name
        self.w = None
        self.r = {}
        self.dsem = None

    def __getitem__(self, idx):
        return self.t[idx]


class Sched:
    def __init__(self, nc, es):
        self.nc = nc
        self.es = es
        self.eng = {"pe": nc.tensor, "dve": nc.vector, "act": nc.scalar, "pool": nc.gpsimd, "sp": nc.sync}
        self.esem = {}
        self.cnt = {}
        self.seen = {e: {} for e in self.eng}
        self.nsem = 0
        for e in ("pe", "dve", "act", "pool"):
            self.esem[e] = self.new_sem("e_" + e)
            self.cnt[e] = 0
        self.free_dsems = []
        self.all_dsems = []

    def new_sem(self, name):
        self.nsem += 1
        return self.es.enter_context(self.nc.semaphore(name + str(self.nsem)))

    def get_dsem(self):
        if self.free_dsems:
            return self.free_dsems.pop()
        s = [self.new_sem("d"), 0]
        self.all_dsems.append(s)
        return s

    def release(self, bufs):
        for b in bufs:
            if b.dsem is not None:
                self.free_dsems.append(b.dsem)
                b.dsem = None

    def _wait(self, e, tok):
        sem, val, seng = tok
        if seng == e and (e == "pe" or not SAME_ENGINE_WAITS):
            return
        k = id(sem)
        if self.seen[e].get(k, 0) >= val:
            return
        self.eng[e].wait_ge(sem, val)
        self.seen[e][k] = val

    def _deps(self, e, reads, writes):
        for b in reads:
            if b.w is not None:
                self._wait(e, b.w)
        for b in writes:
            if b.w is not None:
                self._wait(e, b.w)
            for tok in b.r.values():
                self._wait(e, tok)

    def op(self, e, fn, reads=(), writes=()):
        self._deps(e, reads, writes)
        if self.cnt[e] >= 30000:
            self.esem[e] = self.new_sem("e_" + e)
            self.cnt[e] = 0
        ins = fn(self.eng[e])
        self.cnt[e] += 1
        ins.then_inc(self.esem[e], 1)
        tok = (self.esem[e], self.cnt[e], e)
        for b in reads:
            b.r[e] = tok
        for b in writes:
            b.w = tok
            b.r = {}
        return ins

    def dma(self, out, in_, sb, reads=(), writes=(), q="sp"):
        self._deps(q, reads, writes)
        if sb.dsem is None:
            sb.dsem = self.get_dsem()
        ins = self.eng[q].dma_start(out=out, in_=in_)
        sb.dsem[1] += 16
        ins.then_inc(sb.dsem[0], 16)
        tok = (sb.dsem[0], sb.dsem[1], "dma")
        for b in reads:
            b.r[id(sb.dsem[0])] = tok
        for b in writes:
            b.w = tok
            b.r = {}

    def barrier(self):
        toks = [(self.esem[e], self.cnt[e], e) for e in self.esem if self.cnt[e] > 0]
        toks += [(s[0], s[1], "dma") for s in self.all_dsems if s[1] > 0]
        for e in self.eng:
            for tok in toks:
                if tok[2] == e:
                    continue
                self._wait(e, tok)


class Ring:
    def __init__(self, bufs):
        self.bufs = bufs
        self.i = 0

    def next(self):
        b = self.bufs[self.i % len(self.bufs)]
        self.i += 1
        return b


def bc3(ap, n):
    p, a = ap.shape
    return ap.unsqueeze(2).broadcast_to([p, a, n])


def build(L, depth, debug=False):
    NCH = L // 128
    HALF = min(L, 2048)
    NHALF = L // HALF
    HCH = HALF // 128
    NTB = HALF // 512
    nc = bass.Bass("TRN2", target_bir_lowering=False)

    def din(name, shape, dt=F32):
        return nc.dram_tensor(name, list(shape), dt, kind="ExternalInput").ap()

    skind = "ExternalOutput" if debug else "Internal"

    def dscr(name, shape, dt):
        return nc.dram_tensor(name, list(shape), dt, kind=skind).ap()

    x_in = din("x", [L, D])
    mem_in = din("mem", [256, D])
    w_in = din("w_in", [depth, D, 14400])
    w_kv = din("w_kv", [depth, D, 2048])
    w_bs = din("w_br_ssd", [depth, 2048, D])
    w_bg = din("w_br_gmlp", [depth, D, D])
    w_bx = din("w_br_xattn", [depth, D, D])
    w_o = din("w_out", [depth, D, D])
    p_row = din("p_row", [depth, 1, 9 * 1024])
    p_convw = din("p_convw", [depth, 128, 32 * 5])
    p_convb = din("p_convb", [depth, 128, 32])
    p_wsT = din("p_wsT", [depth, 128, 8 * 128])
    p_bsT = din("p_bsT", [depth, 128, 8])
    c_ident = din("c_ident", [128, 128])
    c_tri0 = din("c_tri0", [128, 128])
    c_tri1 = din("c_tri1", [128, 128])
    c_esel = din("c_esel", [96, 32 * 128])
    c_nm = din("c_nm", [128, 2 * 512])
    out_d = nc.dram_tensor("out", [L, D], F32, kind="ExternalOutput").ap()

    X1 = dscr("s_x1", [L, D], F32)
    PJ = dscr("s_pj", [L, PJ_W], BF16)
    XBC = dscr("s_xbc", [NCH, 128, 32, 128], BF16)
    XQ = dscr("s_xq", [NCH, 128, 8, 128], BF16)
    DTS = dscr("s_dts", [L, 384], F32)
    CS3 = dscr("s_cs3", [L, 192], BF16)
    XS = dscr("s_xs", [L, 2048], BF16)
    BTM = dscr("s_btm", [L, 1024], BF16)
    Y0 = dscr("s_y0", [L, 2048], F32)
    M0 = dscr("s_m0", [L, D], F32)

    with contextlib.ExitStack() as es:
        S = Sched(nc, es)

        uid = [0]

        def sb(st, name, shape, dt):
            uid[0] += 1
            return Buf(st.enter_context(nc.sbuf_tensor(f"{name}_u{uid[0]}", list(shape), dt)), name)

        def ps(st, name, shape, dt):
            uid[0] += 1
            return Buf(st.enter_context(nc.psum_tensor(f"{name}_u{uid[0]}", list(shape), dt)), name)

        ident = sb(es, "ident", [128, 128], BF16)
        tri0 = sb(es, "tri0", [128, 128], F32)
        tri1 = sb(es, "tri1", [128, 128], F32)
        onesf = sb(es, "onesf", [128, 128], F32)
        onesb = sb(es, "onesb", [128, 2], BF16)
        esel = sb(es, "esel", [96, 32 * 128], BF16)
        negm = sb(es, "negm", [128, 2 * 512], BF16)
        prow = sb(es, "prow", [128, 9 * 1024], F32)
        convw = sb(es, "convw", [128, 160], F32)
        convb = sb(es, "convb", [128, 32], F32)
        wsT = sb(es, "wsT", [128, 1024], BF16)
        bsT = sb(es, "bsT", [128, 8], F32)
        a_bc = sb(es, "a_bc", [128, 64], F32)
        G_PRE, G_SSD, G_LNG, G_LNB, G_MEM, G_POST, G_MISC = 0, 1024, 3072, 4096, 5120, 6144, 7168

        with contextlib.ExitStack() as st:
            tmpf = sb(st, "c_tmpf", [128, 32 * 128], F32)
            S.dma(tmpf[:, 0:128], c_ident[:, :], tmpf, writes=[tmpf])
            S.op("dve", lambda e: e.tensor_copy(out=ident[:], in_=tmpf[:, 0:128]), [tmpf], [ident])
            S.dma(tri0[:], c_tri0[:, :], tri0, writes=[tri0])
            S.dma(tri1[:], c_tri1[:, :], tri1, writes=[tri1])
            S.dma(tmpf[0:96, :], c_esel[:, :], tmpf, writes=[tmpf])
            S.op("dve", lambda e: e.tensor_copy(out=esel[:], in_=tmpf[0:96, :]), [tmpf], [esel])
            S.dma(tmpf[:, 0:1024], c_nm[:, :], tmpf, writes=[tmpf])
            S.op("dve", lambda e: e.tensor_copy(out=negm[:], in_=tmpf[:, 0:1024]), [tmpf], [negm])
            S.op("pool", lambda e: e.memset(onesf[:], 1.0), [], [onesf])
            S.op("pool", lambda e: e.memset(onesb[:], 1.0), [], [onesb])
            S.barrier()
            S.release([tmpf, tri0, tri1])

        for l in range(depth):
            Xsrc = x_in if l == 0 else X1
            Xdst = out_d if l == depth - 1 else X1
            if depth == 1:
                Xdst = out_d

            with contextlib.ExitStack() as st:
                tmpw = sb(st, "p_tmpw", [128, 1024], F32)
                S.dma(prow[:], p_row[l, 0:1, :].partition_broadcast(128), prow, writes=[prow])
                S.dma(convw[:], p_convw[l, :, :], convw, writes=[convw])
                S.dma(convb[:], p_convb[l, :, :], convb, writes=[convb])
                S.dma(bsT[:], p_bsT[l, :, :], bsT, writes=[bsT])
                S.dma(tmpw[:], p_wsT[l, :, :], tmpw, writes=[tmpw])
                S.op("dve", lambda e: e.tensor_copy(out=wsT[:], in_=tmpw[:]), [tmpw], [wsT])
                S.op("act", lambda e: e.activation(out=a_bc[:], in_=prow[:, G_MISC + 64:G_MISC + 128], func=AF.Exp),
                     [prow], [a_bc])
                S.op("dve", lambda e: e.tensor_scalar(out=a_bc[:], in0=a_bc[:], scalar1=-1.0, scalar2=None,
                                                       op0=ALU.mult), [a_bc], [a_bc])
                S.barrier()
                S.release([tmpw, prow, convw, convb, bsT])
            dtb_bc = prow[:, G_MISC:G_MISC + 64]
            dsk_bc = prow[:, G_MISC + 128:G_MISC + 160]

            for hf in range(NHALF):
                hs = hf * HALF
                with contextlib.ExitStack() as st:
                    hT = sb(st, "hT", [128, 8, HALF + 4], BF16)
                    with contextlib.ExitStack() as st1:
                        xin_r = Ring([sb(st1, f"xin{i}", [128, 1024], F32) for i in range(9)])
                        jk_r = Ring([sb(st1, f"jk{i}", [128, 1024], BF16) for i in range(4)])
                        junk = sb(st1, "junk", [128, 1024], BF16)
                        hb_r = Ring([sb(st1, f"hb{i}", [128, 1024], BF16) for i in range(5)])
                        ss_r = Ring([sb(st1, f"ss{i}", [128, 4], F32) for i in range(5)])
                        pT_r = Ring([ps(st1, f"pT{i}", [128, 1024], BF16) for i in range(2)])

                        def norm_load(r0, nr):
                            xin = xin_r.next()
                            S.dma(xin[0:nr, :], Xsrc[r0:r0 + nr, :], xin, writes=[xin])
                            return xin

                        def norm_rows(r0, nr, col0, xin=None):
                            hb = hb_r.next(); ss = ss_r.next(); pT = pT_r.next()
                            if xin is None:
                                xin = norm_load(r0, nr)
                            S.op("act", lambda e: e.activation(out=junk[0:nr, :], in_=xin[0:nr, :], func=AF.Square,
                                                                accum_out=ss[0:nr, 0:1]), [xin], [junk, ss])
                            S.op("dve", lambda e: e.tensor_scalar(out=ss[0:nr, 1:2], in0=ss[0:nr, 0:1], scalar1=1.0 / D,
                                                                   scalar2=EPS, op0=ALU.mult, op1=ALU.add), [ss], [ss])
                            S.op("act", lambda e: e.activation(out=ss[0:nr, 2:3], in_=ss[0:nr, 1:2], func=AF.Ln), [ss], [ss])
                            S.op("act", lambda e: e.activation(out=ss[0:nr, 3:4], in_=ss[0:nr, 2:3], func=AF.Exp, scale=-0.5), [ss], [ss])
                            S.op("dve", lambda e: e.scalar_tensor_tensor(out=hb[0:nr, :], in0=xin[0:nr, :], scalar=ss[0:nr, 3:4],
                                                                          in1=prow[0:nr, G_PRE:G_PRE + 1024], op0=ALU.mult,
                                                                          op1=ALU.mult), [xin, ss, prow], [hb])
                            for k in range(8):
                                S.op("pe", lambda e, k=k: e.transpose(out=pT[:, k * 128:k * 128 + nr],
                                                                      in_=hb[0:nr, k * 128:(k + 1) * 128],
                                                                      identity=ident[0:nr, 0:nr]), [hb, ident], [pT])
                            src = pT[:, :].rearrange("p (k t) -> p k t", t=128)[:, :, 0:nr]
                            S.op("act", lambda e: e.activation(out=hT[:, :, col0:col0 + nr], in_=src, func=AF.Copy),
                                 [pT], [hT])

                        if hs - 2 >= 0:
                            norm_rows(hs - 2, 2, 0)
                        else:
                            S.op("pool", lambda e: e.memset(hT[:, :, 0:2], 0.0), [], [hT])
                        if hs + HALF + 2 <= L:
                            norm_rows(hs + HALF, 2, HALF + 2)
                        else:
                            S.op("pool", lambda e: e.memset(hT[:, :, HALF + 2:HALF + 4], 0.0), [], [hT])
                        NB1 = 4
                        loads = {}

                        def ensure_load(c):
                            if c < HCH and c not in loads:
                                loads[c] = norm_load(hs + c * 128, 128)

                        for c in range(min(NB1, HCH)):
                            ensure_load(c)
                        for b0 in range(0, HCH, NB1):
                            batch = list(range(b0, min(b0 + NB1, HCH)))
                            for c in batch:
                                ensure_load(c + NB1)
                            ctx = {c: dict(xin=loads.pop(c), hb=hb_r.next(), ss=ss_r.next(), jk=jk_r.next()) for c in batch}
                            for c in batch:
                                X = ctx[c]
                                S.op("act", lambda e, X=X: e.activation(out=X["jk"][:], in_=X["xin"][:], func=AF.Square,
                                                                        accum_out=X["ss"][:, 0:1]), [X["xin"]], [X["jk"], X["ss"]])
                            for c in batch:
                                X = ctx[c]
                                S.op("dve", lambda e, X=X: e.tensor_scalar(out=X["ss"][:, 1:2], in0=X["ss"][:, 0:1], scalar1=1.0 / D,
                                                                           scalar2=EPS, op0=ALU.mult, op1=ALU.add), [X["ss"]], [X["ss"]])
                            for c in batch:
                                X = ctx[c]
                                S.op("act", lambda e, X=X: e.activation(out=X["ss"][:, 2:3], in_=X["ss"][:, 1:2], func=AF.Ln), [X["ss"]], [X["ss"]])
                            for c in batch:
                                X = ctx[c]
                                S.op("act", lambda e, X=X: e.activation(out=X["ss"][:, 3:4], in_=X["ss"][:, 2:3], func=AF.Exp, scale=-0.5), [X["ss"]], [X["ss"]])
                            for c in batch:
                                X = ctx[c]
                                S.op("dve", lambda e, X=X: e.scalar_tensor_tensor(out=X["hb"][:], in0=X["xin"][:], scalar=X["ss"][:, 3:4],
                                                                                  in1=prow[:, G_PRE:G_PRE + 1024], op0=ALU.mult,
                                                                                  op1=ALU.mult), [X["xin"], X["ss"], prow], [X["hb"]])
                            for c in batch:
                                X = ctx[c]
                                pT = pT_r.next()
                                col0 = 2 + c * 128
                                for k in range(8):
                                    S.op("pe", lambda e, k=k, X=X, pT=pT: e.transpose(out=pT[:, k * 128:(k + 1) * 128],
                                                                                      in_=X["hb"][:, k * 128:(k + 1) * 128],
                                                                                      identity=ident[:]), [X["hb"], ident], [pT])
                                S.op("act", lambda e, pT=pT, col0=col0: e.activation(out=hT[:, :, col0:col0 + 128],
                                                                                     in_=pT[:, :].rearrange("p (k t) -> p k t", t=128),
                                                                                     func=AF.Copy), [pT], [hT])
                        S.barrier()
                        S.release(xin_r.bufs)

                    with contextlib.ExitStack() as st2:
                        wf_r = Ring([sb(st2, f"wf{i}", [128, 8, 512], F32) for i in range(0 if DMA_CAST else 2)])
                        wb_r = Ring([sb(st2, f"wb{i}", [128, 8, 512], BF16) for i in range(3)])
                        pre_r = Ring([sb(st2, f"pre{i}", [128, HALF + 4], BF16) for i in range(2)])
                        acc_r = Ring([sb(st2, f"acc{i}", [128, HALF], F32) for i in range(2)])
                        xc_r = Ring([sb(st2, f"xc{i}", [128, HALF], BF16) for i in range(2)])
                        stg_r = Ring([sb(st2, f"stg{i}", [128, 512], BF16) for i in range(4)])
                        pm_r = Ring([ps(st2, f"pm{i}", [128, 512], F32) for i in range(4)])
                        pd_r = Ring([ps(st2, f"pd{i}", [128, 256], F32) for i in range(2)])
                        dt_r = Ring([sb(st2, f"dtw{i}", [128, 384 + 384], F32) for i in range(5)])
                        c3_r = Ring([sb(st2, f"c3w{i}", [128, 192], BF16) for i in range(5)])

                        def load_w(col0, ncol):
                            wb = wb_r.next()
                            src = w_in[l, :, col0:col0 + ncol].rearrange("(k p) c -> p k c", p=128)
                            if DMA_CAST:
                                S.dma(wb[:, :, 0:ncol], src, wb, writes=[wb], q="pool")
                            else:
                                wf = wf_r.next()
                                S.dma(wf[:, :, 0:ncol], src, wf, writes=[wf])
                                S.op("act", lambda e: e.activation(out=wb[:, :, 0:ncol], in_=wf[:, :, 0:ncol], func=AF.Copy), [wf], [wb])
                            return wb

                        segs = [(C_Z, PJ_Z, 2048, AF.Silu), (C_GG, PJ_GG, 1024, AF.Silu),
                                (C_GUV, PJ_GUV, 2048, AF.Gelu_apprx_tanh), (C_XG, PJ_XG, 1024, AF.Silu),
                                (C_MG, PJ_MG, 3072, AF.Sigmoid)]
                        wlist = ([(C_XBC + bi * 512, 512) for bi in range(8)] + [(C_XQ + bi * 512, 512) for bi in range(2)]
                                 + [(wc + b0, 512) for (wc, pc, width, fn) in segs for b0 in range(0, width, 512)] + [(C_DT, 64)])
                        wq = []
                        widx = [0]

                        def issue_w():
                            if widx[0] < len(wlist):
                                wq.append(load_w(*wlist[widx[0]]))
                                widx[0] += 1

                        def next_w():
                            if not wq:
                                issue_w()
                            wb_ = wq.pop(0)
                            issue_w()
                            return wb_

                        pending = []

                        def flush_pending():
                            while pending:
                                pending.pop(0)()

                        def store_tile(dst, xc, m):
                            for q in range(0, HCH, 8):
                                nq = min(8, HCH - q)
                                o = dst[hs // 128 + q:hs // 128 + q + nq, :, m, :].rearrange("c p t -> p c t")
                                i_ = xc[:, q * 128:(q + nq) * 128].rearrange("p (c t) -> p c t", t=128)
                                S.dma(o, i_, xc, reads=[xc])

                        def feat_block(col0, kind, mbase):
                            wb = next_w()
                            if kind == "xq":
                                flush_pending()
                            for mi in range(4):
                                m = mbase + mi
                                if kind == "xbc":
                                    pre = pre_r.next()
                                    blocks = [(i * 512, 512) for i in range(NTB)] + [(HALF, 4)]
                                else:
                                    pre = xc_r.next()
                                    blocks = [(i * 512, 512) for i in range(NTB)]
                                for (c0, n) in blocks:
                                    pm = pm_r.next()
                                    hc0 = c0 if kind == "xbc" else c0 + 2
                                    for k in range(8):
                                        S.op("pe", lambda e, k=k: e.matmul(pm[:, 0:n], lhsT=wb[:, k, mi * 128:(mi + 1) * 128],
                                                                           rhs=hT[:, k, hc0:hc0 + n], start=(k == 0), stop=(k == 7)),
                                             [wb, hT], [pm])
                                    S.op("act", lambda e: e.activation(out=pre[:, c0:c0 + n], in_=pm[:, 0:n], func=AF.Copy),
                                         [pm], [pre])
                                if kind == "xbc":
                                    acc = acc_r.next(); xc = xc_r.next()
                                    S.op("act", lambda e: e.activation(out=acc[:], in_=pre[:, 0:HALF], func=AF.Copy,
                                                                        scale=convw[:, m * 5:m * 5 + 1]), [pre, convw], [acc])
                                    flush_pending()
                                    for k in range(1, 5):
                                        S.op("dve", lambda e, k=k: e.scalar_tensor_tensor(
                                            out=acc[:], in0=pre[:, k:k + HALF], scalar=convw[:, m * 5 + k:m * 5 + k + 1],
                                            in1=acc[:], op0=ALU.mult, op1=ALU.add), [pre, convw, acc], [acc])

                                    def fin(acc=acc, xc=xc, m=m):
                                        S.op("act", lambda e: e.activation(out=xc[:], in_=acc[:], func=AF.Silu,
                                                                            bias=convb[:, m:m + 1]), [acc, convb], [xc])
                                        store_tile(XBC, xc, m)
                                    pending.append(fin)
                                else:
                                    store_tile(XQ, pre, m)

                        for bi in range(8):
                            feat_block(C_XBC + bi * 512, "xbc", bi * 4)
                        for bi in range(2):
                            feat_block(C_XQ + bi * 512, "xq", bi * 4)
                        flush_pending()

                        for (wc, pc, width, fn) in segs:
                            for b0 in range(0, width, 512):
                                wb = next_w()
                                for c in range(HCH):
                                    pm = pm_r.next(); stg = stg_r.next()
                                    for k in range(8):
                                        S.op("pe", lambda e, k=k: e.matmul(pm[:, :], lhsT=hT[:, k, 2 + c * 128:2 + (c + 1) * 128],
                                                                           rhs=wb[:, k, :], start=(k == 0), stop=(k == 7)),
                                             [wb, hT], [pm])
                                    S.op("act", lambda e: e.activation(out=stg[:], in_=pm[:, :], func=fn), [pm], [stg])
                                    r0 = hs + c * 128
                                    S.dma(PJ[r0:r0 + 128, pc + b0:pc + b0 + 512], stg[:], stg, reads=[stg])

                        wb = next_w()
                        pdq = Ring(pm_r.bufs + pd_r.bufs)

                        def dt_chunk(c):
                            pd = pdq.next(); dw = dt_r.next(); c3 = c3_r.next()
                            r0 = hs + c * 128
                            for k in range(8):
                                S.op("pe", lambda e, k=k: e.matmul(pd[:, 0:64], lhsT=hT[:, k, 2 + c * 128:2 + (c + 1) * 128],
                                                                   rhs=wb[:, k, 0:64], start=(k == 0), stop=(k == 7)), [wb, hT], [pd])
                            V_, A_, E_, R_ = 384, 448, 512, 576
                            S.op("dve", lambda e: e.tensor_tensor(out=dw[:, V_:V_ + 64], in0=pd[:, 0:64], in1=dtb_bc, op=ALU.add),
                                 [pd, prow], [dw])
                            yield
                            S.op("act", lambda e: e.activation(out=dw[:, A_:A_ + 64], in_=dw[:, V_:V_ + 64], func=AF.Abs), [dw], [dw])
                            yield
                            S.op("act", lambda e: e.activation(out=dw[:, E_:E_ + 64], in_=dw[:, A_:A_ + 64], func=AF.Exp, scale=-1.0),
                                 [dw], [dw])
                            yield
                            S.op("act", lambda e: e.activation(out=dw[:, E_:E_ + 64], in_=dw[:, E_:E_ + 64], func=AF.Ln, bias=1.0),
                                 [dw], [dw])
                            yield
                            S.op("dve", lambda e: e.tensor_scalar(out=dw[:, R_:R_ + 64], in0=dw[:, V_:V_ + 64], scalar1=0.0,
                                                                   scalar2=None, op0=ALU.max), [dw], [dw])
                            yield
                            S.op("dve", lambda e: e.tensor_tensor(out=dw[:, 0:64], in0=dw[:, R_:R_ + 64], in1=dw[:, E_:E_ + 64],
                                                                   op=ALU.add), [dw], [dw])
                            yield
                            S.op("dve", lambda e: e.tensor_tensor(out=dw[:, V_:V_ + 64], in0=dw[:, 0:64], in1=a_bc[:], op=ALU.mult),
                                 [dw, a_bc], [dw])
                            yield
                            S.op("pe", lambda e: e.matmul(pd[:, 64:96], lhsT=tri0[:], rhs=dw[:, V_:V_ + 32], start=True, stop=True),
                                 [tri0, dw], [pd])
                            yield
                            S.op("pe", lambda e: e.matmul(pd[:, 96:128], lhsT=tri1[:], rhs=dw[:, V_ + 32:V_ + 64], start=True, stop=True),
                                 [tri1, dw], [pd])
                            yield
                            S.op("pe", lambda e: e.matmul(pd[:, 128:192], lhsT=onesf[:], rhs=dw[:, V_:V_ + 64], start=True, stop=True),
                                 [onesf, dw], [pd])
                            yield
                            T_ = 640
                            S.op("dve", lambda e: e.tensor_copy(out=dw[:, 320:384], in_=pd[:, 64:128]), [pd], [dw])
                            yield
                            S.op("dve", lambda e: e.tensor_copy(out=dw[:, T_:T_ + 64], in_=pd[:, 128:192]), [pd], [dw])
                            yield
                            L_ = 704
                            S.op("act", lambda e: e.activation(out=dw[:, L_:L_ + 64], in_=dw[:, 0:64], func=AF.Ln), [dw], [dw])
                            yield
                            S.op("dve", lambda e: e.tensor_tensor(out=dw[:, 64:128], in0=dw[:, L_:L_ + 64], in1=dw[:, 320:384],
                                                                   op=ALU.subtract), [dw], [dw])
                            yield
                            S.op("act", lambda e: e.activation(out=dw[:, 128:192], in_=dw[:, 320:384], func=AF.Exp), [dw], [dw])
                            yield
                            S.op("act", lambda e: e.activation(out=dw[:, 256:320], in_=dw[:, T_:T_ + 64], func=AF.Exp), [dw], [dw])
                            yield
                            S.op("dve", lambda e: e.tensor_tensor(out=dw[:, A_:A_ + 64], in0=dw[:, T_:T_ + 64], in1=dw[:, 320:384],
                                                                   op=ALU.subtract), [dw], [dw])
                            yield
                            S.op("act", lambda e: e.activation(out=dw[:, A_:A_ + 64], in_=dw[:, A_:A_ + 64], func=AF.Exp), [dw], [dw])
                            yield
                            S.op("dve", lambda e: e.tensor_tensor(out=dw[:, 192:256], in0=dw[:, A_:A_ + 64], in1=dw[:, 0:64],
                                                                   op=ALU.mult), [dw], [dw])
                            yield
                            c3v = c3[:, :].rearrange("p (d j h) -> p d j h", d=2, j=3)
                            csv = dw[:, 320:384].rearrange("p (d h) -> p d h", d=2)
                            r1 = dw[:, E_:E_ + 64].rearrange("p (d h) -> p d h", d=2)
                            r2 = dw[:, R_:R_ + 64].rearrange("p (d h) -> p d h", d=2)
                            S.op("dve", lambda e: e.tensor_copy(out=c3v[:, :, 0, :], in_=csv), [dw], [c3])
                            yield
                            S.op("dve", lambda e: e.tensor_tensor(out=r1, in0=csv, in1=c3v[:, :, 0, :], op=ALU.subtract), [dw, c3], [dw])
                            yield
                            S.op("dve", lambda e: e.tensor_copy(out=c3v[:, :, 1, :], in_=r1), [dw], [c3])
                            yield
                            S.op("dve", lambda e: e.tensor_tensor(out=r2, in0=r1, in1=c3v[:, :, 1, :], op=ALU.subtract), [dw, c3], [dw])
                            yield
                            S.op("dve", lambda e: e.tensor_copy(out=c3v[:, :, 2, :], in_=r2), [dw], [c3])
                            yield
                            S.dma(DTS[r0:r0 + 128, :], dw[:, 0:384], dw, reads=[dw])
                            yield
                            S.dma(CS3[r0:r0 + 128, :], c3[:], c3, reads=[c3])
                            yield

                        NBD = 4
                        for b0 in range(0, HCH, NBD):
                            gens = [dt_chunk(c) for c in range(b0, min(b0 + NBD, HCH))]
                            while gens:
                                for g_ in list(gens):
                                    try:
                                        next(g_)
                                    except StopIteration:
                                        gens.remove(g_)
                        S.barrier()
                        S.release(wf_r.bufs + wb_r.bufs + xc_r.bufs + stg_r.bufs + dt_r.bufs + c3_r.bufs)

            for d in range(2):
                with contextlib.ExitStack() as st:
                    if d == 1:
                        wbs = sb(st, "wbs", [128, 16, 1024], BF16)
                        for q in range(4):
                            S.dma(wbs[:, q * 4:(q + 1) * 4, :], w_bs[l, q * 512:(q + 1) * 512, :].rearrange("(k p) c -> p k c", p=128),
                                  wbs, writes=[wbs], q="pool")
                    xbw_r = Ring([sb(st, f"xbw{i}", [128, 32, 128], BF16) for i in range(2)])
                    dts_r = Ring([sb(st, f"dts{i}", [128, 384], F32) for i in range(2)])
                    cs3_r = Ring([sb(st, f"cs3{i}", [128, 192], BF16) for i in range(2)])
                    xs_r = Ring([sb(st, f"xs{i}", [128, 2048], BF16) for i in range(2)])
                    bt_r = Ring([sb(st, f"bt{i}", [128, 1024], BF16) for i in range(2)])
                    csT_r = Ring([sb(st, f"csT{i}", [96, 128], BF16) for i in range(2)])
                    scs_r = Ring([sb(st, f"scs{i}", [128, 1024], BF16) for i in range(2)])
                    xde_r = Ring([sb(st, f"xde{i}", [128, 2048], BF16) for i in range(2)])
                    dec_r = Ring([sb(st, f"dec{i}", [128, 4, 128], BF16) for i in range(3)])
                    M_r = Ring([sb(st, f"M{i}", [128, 4, 128], BF16) for i in range(3)])
                    tmp_r = Ring([sb(st, f"tmp{i}", [128, 256], F32) for i in range(3)])
                    yacc_r = Ring([sb(st, f"yacc{i}", [128, 2048], F32) for i in range(2 + d)])
                    for yb in yacc_r.bufs:
                        yb.views = [yb] + [Buf(yb.t, yb.name + f"_v{i}") for i in range(1, 8)]
                    Hf = sb(st, "Hf", [128, 2048], F32)
                    Hb = sb(st, "Hb", [128, 2048], BF16)
                    Hfv = [Hf] + [Buf(Hf.t, f"Hf_v{i}") for i in range(1, 8)]
                    Hbv = [Hb] + [Buf(Hb.t, f"Hb_v{i}") for i in range(1, 8)]
                    p_sc = ps(st, "p_sc", [128, 1024], F32)
                    p_cb_r = Ring([ps(st, f"p_cb{i}", [128, 512], F32) for i in range(2)])
                    pbk = [ps(st, f"pbk{i}", [128, 512], F32) for i in range(3)]
                    p_T = ps(st, "p_T", [128, 1024], BF16)
                    if d == 1:
                        zs_r = Ring([sb(st, f"zs{i}", [128, 2048], BF16) for i in range(3)])
                        g0_r = Ring([sb(st, f"g0{i}", [128, 1024], BF16) for i in range(3)])
                        ynb = sb(st, "ynb", [128, 2048], BF16)
                        ynT = sb(st, "ynT", [128, 16, 128], BF16)
                        gs = sb(st, "gs", [128, 32], F32)
                        m0_r = Ring([sb(st, f"m0{i}", [128, 1024], F32) for i in range(2)])
                    S.op("pool", lambda e: e.memset(Hf[:], 0.0), [], Hfv)
                    S.op("pool", lambda e: e.memset(Hb[:], 0.0), [], Hbv)

                    order = list(range(NCH)) if d == 0 else list(range(NCH - 1, -1, -1))
                    def sweep_loads(c):
                        r0 = c * 128
                        Bn = dict(xbw=xbw_r.next(), dts=dts_r.next(), cs3=cs3_r.next(), xs=xs_r.next(), bt=bt_r.next(), yacc=yacc_r.next())
                        S.dma(Bn["dts"][:], DTS[r0:r0 + 128, :], Bn["dts"], writes=[Bn["dts"]])
                        S.dma(Bn["cs3"][:], CS3[r0:r0 + 128, :], Bn["cs3"], writes=[Bn["cs3"]])
                        if d == 0:
                            S.dma(Bn["xbw"][:], XBC[c, :, :, :], Bn["xbw"], writes=[Bn["xbw"]])
                        else:
                            Bn["zs"] = zs_r.next(); Bn["g0"] = g0_r.next()
                            S.dma(Bn["xbw"][:, 16:32, :], XBC[c, :, 16:32, :], Bn["xbw"], writes=[Bn["xbw"]])
                            S.dma(Bn["xs"][:], XS[r0:r0 + 128, :], Bn["xs"], writes=[Bn["xs"]])
                            S.dma(Bn["bt"][:], BTM[r0:r0 + 128, :], Bn["bt"], writes=[Bn["bt"]])
                            S.dma(Bn["yacc"][:], Y0[r0:r0 + 128, :], Bn["yacc"], writes=Bn["yacc"].views)
                            S.dma(Bn["zs"][:], PJ[r0:r0 + 128, PJ_Z:PJ_Z + 2048], Bn["zs"], writes=[Bn["zs"]])
                            S.dma(Bn["g0"][:], PJ[r0:r0 + 128, PJ_MG:PJ_MG + 1024], Bn["g0"], writes=[Bn["g0"]])
                        return Bn

                    for db in dec_r.bufs:
                        db.views = [db] + [Buf(db.t, db.name + f"_v{i}") for i in range(1, 4)]

                    def prep_a(c, Bn):
                        r0 = c * 128
                        xbw = Bn["xbw"]; dts = Bn["dts"]; cs3 = Bn["cs3"]; xs = Bn["xs"]; bt = Bn["bt"]; yacc = Bn["yacc"]
                        Bn["csT"] = csT_r.next(); Bn["xde"] = xde_r.next()
                        csT = Bn["csT"]; xde = Bn["xde"]
                        if d == 0:
                            for half in range(3):
                                for j in range(8):
                                    S.op("pe", lambda e, j=j: e.transpose(out=p_T[:, j * 128:(j + 1) * 128],
                                                                          in_=xbw[:, half * 8 + j, :], identity=ident[:]),
                                         [xbw, ident], [p_T])
                                dstb = xs[:, half * 1024:(half + 1) * 1024] if half < 2 else bt[:]
                                dbuf = xs if half < 2 else bt
                                S.op("act", lambda e: e.activation(out=dstb, in_=p_T[:, :], func=AF.Copy), [p_T], [dbuf])
                            S.dma(XS[r0:r0 + 128, :], xs[:], xs, reads=[xs])
                            S.dma(BTM[r0:r0 + 128, :], bt[:], bt, reads=[bt])
                        S.op("pe", lambda e: e.transpose(out=p_T[0:96, 0:128], in_=cs3[:, d * 96:(d + 1) * 96], identity=ident[:]),
                             [cs3, ident], [p_T])
                        S.op("act", lambda e: e.activation(out=csT[:], in_=p_T[0:96, 0:128], func=AF.Copy), [p_T], [csT])
                        dtd = dts[:, d * 32:(d + 1) * 32]
                        wend = dts[:, 192 + d * 32:192 + (d + 1) * 32]
                        xs3 = xs[:, :].rearrange("p (h q) -> p h q", q=64)
                        S.op("dve", lambda e: e.tensor_tensor(out=xde[:, :].rearrange("p (h q) -> p h q", q=64), in0=xs3,
                                                               in1=bc3(wend, 64), op=ALU.mult), [xs, dts], [xde])
                        if d == 0:
                            S.op("dve", lambda e: e.tensor_tensor(out=yacc[:, :].rearrange("p (h q) -> p h q", q=64), in0=xs3,
                                                                   in1=bc3(dsk_bc, 64), op=ALU.mult), [xs, prow], yacc.views)

                    def prep_b(c, Bn):
                        xbw = Bn["xbw"]
                        Bn["scs"] = scs_r.next()
                        scs_ = Bn["scs"]
                        for g in range(8):
                            S.op("pe", lambda e, g=g: e.matmul(p_sc[:, g * 128:(g + 1) * 128], lhsT=xbw[:, 16 + g, :],
                                                               rhs=xbw[:, 24 + g, :], start=True, stop=True), [xbw], [p_sc])
                        S.op("act", lambda e: e.activation(out=scs_[:], in_=p_sc[:, :], func=AF.Copy), [p_sc], [scs_])

                    pending_post = []
                    pend = [sweep_loads(order[0])]
                    prep_a(order[0], pend[0])
                    prep_b(order[0], pend[0])
                    for ci, c in enumerate(order):
                        r0 = c * 128
                        Bn = pend.pop(0)
                        xbw = Bn["xbw"]; dts = Bn["dts"]; cs3 = Bn["cs3"]; xs = Bn["xs"]; bt = Bn["bt"]; yacc = Bn["yacc"]
                        csT = Bn["csT"]; xde = Bn["xde"]; scs = Bn["scs"]
                        if d == 1:
                            zs = Bn["zs"]; g0 = Bn["g0"]
                        GB = [None] * 8

                        def stage_a(g):
                            G_ = dict(p_cb=p_cb_r.next(), dec=dec_r.next(), M=M_r.next(), tmp=tmp_r.next())
                            GB[g] = G_
                            p_cb = G_["p_cb"]; dec = G_["dec"]; tmp = G_["tmp"]
                            bkA = pbk[g % 2]
                            p_yo = bkA.t[:, 0:256]; p_st = bkA.t[:, 256:512]
                            gc = slice(g * 256, (g + 1) * 256)
                            S.op("pe", lambda e: e.matmul(p_yo[:, :], lhsT=xbw[:, 24 + g, :], rhs=Hb[:, gc], start=True, stop=True),
                                 [xbw, Hbv[g]], [bkA])
                            S.op("pe", lambda e: e.matmul(p_st[:, :], lhsT=bt[:, g * 128:(g + 1) * 128], rhs=xde[:, gc], start=True, stop=True),
                                 [bt, xde], [bkA])
                            S.op("pe", lambda e: e.matmul(p_cb[:, :], lhsT=ident[:], rhs=negm[:, d * 512:(d + 1) * 512], start=True, stop=False),
                                 [ident, negm], [p_cb])
                            for r in range(4):
                                h = g * 4 + r
                                S.op("pe", lambda e, r=r, h=h: e.matmul(p_cb[:, r * 128:(r + 1) * 128], lhsT=esel[:, h * 128:(h + 1) * 128],
                                                                        rhs=csT[:], start=False, stop=(r == 3)), [esel, csT], [p_cb])
                            for r in range(4):
                                h = g * 4 + r
                                S.op("act", lambda e, r=r, h=h: e.activation(out=dec[:, r, :], in_=p_cb[:, r * 128:(r + 1) * 128],
                                                                             func=AF.Exp, bias=dts[:, 64 + d * 32 + h:64 + d * 32 + h + 1]),
                                     [p_cb, dts], [dec.views[r]])
                            cdec = dts[:, 256 + d * 32 + g * 4:256 + d * 32 + g * 4 + 4]
                            S.op("dve", lambda e: e.tensor_tensor(out=Hf[:, gc].rearrange("p (r q) -> p r q", q=64),
                                                                   in0=Hf[:, gc].rearrange("p (r q) -> p r q", q=64),
                                                                   in1=bc3(cdec, 64), op=ALU.mult), [Hfv[g], dts], [Hfv[g]])
                            ecs = dts[:, 128 + d * 32 + g * 4:128 + d * 32 + g * 4 + 4]
                            S.op("dve", lambda e: e.tensor_tensor(out=tmp[:, :].rearrange("p (r q) -> p r q", q=64),
                                                                   in0=p_yo[:, :].rearrange("p (r q) -> p r q", q=64),
                                                                   in1=bc3(ecs, 64), op=ALU.mult), [bkA, dts], [tmp])
                            S.op("dve", lambda e: e.tensor_tensor(out=Hf[:, gc], in0=p_st[:, :], in1=Hf[:, gc], op=ALU.add),
                                 [bkA, Hfv[g]], [Hfv[g]])
                            S.op("act", lambda e: e.activation(out=Hb[:, gc], in_=Hf[:, gc], func=AF.Copy), [Hfv[g]], [Hbv[g]])
                            S.op("dve", lambda e: e.tensor_tensor(out=yacc[:, gc], in0=yacc[:, gc], in1=tmp[:], op=ALU.add),
                                 [yacc.views[g], tmp], [yacc.views[g]])

                        def stage_b(g):
                            G_ = GB[g]
                            dec = G_["dec"]; M = G_["M"]
                            bky = pbk[2]
                            p_y = bky.t[:, (g % 2) * 256:(g % 2 + 1) * 256]
                            scg = scs[:, g * 128:(g + 1) * 128].unsqueeze(1).broadcast_to([128, 4, 128])
                            S.op("dve", lambda e: e.tensor_tensor(out=M[:], in0=dec[:], in1=scg, op=ALU.mult), dec.views + [scs], [M])
                            for r in range(4):
                                h = g * 4 + r
                                S.op("pe", lambda e, r=r, h=h: e.matmul(p_y[:, r * 64:(r + 1) * 64], lhsT=M[:, r, :],
                                                                        rhs=xs[:, h * 64:(h + 1) * 64], start=True, stop=True), [M, xs], [bky])

                        def stage_c(g):
                            bky = pbk[2]
                            p_y = bky.t[:, (g % 2) * 256:(g % 2 + 1) * 256]
                            gc = slice(g * 256, (g + 1) * 256)
                            S.op("dve", lambda e: e.tensor_tensor(out=yacc[:, gc], in0=p_y[:, :], in1=yacc[:, gc], op=ALU.add),
                                 [bky, yacc.views[g]], [yacc.views[g]])

                        for step in range(10):
                            if step < 8:
                                stage_a(step)
                            if 2 <= step <= 9:
                                stage_c(step - 2)
                            if 1 <= step <= 8:
                                stage_b(step - 1)
                            if step == 1 and ci + 1 < len(order):
                                pend.append(sweep_loads(order[ci + 1]))
                            if step >= 1:
                                for pg in list(pending_post):
                                    try:
                                        next(pg)
                                    except StopIteration:
                                        pending_post.remove(pg)
                            if step == 5 and ci + 1 < len(order):
                                prep_a(order[ci + 1], pend[0])
                        if d == 0:
                            S.dma(Y0[r0:r0 + 128, :], yacc[:], yacc, reads=yacc.views)
                        else:
                            def post(r0=r0, yacc=yacc, zs=zs, g0=g0):
                                m0 = m0_r.next()
                                yz = yacc
                                S.op("dve", lambda e: e.tensor_tensor(out=yz[:], in0=yacc[:], in1=zs[:], op=ALU.mult), yacc.views + [zs], yacc.views)
                                yield
                                for g in range(8):
                                    S.op("act", lambda e, g=g: e.activation(out=ynb[:, g * 256:(g + 1) * 256], in_=yz[:, g * 256:(g + 1) * 256],
                                                                            func=AF.Square, accum_out=gs[:, g:g + 1]), [yz], [ynb, gs])
                                yield
                                S.op("dve", lambda e: e.tensor_scalar(out=gs[:, 8:16], in0=gs[:, 0:8], scalar1=1.0 / 256, scalar2=EPS,
                                                                       op0=ALU.mult, op1=ALU.add), [gs], [gs])
                                S.op("act", lambda e: e.activation(out=gs[:, 16:24], in_=gs[:, 8:16], func=AF.Ln), [gs], [gs])
                                S.op("act", lambda e: e.activation(out=gs[:, 24:32], in_=gs[:, 16:24], func=AF.Exp, scale=-0.5), [gs], [gs])
                                yield
                                for g in range(8):
                                    S.op("dve", lambda e, g=g: e.scalar_tensor_tensor(
                                        out=ynb[:, g * 256:(g + 1) * 256], in0=yz[:, g * 256:(g + 1) * 256], scalar=gs[:, 24 + g:25 + g],
                                        in1=prow[:, G_SSD + g * 256:G_SSD + (g + 1) * 256], op0=ALU.mult, op1=ALU.mult), [yz, gs, prow], [ynb])
                                for half in range(2):
                                    yield
                                    for j in range(8):
                                        S.op("pe", lambda e, j=j: e.transpose(out=p_T[:, j * 128:(j + 1) * 128],
                                                                              in_=ynb[:, (half * 8 + j) * 128:(half * 8 + j + 1) * 128],
                                                                              identity=ident[:]), [ynb, ident], [p_T])
                                    S.op("act", lambda e: e.activation(out=ynT[:, half * 8:(half + 1) * 8, :],
                                                                        in_=p_T[:, :].rearrange("p (k t) -> p k t", t=128), func=AF.Copy),
                                         [p_T], [ynT])
                                for nb in range(2):
                                    yield
                                    for k in range(16):
                                        S.op("pe", lambda e, k=k: e.matmul(p_sc[:, nb * 512:(nb + 1) * 512], lhsT=ynT[:, k, :],
                                                                           rhs=wbs[:, k, nb * 512:(nb + 1) * 512], start=(k == 0), stop=(k == 15)),
                                             [ynT, wbs], [p_sc])
                                S.op("dve", lambda e: e.tensor_tensor(out=m0[:], in0=p_sc[:, :], in1=g0[:], op=ALU.mult), [p_sc, g0], [m0])
                                S.dma(M0[r0:r0 + 128, :], m0[:], m0, reads=[m0])
                            pending_post.append(post())
                        if ci + 1 < len(order):
                            for pg in list(pending_post):
                                if pg is not pending_post[-1] or d == 0:
                                    for _ in pg:
                                        pass
                                    pending_post.remove(pg)
                            prep_b(order[ci + 1], pend[0])
                    for pg in pending_post:
                        for _ in pg:
                            pass
                    S.barrier()
                    rel = xbw_r.bufs + dts_r.bufs + cs3_r.bufs + xs_r.bufs + bt_r.bufs + yacc_r.bufs
                    if d == 1:
                        rel += zs_r.bufs + g0_r.bufs + m0_r.bufs + [wbs]
                    S.release(rel)

            with contextlib.ExitStack() as st:
                wbg = sb(st, "wbg", [128, 8, 1024], BF16)
                wbx = sb(st, "wbx", [128, 8, 1024], BF16)
                wo = sb(st, "wo", [128, 8, 1024], BF16)
                kT = sb(st, "kT", [128, 8, 256], BF16)
                Vv = sb(st, "Vv", [128, 2, 1024], BF16)
                pA = ps(st, "pA", [128, 1024], F32)
                pB = ps(st, "pB", [128, 1024], F32)
                pC = ps(st, "pC", [128, 1024], F32)
                p_T = ps(st, "p5_T", [128, 1024], BF16)
                p_den = ps(st, "p_den", [128, 8], F32)
                with contextlib.ExitStack() as stw:
                    memT = sb(stw, "memT", [128, 8, 256], BF16)
                    mx = sb(stw, "mx", [128, 1024], F32)
                    mjunk = sb(stw, "mjunk", [128, 1024], BF16)
                    mh = sb(stw, "mh", [128, 1024], BF16)
                    mss = sb(stw, "mss", [128, 4], F32)
                    wkb = sb(stw, "wkb", [128, 8, 2048], BF16)
                    for q in range(4):
                        S.dma(wkb[:, :, q * 512:(q + 1) * 512], w_kv[l, :, q * 512:(q + 1) * 512].rearrange("(k p) c -> p k c", p=128),
                              wkb, writes=[wkb], q="pool")
                    for (wsrc, wdst) in ((w_bg, wbg), (w_bx, wbx), (w_o, wo)):
                        for q in range(2):
                            S.dma(wdst[:, q * 4:(q + 1) * 4, :], wsrc[l, q * 512:(q + 1) * 512, :].rearrange("(k p) c -> p k c", p=128),
                                  wdst, writes=[wdst], q="pool")
                    for mt in range(2):
                        S.dma(mx[:], mem_in[mt * 128:(mt + 1) * 128, :], mx, writes=[mx])
                        S.op("act", lambda e: e.activation(out=mjunk[:], in_=mx[:], func=AF.Square, accum_out=mss[:, 0:1]), [mx], [mjunk, mss])
                        S.op("dve", lambda e: e.tensor_scalar(out=mss[:, 1:2], in0=mss[:, 0:1], scalar1=1.0 / D, scalar2=EPS,
                                                               op0=ALU.mult, op1=ALU.add), [mss], [mss])
                        S.op("act", lambda e: e.activation(out=mss[:, 2:3], in_=mss[:, 1:2], func=AF.Ln), [mss], [mss])
                        S.op("act", lambda e: e.activation(out=mss[:, 3:4], in_=mss[:, 2:3], func=AF.Exp, scale=-0.5), [mss], [mss])
                        S.op("dve", lambda e: e.scalar_tensor_tensor(out=mh[:], in0=mx[:], scalar=mss[:, 3:4],
                                                                      in1=prow[:, G_MEM:G_MEM + 1024], op0=ALU.mult, op1=ALU.mult),
                             [mx, mss, prow], [mh])
                        for k in range(8):
                            S.op("pe", lambda e, k=k: e.transpose(out=p_T[:, k * 128:(k + 1) * 128], in_=mh[:, k * 128:(k + 1) * 128],
                                                                  identity=ident[:]), [mh, ident], [p_T])
                        S.op("act", lambda e, mt=mt: e.activation(out=memT[:, :, mt * 128:(mt + 1) * 128],
                                                                  in_=p_T[:, :].rearrange("p (k t) -> p k t", t=128), func=AF.Copy),
                             [p_T], [memT])
                    for j in range(8):
                        for k in range(8):
                            S.op("pe", lambda e, k=k, j=j: e.matmul(pA[:, 0:256], lhsT=wkb[:, k, j * 128:(j + 1) * 128], rhs=memT[:, k, :],
                                                                    start=(k == 0), stop=(k == 7)), [wkb, memT], [pA])
                        S.op("act", lambda e, j=j: e.activation(out=kT[:, j, :], in_=pA[:, 0:256], func=AF.Copy), [pA], [kT])
                    for mt in range(2):
                        for nb in range(2):
                            for k in range(8):
                                S.op("pe", lambda e, k=k, mt=mt, nb=nb: e.matmul(pB[:, 0:512], lhsT=memT[:, k, mt * 128:(mt + 1) * 128],
                                                                                 rhs=wkb[:, k, 1024 + nb * 512:1024 + (nb + 1) * 512],
                                                                                 start=(k == 0), stop=(k == 7)), [wkb, memT], [pB])
                            S.op("act", lambda e, mt=mt, nb=nb: e.activation(out=Vv[:, mt, nb * 512:(nb + 1) * 512], in_=pB[:, 0:512],
                                                                             func=AF.Copy), [pB], [Vv])
                    S.barrier()
                    S.release([mx, wkb, wbg, wbx, wo])

                pj_r = Ring([sb(st, f"pj{i}", [128, PJ_W - 2048], BF16) for i in range(2)])
                xq_r = Ring([sb(st, f"xq{i}", [128, 8, 128], BF16) for i in range(2)])
                m0_r = Ring([sb(st, f"m05{i}", [128, 1024], F32) for i in range(2)])
                xi_r = Ring([sb(st, f"xi5{i}", [128, 1024], F32) for i in range(2)])
                xo_r = Ring([sb(st, f"xo5{i}", [128, 1024], F32) for i in range(2)])
                bst = sb(st, "bst", [128, 16], F32)
                vt = sb(st, "vt", [128, 1024], F32)
                vn = sb(st, "vn", [128, 1024], BF16)
                svt = sb(st, "svt", [128, 1024], F32)
                yg = sb(st, "yg", [128, 1024], BF16)
                tT = sb(st, "tT", [128, 8, 128], BF16)
                Eb = sb(st, "Eb", [128, 8, 128], BF16)
                rden = sb(st, "rden", [128, 8], F32)
                ot = sb(st, "ot", [128, 1024], F32)
                yx = sb(st, "yx", [128, 1024], BF16)
                macc = sb(st, "macc", [128, 1024], F32)
                mb = sb(st, "mb", [128, 1024], BF16)
                pjunk = sb(st, "pjunk", [128, 1024], BF16)
                pss = sb(st, "pss", [128, 4], F32)

                def transpose8(src, srcbuf):
                    for k in range(8):
                        S.op("pe", lambda e, k=k: e.transpose(out=p_T[:, k * 128:(k + 1) * 128], in_=src[:, k * 128:(k + 1) * 128],
                                                              identity=ident[:]), [srcbuf, ident], [p_T])
                    S.op("act", lambda e: e.activation(out=tT[:], in_=p_T[:, :].rearrange("p (k t) -> p k t", t=128), func=AF.Copy),
                         [p_T], [tT])

                def proj(pdst, wsb):
                    for nb in range(2):
                        for k in range(8):
                            S.op("pe", lambda e, k=k, nb=nb: e.matmul(pdst[:, nb * 512:(nb + 1) * 512], lhsT=tT[:, k, :],
                                                                      rhs=wsb[:, k, nb * 512:(nb + 1) * 512], start=(k == 0), stop=(k == 7)),
                                 [tT, wsb], [pdst])

                def p5_loads(c):
                    r0 = c * 128
                    Bn = dict(pj=pj_r.next(), xq=xq_r.next(), m0=m0_r.next(), xi=xi_r.next())
                    S.dma(Bn["pj"][:], PJ[r0:r0 + 128, 2048:PJ_W], Bn["pj"], writes=[Bn["pj"]])
                    S.dma(Bn["xq"][:], XQ[c, :, :, :], Bn["xq"], writes=[Bn["xq"]])
                    S.dma(Bn["m0"][:], M0[r0:r0 + 128, :], Bn["m0"], writes=[Bn["m0"]])
                    S.dma(Bn["xi"][:], Xsrc[r0:r0 + 128, :], Bn["xi"], writes=[Bn["xi"]])
                    return Bn

                def views(Bn):
                    pj = Bn["pj"]
                    o = -2048
                    return dict(pj=pj, xq=Bn["xq"], m0=Bn["m0"], xi=Bn["xi"],
                                sgg=pj[:, PJ_GG + o:PJ_GG + o + 1024], uu=pj[:, PJ_GUV + o:PJ_GUV + o + 1024],
                                vv=pj[:, PJ_GUV + o + 1024:PJ_GUV + o + 2048], sxg=pj[:, PJ_XG + o:PJ_XG + o + 1024],
                                g1=pj[:, PJ_MG + o + 1024:PJ_MG + o + 2048], g2=pj[:, PJ_MG + o + 2048:PJ_MG + o + 3072])

                def head(c, Bn):
                    V_ = views(Bn)
                    pj = V_["pj"]; xq = V_["xq"]; vv = V_["vv"]
                    for h in range(4):
                        for mt in range(2):
                            for dc in range(2):
                                S.op("pe", lambda e, h=h, mt=mt, dc=dc: e.matmul(
                                    pA[:, (h * 2 + mt) * 128:(h * 2 + mt + 1) * 128], lhsT=kT[:, h * 2 + dc, mt * 128:(mt + 1) * 128],
                                    rhs=xq[:, h * 2 + dc, :], start=(dc == 0), stop=(dc == 1)), [kT, xq], [pA])
                    yield
                    S.op("act", lambda e: e.activation(out=Eb[:], in_=pA[:, :].rearrange("p (j t) -> p j t", t=128), func=AF.Exp,
                                                        scale=1.0 / 16.0), [pA], [Eb])
                    yield
                    for i in range(2):
                        S.op("dve", lambda e, i=i: e.bn_stats(out=bst[:, i * 6:(i + 1) * 6], in_=vv[:, i * 512:(i + 1) * 512]), [pj], [bst])
                    S.op("dve", lambda e: e.bn_aggr(out=bst[:, 12:14], in_=bst[:, 0:12]), [bst], [bst])
                    S.op("dve", lambda e: e.tensor_scalar(out=bst[:, 14:15], in0=bst[:, 13:14], scalar1=EPS, scalar2=None, op0=ALU.add),
                         [bst], [bst])
                    S.op("act", lambda e: e.activation(out=bst[:, 14:15], in_=bst[:, 14:15], func=AF.Ln), [bst], [bst])
                    S.op("act", lambda e: e.activation(out=bst[:, 15:16], in_=bst[:, 14:15], func=AF.Exp, scale=-0.5), [bst], [bst])
                    yield
                    S.op("dve", lambda e: e.tensor_scalar(out=vt[:], in0=vv, scalar1=bst[:, 12:13], scalar2=bst[:, 15:16],
                                                           op0=ALU.subtract, op1=ALU.mult), [pj, bst], [vt])
                    S.op("dve", lambda e: e.tensor_tensor(out=vt[:], in0=vt[:], in1=prow[:, G_LNG:G_LNG + 1024], op=ALU.mult), [vt, prow], [vt])
                    S.op("dve", lambda e: e.tensor_tensor(out=vn[:], in0=vt[:], in1=prow[:, G_LNB:G_LNB + 1024], op=ALU.add), [vt, prow], [vn])
                    yield
                    for g in range(8):
                        S.op("pe", lambda e, g=g: e.matmul(pB[:, g * 128:(g + 1) * 128], lhsT=wsT[:, g * 128:(g + 1) * 128],
                                                           rhs=vn[:, g * 128:(g + 1) * 128], start=True, stop=True), [wsT, vn], [pB])

                def mid(c, Bn):
                    V_ = views(Bn)
                    pj = V_["pj"]; m0 = V_["m0"]
                    for h in range(4):
                        for mt in range(2):
                            S.op("pe", lambda e, h=h, mt=mt: e.matmul(pC[:, h * 256:(h + 1) * 256], lhsT=Eb[:, h * 2 + mt, :],
                                                                      rhs=Vv[:, mt, h * 256:(h + 1) * 256], start=(mt == 0), stop=(mt == 1)),
                                 [Eb, Vv], [pC])
                    for h in range(4):
                        for mt in range(2):
                            S.op("pe", lambda e, h=h, mt=mt: e.matmul(p_den[:, h * 2:h * 2 + 2], lhsT=Eb[:, h * 2 + mt, :], rhs=onesb[:],
                                                                      start=(mt == 0), stop=(mt == 1)), [Eb, onesb], [p_den])
                    yield
                    S.op("dve", lambda e: e.tensor_tensor(out=svt[:, :].rearrange("p (g q) -> p g q", q=128),
                                                           in0=pB[:, :].rearrange("p (g q) -> p g q", q=128),
                                                           in1=bc3(bsT[:, 0:8], 128), op=ALU.add), [pB, bsT], [svt])
                    S.op("dve", lambda e: e.tensor_tensor(out=svt[:], in0=svt[:], in1=V_["uu"], op=ALU.mult), [svt, pj], [svt])
                    S.op("dve", lambda e: e.tensor_tensor(out=yg[:], in0=svt[:], in1=V_["sgg"], op=ALU.mult), [svt, pj], [yg])
                    S.op("dve", lambda e: e.reciprocal(out=rden[:], in_=p_den[:, :]), [p_den], [rden])
                    rd4 = rden[:, :].rearrange("p (h two) -> p h two", two=2)[:, :, 0:1].broadcast_to([128, 4, 256])
                    S.op("dve", lambda e: e.tensor_tensor(out=ot[:, :].rearrange("p (h q) -> p h q", q=256),
                                                           in0=pC[:, :].rearrange("p (h q) -> p h q", q=256), in1=rd4, op=ALU.mult),
                         [pC, rden], [ot])
                    S.op("dve", lambda e: e.tensor_tensor(out=yx[:], in0=ot[:], in1=V_["sxg"], op=ALU.mult), [ot, pj], [yx])
                    yield
                    transpose8(yg, yg)
                    yield
                    proj(pA, wbg)
                    yield
                    S.op("dve", lambda e: e.tensor_tensor(out=macc[:], in0=pA[:, :], in1=V_["g1"], op=ALU.mult), [pA, pj], [macc])
                    S.op("dve", lambda e: e.tensor_tensor(out=macc[:], in0=macc[:], in1=m0[:], op=ALU.add), [macc, m0], [macc])
                    transpose8(yx, yx)
                    proj(pB, wbx)
                    S.op("dve", lambda e: e.tensor_tensor(out=ot[:], in0=pB[:, :], in1=V_["g2"], op=ALU.mult), [pB, pj], [ot])
                    S.op("dve", lambda e: e.tensor_tensor(out=mb[:], in0=ot[:], in1=macc[:], op=ALU.add), [ot, macc], [mb])

                def tail(c, Bn):
                    r0 = c * 128
                    xi = Bn["xi"]; xo = xo_r.next()
                    transpose8(mb, mb)
                    proj(pC, wo)
                    S.op("act", lambda e: e.activation(out=pjunk[:], in_=pC[:, :], func=AF.Square, accum_out=pss[:, 0:1]), [pC], [pjunk, pss])
                    S.op("dve", lambda e: e.tensor_scalar(out=pss[:, 1:2], in0=pss[:, 0:1], scalar1=1.0 / D, scalar2=EPS,
                                                           op0=ALU.mult, op1=ALU.add), [pss], [pss])
                    S.op("act", lambda e: e.activation(out=pss[:, 2:3], in_=pss[:, 1:2], func=AF.Ln), [pss], [pss])
                    S.op("act", lambda e: e.activation(out=pss[:, 3:4], in_=pss[:, 2:3], func=AF.Exp, scale=-0.5), [pss], [pss])
                    S.op("dve", lambda e: e.scalar_tensor_tensor(out=xo[:], in0=pC[:, :], scalar=pss[:, 3:4],
                                                                  in1=prow[:, G_POST:G_POST + 1024], op0=ALU.mult, op1=ALU.mult),
                         [pC, pss, prow], [xo])
                    S.op("dve", lambda e: e.tensor_tensor(out=xo[:], in0=xo[:], in1=xi[:], op=ALU.add), [xo, xi], [xo])
                    S.dma(Xdst[r0:r0 + 128, :], xo[:], xo, reads=[xo])

                def drain(gen):
                    for _ in gen:
                        pass

                cur = p5_loads(0)
                drain(head(0, cur))
                for c in range(NCH):
                    nxt = p5_loads(c + 1) if c + 1 < NCH else None
                    gm = mid(c, cur)
                    gh = head(c + 1, nxt) if nxt is not None else iter(())
                    for _ in range(4):
                        next(gm, None)
                        next(gh, None)
                    drain(gm)
                    drain(gh)
                    tail(c, cur)
                    cur = nxt
                S.barrier()
                S.release(pj_r.bufs + xq_r.bufs + m0_r.bufs + xi_r.bufs + xo_r.bufs)
        S.barrier()
    return nc


def host_consts():
    ident = np.eye(128, dtype=np.float32)
    k = np.arange(128)[:, None]
    t = np.arange(128)[None, :]
    tri0 = (k <= t).astype(np.float32)
    tri1 = (k >= t).astype(np.float32)
    esel = np.zeros((96, 32, 128), np.float32)
    for h in range(32):
        for j in range(3):
            esel[j * 32 + h, h, :] = 1.0
    nm0 = np.where(k > t, -30000.0, 0.0).astype(np.float32)
    nm1 = np.where(k < t, -30000.0, 0.0).astype(np.float32)
    c_nm = np.concatenate([np.tile(nm0, (1, 4)), np.tile(nm1, (1, 4))], axis=1)
    return {"c_ident": ident, "c_tri0": tri0, "c_tri1": tri1, "c_esel": esel.reshape(96, 32 * 128), "c_nm": c_nm}


def host_params(inp, depth):
    f = lambda a: np.asarray(a, dtype=np.float32)
    p_row = np.zeros((depth, 1, 9 * 1024), np.float32)
    p_row[:, 0, 0:1024] = f(inp["norm_pre_g"])[:depth]
    p_row[:, 0, 1024:3072] = f(inp["ssd_norm_g"])[:depth]
    p_row[:, 0, 3072:4096] = f(inp["gmlp_ln_g"])[:depth]
    p_row[:, 0, 4096:5120] = f(inp["gmlp_ln_b"])[:depth]
    p_row[:, 0, 5120:6144] = f(inp["mem_norm_g"])[:depth]
    p_row[:, 0, 6144:7168] = f(inp["norm_post_g"])[:depth]
    p_row[:, 0, 7168:7232] = f(inp["dt_bias"])[:depth].reshape(depth, 64)
    p_row[:, 0, 7232:7296] = f(inp["a_log"])[:depth].reshape(depth, 64)
    p_row[:, 0, 7296:7328] = f(inp["d_skip"])[:depth]
    cw = f(inp["conv_w"])[:depth]
    p_convw = np.ascontiguousarray(cw.reshape(depth, 5, 32, 128).transpose(0, 3, 2, 1)).reshape(depth, 128, 160)
    cb = f(inp["conv_b"])[:depth]
    p_convb = np.ascontiguousarray(cb.reshape(depth, 32, 128).transpose(0, 2, 1))
    ws = f(inp["w_spatial"])[:depth]
    p_wsT = np.ascontiguousarray(ws.transpose(0, 3, 1, 2)).reshape(depth, 128, 1024)
    bs = f(inp["b_spatial"])[:depth]
    p_bsT = np.ascontiguousarray(bs.transpose(0, 2, 1))
    return {"p_row": p_row, "p_convw": p_convw, "p_convb": p_convb, "p_wsT": p_wsT, "p_bsT": p_bsT}


_NC_CACHE = {}


def kernel(**inputs):
    x = np.asarray(inputs["x"], dtype=np.float32)
    B, L, _ = x.shape
    depth = inputs["w_in"].shape[0]
    key = (L, depth)
    if key not in _NC_CACHE:
        _NC_CACHE[key] = build(L, depth)
    nc = _NC_CACHE[key]
    shared = {}
    shared.update(host_consts())
    shared.update(host_params(inputs, depth))
    for n in ("w_in", "w_kv", "w_br_ssd", "w_br_gmlp", "w_br_xattn", "w_out"):
        shared[n] = np.ascontiguousarray(np.asarray(inputs[n], dtype=np.float32))
    mem = np.asarray(inputs["mem"], dtype=np.float32)
    in_maps = []
    for b in range(B):
        m = dict(shared)
        m["x"] = np.ascontiguousarray(x[b])
        m["mem"] = np.ascontiguousarray(mem[b])
        in_maps.append(m)
    res = run_bass_kernel_spmd(nc, in_maps, core_ids=list(range(B)))
    return np.stack([np.asarray(res.results[b]["out"], dtype=np.float32) for b in range(B)], axis=0)
```

```python
import contextlib
import numpy as np
import concourse.bass as bass
import concourse.mybir as mybir
from concourse.bass_utils import run_bass_kernel_spmd

F32 = mybir.dt.float32
BF16 = mybir.dt.bfloat16
AF = mybir.ActivationFunctionType
ALU = mybir.AluOpType

D = 1024
NCORES = 4
EPS = 1e-6
SAME_ENGINE_WAITS = True
DMA_CAST = True

C_Z, C_XBC, C_DT, C_GG, C_GUV, C_XQ, C_XG, C_MG = 0, 2048, 6144, 6208, 7232, 9280, 10304, 11328
PJ_Z, PJ_GG, PJ_GUV, PJ_XG, PJ_MG, PJ_W = 0, 2048, 3072, 5120, 6144, 9216


class Buf:
    def __init__(self, t, name):
        self.t = t
        self.name = name
        self.w = None
        self.r = {}
        self.dsem = None

    def __getitem__(self, idx):
        return self.t[idx]


class Sched:
    def __init__(self, nc, es):
        self.nc = nc
        self.es = es
        self.eng = {"pe": nc.tensor, "dve": nc.vector, "act": nc.scalar, "pool": nc.gpsimd, "sp": nc.sync}
        self.esem = {}
        self.cnt = {}
        self.seen = {e: {} for e in self.eng}
        self.nsem = 0
        for e in ("pe", "dve", "act", "pool"):
            self.esem[e] = self.new_sem("e_" + e)
            self.cnt[e] = 0
        self.free_dsems = []
        self.all_dsems = []

    def new_sem(self, name):
        self.nsem += 1
        return self.es.enter_context(self.nc.semaphore(name + str(self.nsem)))

    def get_dsem(self):
        if self.free_dsems:
            return self.free_dsems.pop()
        s = [self.new_sem("d"), 0]
        self.all_dsems.append(s)
        return s

    def release(self, bufs):
        for b in bufs:
            if b.dsem is not None:
                self.free_dsems.append(b.dsem)
                b.dsem = None

    def _wait(self, e, tok):
        sem, val, seng = tok
        if seng == e and (e == "pe" or not SAME_ENGINE_WAITS):
            return
        k = id(sem)
        if self.seen[e].get(k, 0) >= val:
            return
        self.eng[e].wait_ge(sem, val)
        self.seen[e][k] = val

    def _deps(self, e, reads, writes):
        for b in reads:
            if b.w is not None:
                self._wait(e, b.w)
        for b in writes:
            if b.w is not None:
                self._wait(e, b.w)
            for tok in b.r.values():
                self._wait(e, tok)

    def op(self, e, fn, reads=(), writes=()):
        self._deps(e, reads, writes)
        if self.cnt[e] >= 30000:
            self.esem[e] = self.new_sem("e_" + e)
            self.cnt[e] = 0
        ins = fn(self.eng[e])
        self.cnt[e] += 1
        ins.then_inc(self.esem[e], 1)
        tok = (self.esem[e], self.cnt[e], e)
        for b in reads:
            b.r[e] = tok
        for b in writes:
            b.w = tok
            b.r = {}
        return ins

    def dma(self, out, in_, sb, reads=(), writes=(), q="sp"):
        self._deps(q, reads, writes)
        if sb.dsem is None:
            sb.dsem = self.get_dsem()
        ins = self.eng[q].dma_start(out=out, in_=in_)
        sb.dsem[1] += 16
        ins.then_inc(sb.dsem[0], 16)
        tok = (sb.dsem[0], sb.dsem[1], "dma")
        for b in reads:
            b.r[id(sb.dsem[0])] = tok
        for b in writes:
            b.w = tok
            b.r = {}

    def barrier(self):
        toks = [(self.esem[e], self.cnt[e], e) for e in self.esem if self.cnt[e] > 0]
        toks += [(s[0], s[1], "dma") for s in self.all_dsems if s[1] > 0]
        for e in self.eng:
            for tok in toks:
                if tok[2] == e:
                    continue
                self._wait(e, tok)


class Ring:
    def __init__(self, bufs):
        self.bufs = bufs
        self.i = 0

    def next(self):
        b = self.bufs[self.i % len(self.bufs)]
        self.i += 1
        return b


def bc3(ap, n):
    p, a = ap.shape
    return ap.unsqueeze(2).broadcast_to([p, a, n])


def build(L, depth, debug=False):
    NCH = L // 128
    HALF = min(L, 2048)
    NHALF = L // HALF
    HCH = HALF // 128
    NTB = HALF // 512
    nc = bass.Bass("TRN2", target_bir_lowering=False)

    def din(name, shape, dt=F32):
        return nc.dram_tensor(name, list(shape), dt, kind="ExternalInput").ap()

    skind = "ExternalOutput" if debug else "Internal"

    def dscr(name, shape, dt):
        return nc.dram_tensor(name, list(shape), dt, kind=skind).ap()

    x_in = din("x", [L, D])
    mem_in = din("mem", [256, D])
    w_in = din("w_in", [depth, D, 14400])
    w_kv = din("w_kv", [depth, D, 2048])
    w_bs = din("w_br_ssd", [depth, 2048, D])
    w_bg = din("w_br_gmlp", [depth, D, D])
    w_bx = din("w_br_xattn", [depth, D, D])
    w_o = din("w_out", [depth, D, D])
    p_row = din("p_row", [depth, 1, 9 * 1024])
    p_convw = din("p_convw", [depth, 128, 32 * 5])
    p_convb = din("p_convb", [depth, 128, 32])
    p_wsT = din("p_wsT", [depth, 128, 8 * 128])
    p_bsT = din("p_bsT", [depth, 128, 8])
    c_ident = din("c_ident", [128, 128])
    c_tri0 = din("c_tri0", [128, 128])
    c_tri1 = din("c_tri1", [128, 128])
    c_esel = din("c_esel", [96, 32 * 128])
    c_nm = din("c_nm", [128, 2 * 512])
    out_d = nc.dram_tensor("out", [L, D], F32, kind="ExternalOutput").ap()

    X1 = dscr("s_x1", [L, D], F32)
    PJ = dscr("s_pj", [L, PJ_W], BF16)
    XBC = dscr("s_xbc", [NCH, 128, 32, 128], BF16)
    XQ = dscr("s_xq", [NCH, 128, 8, 128], BF16)
    DTS = dscr("s_dts", [L, 384], F32)
    CS3 = dscr("s_cs3", [L, 192], BF16)
    XS = dscr("s_xs", [L, 2048], BF16)
    BTM = dscr("s_btm", [L, 1024], BF16)
    Y0 = dscr("s_y0", [L, 2048], F32)
    M0 = dscr("s_m0", [L, D], F32)

    with contextlib.ExitStack() as es:
        S = Sched(nc, es)

        uid = [0]

        def sb(st, name, shape, dt):
            uid[0] += 1
            return Buf(st.enter_context(nc.sbuf_tensor(f"{name}_u{uid[0]}", list(shape), dt)), name)

        def ps(st, name, shape, dt):
            uid[0] += 1
            return Buf(st.enter_context(nc.psum_tensor(f"{name}_u{uid[0]}", list(shape), dt)), name)

        ident = sb(es, "ident", [128, 128], BF16)
        tri0 = sb(es, "tri0", [128, 128], F32)
        tri1 = sb(es, "tri1", [128, 128], F32)
        onesf = sb(es, "onesf", [128, 128], F32)
        onesb = sb(es, "onesb", [128, 2], BF16)
        esel = sb(es, "esel", [96, 32 * 128], BF16)
        negm = sb(es, "negm", [128, 2 * 512], BF16)
        prow = sb(es, "prow", [128, 9 * 1024], F32)
        convw = sb(es, "convw", [128, 160], F32)
        convb = sb(es, "convb", [128, 32], F32)
        wsT = sb(es, "wsT", [128, 1024], BF16)
        bsT = sb(es, "bsT", [128, 8], F32)
        a_bc = sb(es, "a_bc", [128, 64], F32)
        G_PRE, G_SSD, G_LNG, G_LNB, G_MEM, G_POST, G_MISC = 0, 1024, 3072, 4096, 5120, 6144, 7168

        with contextlib.ExitStack() as st:
            tmpf = sb(st, "c_tmpf", [128, 32 * 128], F32)
            S.dma(tmpf[:, 0:128], c_ident[:, :], tmpf, writes=[tmpf])
            S.op("dve", lambda e: e.tensor_copy(out=ident[:], in_=tmpf[:, 0:128]), [tmpf], [ident])
            S.dma(tri0[:], c_tri0[:, :], tri0, writes=[tri0])
            S.dma(tri1[:], c_tri1[:, :], tri1, writes=[tri1])
            S.dma(tmpf[0:96, :], c_esel[:, :], tmpf, writes=[tmpf])
            S.op("dve", lambda e: e.tensor_copy(out=esel[:], in_=tmpf[0:96, :]), [tmpf], [esel])
            S.dma(tmpf[:, 0:1024], c_nm[:, :], tmpf, writes=[tmpf])
            S.op("dve", lambda e: e.tensor_copy(out=negm[:], in_=tmpf[:, 0:1024]), [tmpf], [negm])
            S.op("pool", lambda e: e.memset(onesf[:], 1.0), [], [onesf])
            S.op("pool", lambda e: e.memset(onesb[:], 1.0), [], [onesb])
            S.barrier()
            S.release([tmpf, tri0, tri1])

        for l in range(depth):
            Xsrc = x_in if l == 0 else X1
            Xdst = out_d if l == depth - 1 else X1
            if depth == 1:
                Xdst = out_d

            with contextlib.ExitStack() as st:
                tmpw = sb(st, "p_tmpw", [128, 1024], F32)
                S.dma(prow[:], p_row[l, 0:1, :].partition_broadcast(128), prow, writes=[prow])
                S.dma(convw[:], p_convw[l, :, :], convw, writes=[convw])
                S.dma(convb[:], p_convb[l, :, :], convb, writes=[convb])
                S.dma(bsT[:], p_bsT[l, :, :], bsT, writes=[bsT])
                S.dma(tmpw[:], p_wsT[l, :, :], tmpw, writes=[tmpw])
                S.op("dve", lambda e: e.tensor_copy(out=wsT[:], in_=tmpw[:]), [tmpw], [wsT])
                S.op("act", lambda e: e.activation(out=a_bc[:], in_=prow[:, G_MISC + 64:G_MISC + 128], func=AF.Exp),
                     [prow], [a_bc])
                S.op("dve", lambda e: e.tensor_scalar(out=a_bc[:], in0=a_bc[:], scalar1=-1.0, scalar2=None,
                                                       op0=ALU.mult), [a_bc], [a_bc])
                S.barrier()
                S.release([tmpw, prow, convw, convb, bsT])
            dtb_bc = prow[:, G_MISC:G_MISC + 64]
            dsk_bc = prow[:, G_MISC + 128:G_MISC + 160]

            for hf in range(NHALF):
                hs = hf * HALF
                with contextlib.ExitStack() as st:
                    hT = sb(st, "hT", [128, 8, HALF + 4], BF16)
                    with contextlib.ExitStack() as st1:
                        xin_r = Ring([sb(st1, f"xin{i}", [128, 1024], F32) for i in range(9)])
                        jk_r = Ring([sb(st1, f"jk{i}", [128, 1024], BF16) for i in range(4)])
                        junk = sb(st1, "junk", [128, 1024], BF16)
                        hb_r = Ring([sb(st1, f"hb{i}", [128, 1024], BF16) for i in range(5)])
                        ss_r = Ring([sb(st1, f"ss{i}", [128, 4], F32) for i in range(5)])
                        pT_r = Ring([ps(st1, f"pT{i}", [128, 1024], BF16) for i in range(2)])

                        def norm_load(r0, nr):
                            xin = xin_r.next()
                            S.dma(xin[0:nr, :], Xsrc[r0:r0 + nr, :], xin, writes=[xin])
                            return xin

                        def norm_rows(r0, nr, col0, xin=None):
                            hb = hb_r.next(); ss = ss_r.next(); pT = pT_r.next()
                            if xin is None:
                                xin = norm_load(r0, nr)
                            S.op("act", lambda e: e.activation(out=junk[0:nr, :], in_=xin[0:nr, :], func=AF.Square,
                                                                accum_out=ss[0:nr, 0:1]), [xin], [junk, ss])
                            S.op("dve", lambda e: e.tensor_scalar(out=ss[0:nr, 1:2], in0=ss[0:nr, 0:1], scalar1=1.0 / D,
                                                                   scalar2=EPS, op0=ALU.mult, op1=ALU.add), [ss], [ss])
                            S.op("act", lambda e: e.activation(out=ss[0:nr, 2:3], in_=ss[0:nr, 1:2], func=AF.Ln), [ss], [ss])
                            S.op("act", lambda e: e.activation(out=ss[0:nr, 3:4], in_=ss[0:nr, 2:3], func=AF.Exp, scale=-0.5), [ss], [ss])
                            S.op("dve", lambda e: e.scalar_tensor_tensor(out=hb[0:nr, :], in0=xin[0:nr, :], scalar=ss[0:nr, 3:4],
                                                                          in1=prow[0:nr, G_PRE:G_PRE + 1024], op0=ALU.mult,
                                                                          op1=ALU.mult), [xin, ss, prow], [hb])
                            for k in range(8):
                                S.op("pe", lambda e, k=k: e.transpose(out=pT[:, k * 128:k * 128 + nr],
                                                                      in_=hb[0:nr, k * 128:(k + 1) * 128],
                                                                      identity=ident[0:nr, 0:nr]), [hb, ident], [pT])
                            src = pT[:, :].rearrange("p (k t) -> p k t", t=128)[:, :, 0:nr]
                            S.op("act", lambda e: e.activation(out=hT[:, :, col0:col0 + nr], in_=src, func=AF.Copy),
                                 [pT], [hT])

                        if hs - 2 >= 0:
                            norm_rows(hs - 2, 2, 0)
                        else:
                            S.op("pool", lambda e: e.memset(hT[:, :, 0:2], 0.0), [], [hT])
                        if hs + HALF + 2 <= L:
                            norm_rows(hs + HALF, 2, HALF + 2)
                        else:
                            S.op("pool", lambda e: e.memset(hT[:, :, HALF + 2:HALF + 4], 0.0), [], [hT])
                        NB1 = 4
                        loads = {}

                        def ensure_load(c):
                            if c < HCH and c not in loads:
                                loads[c] = norm_load(hs + c * 128, 128)

                        for c in range(min(NB1, HCH)):
                            ensure_load(c)
                        for b0 in range(0, HCH, NB1):
                            batch = list(range(b0, min(b0 + NB1, HCH)))
                            for c in batch:
                                ensure_load(c + NB1)
                            ctx = {c: dict(xin=loads.pop(c), hb=hb_r.next(), ss=ss_r.next(), jk=jk_r.next()) for c in batch}
                            for c in batch:
                                X = ctx[c]
                                S.op("act", lambda e, X=X: e.activation(out=X["jk"][:], in_=X["xin"][:], func=AF.Square,
                                                                        accum_out=X["ss"][:, 0:1]), [X["xin"]], [X["jk"], X["ss"]])
                            for c in batch:
                                X = ctx[c]
                                S.op("dve", lambda e, X=X: e.tensor_scalar(out=X["ss"][:, 1:2], in0=X["ss"][:, 0:1], scalar1=1.0 / D,
                                                                           scalar2=EPS, op0=ALU.mult, op1=ALU.add), [X["ss"]], [X["ss"]])
                            for c in batch:
                                X = ctx[c]
                                S.op("act", lambda e, X=X: e.activation(out=X["ss"][:, 2:3], in_=X["ss"][:, 1:2], func=AF.Ln), [X["ss"]], [X["ss"]])
                            for c in batch:
                                X = ctx[c]
                                S.op("act", lambda e, X=X: e.activation(out=X["ss"][:, 3:4], in_=X["ss"][:, 2:3], func=AF.Exp, scale=-0.5), [X["ss"]], [X["ss"]])
                            for c in batch:
                                X = ctx[c]
                                S.op("dve", lambda e, X=X: e.scalar_tensor_tensor(out=X["hb"][:], in0=X["xin"][:], scalar=X["ss"][:, 3:4],
                                                                                  in1=prow[:, G_PRE:G_PRE + 1024], op0=ALU.mult,
                                                                                  op1=ALU.mult), [X["xin"], X["ss"], prow], [X["hb"]])
                            for c in batch:
                                X = ctx[c]
                                pT = pT_r.next()
                                col0 = 2 + c * 128
                                for k in range(8):
                                    S.op("pe", lambda e, k=k, X=X, pT=pT: e.transpose(out=pT[:, k * 128:(k + 1) * 128],
                                                                                      in_=X["hb"][:, k * 128:(k + 1) * 128],
                                                                                      identity=ident[:]), [X["hb"], ident], [pT])
                                S.op("act", lambda e, pT=pT, col0=col0: e.activation(out=hT[:, :, col0:col0 + 128],
                                                                                     in_=pT[:, :].rearrange("p (k t) -> p k t", t=128),
                                                                                     func=AF.Copy), [pT], [hT])
                        S.barrier()
                        S.release(xin_r.bufs)

                    with contextlib.ExitStack() as st2:
                        wf_r = Ring([sb(st2, f"wf{i}", [128, 8, 512], F32) for i in range(0 if DMA_CAST else 2)])
                        wb_r = Ring([sb(st2, f"wb{i}", [128, 8, 512], BF16) for i in range(3)])
                        pre_r = Ring([sb(st2, f"pre{i}", [128, HALF + 4], BF16) for i in range(2)])
                        acc_r = Ring([sb(st2, f"acc{i}", [128, HALF], F32) for i in range(2)])
                        xc_r = Ring([sb(st2, f"xc{i}", [128, HALF], BF16) for i in range(2)])
                        stg_r = Ring([sb(st2, f"stg{i}", [128, 512], BF16) for i in range(4)])
                        pm_r = Ring([ps(st2, f"pm{i}", [128, 512], F32) for i in range(4)])
                        pd_r = Ring([ps(st2, f"pd{i}", [128, 256], F32) for i in range(2)])
                        dt_r = Ring([sb(st2, f"dtw{i}", [128, 384 + 384], F32) for i in range(5)])
                        c3_r = Ring([sb(st2, f"c3w{i}", [128, 192], BF16) for i in range(5)])

                        def load_w(col0, ncol):
                            wb = wb_r.next()
                            src = w_in[l, :, col0:col0 + ncol].rearrange("(k p) c -> p k c", p=128)
                            if DMA_CAST:
                                S.dma(wb[:, :, 0:ncol], src, wb, writes=[wb], q="pool")
                            else:
                                wf = wf_r.next()
                                S.dma(wf[:, :, 0:ncol], src, wf, writes=[wf])
                                S.op("act", lambda e: e.activation(out=wb[:, :, 0:ncol], in_=wf[:, :, 0:ncol], func=AF.Copy), [wf], [wb])
                            return wb

                        segs = [(C_Z, PJ_Z, 2048, AF.Silu), (C_GG, PJ_GG, 1024, AF.Silu),
                                (C_GUV, PJ_GUV, 2048, AF.Gelu_apprx_tanh), (C_XG, PJ_XG, 1024, AF.Silu),
                                (C_MG, PJ_MG, 3072, AF.Sigmoid)]
                        wlist = ([(C_XBC + bi * 512, 512) for bi in range(8)] + [(C_XQ + bi * 512, 512) for bi in range(2)]
                                 + [(wc + b0, 512) for (wc, pc, width, fn) in segs for b0 in range(0, width, 512)] + [(C_DT, 64)])
                        wq = []
                        widx = [0]

                        def issue_w():
                            if widx[0] < len(wlist):
                                wq.append(load_w(*wlist[widx[0]]))
                                widx[0] += 1

                        def next_w():
                            if not wq:
                                issue_w()
                            wb_ = wq.pop(0)
                            issue_w()
                            return wb_

                        pending = []

                        def flush_pending():
                            while pending:
                                pending.pop(0)()

                        def store_tile(dst, xc, m):
                            for q in range(0, HCH, 8):
                                nq = min(8, HCH - q)
                                o = dst[hs // 128 + q:hs // 128 + q + nq, :, m, :].rearrange("c p t -> p c t")
                                i_ = xc[:, q * 128:(q + nq) * 128].rearrange("p (c t) -> p c t", t=128)
                                S.dma(o, i_, xc, reads=[xc])

                        def feat_block(col0, kind, mbase):
                            wb = next_w()
                            if kind == "xq":
                                flush_pending()
                            for mi in range(4):
                                m = mbase + mi
                                if kind == "xbc":
                                    pre = pre_r.next()
                                    blocks = [(i * 512, 512) for i in range(NTB)] + [(HALF, 4)]
                                else:
                                    pre = xc_r.next()
                                    blocks = [(i * 512, 512) for i in range(NTB)]
                                for (c0, n) in blocks:
                                    pm = pm_r.next()
                                    hc0 = c0 if kind == "xbc" else c0 + 2
                                    for k in range(8):
                                        S.op("pe", lambda e, k=k: e.matmul(pm[:, 0:n], lhsT=wb[:, k, mi * 128:(mi + 1) * 128],
                                                                           rhs=hT[:, k, hc0:hc0 + n], start=(k == 0), stop=(k == 7)),
                                             [wb, hT], [pm])
                                    S.op("act", lambda e: e.activation(out=pre[:, c0:c0 + n], in_=pm[:, 0:n], func=AF.Copy),
                                         [pm], [pre])
                                if kind == "xbc":
                                    acc = acc_r.next(); xc = xc_r.next()
                                    S.op("act", lambda e: e.activation(out=acc[:], in_=pre[:, 0:HALF], func=AF.Copy,
                                                                        scale=convw[:, m * 5:m * 5 + 1]), [pre, convw], [acc])
                                    flush_pending()
                                    for k in range(1, 5):
                                        S.op("dve", lambda e, k=k: e.scalar_tensor_tensor(
                                            out=acc[:], in0=pre[:, k:k + HALF], scalar=convw[:, m * 5 + k:m * 5 + k + 1],
                                            in1=acc[:], op0=ALU.mult, op1=ALU.add), [pre, convw, acc], [acc])

                                    def fin(acc=acc, xc=xc, m=m):
                                        S.op("act", lambda e: e.activation(out=xc[:], in_=acc[:], func=AF.Silu,
                                                                            bias=convb[:, m:m + 1]), [acc, convb], [xc])
                                        store_tile(XBC, xc, m)
                                    pending.append(fin)
                                else:
                                    store_tile(XQ, pre, m)

                        for bi in range(8):
                            feat_block(C_XBC + bi * 512, "xbc", bi * 4)
                        for bi in range(2):
                            feat_block(C_XQ + bi * 512, "xq", bi * 4)
                        flush_pending()

                        for (wc, pc, width, fn) in segs:
                            for b0 in range(0, width, 512):
                                wb = next_w()
                                for c in range(HCH):
                                    pm = pm_r.next(); stg = stg_r.next()
                                    for k in range(8):
                                        S.op("pe", lambda e, k=k: e.matmul(pm[:, :], lhsT=hT[:, k, 2 + c * 128:2 + (c + 1) * 128],
                                                                           rhs=wb[:, k, :], start=(k == 0), stop=(k == 7)),
                                             [wb, hT], [pm])
                                    S.op("act", lambda e: e.activation(out=stg[:], in_=pm[:, :], func=fn), [pm], [stg])
                                    r0 = hs + c * 128
                                    S.dma(PJ[r0:r0 + 128, pc + b0:pc + b0 + 512], stg[:], stg, reads=[stg])

                        wb = next_w()
                        pdq = Ring(pm_r.bufs + pd_r.bufs)

                        def dt_chunk(c):
                            pd = pdq.next(); dw = dt_r.next(); c3 = c3_r.next()
                            r0 = hs + c * 128
                            for k in range(8):
                                S.op("pe", lambda e, k=k: e.matmul(pd[:, 0:64], lhsT=hT[:, k, 2 + c * 128:2 + (c + 1) * 128],
                                                                   rhs=wb[:, k, 0:64], start=(k == 0), stop=(k == 7)), [wb, hT], [pd])
                            V_, A_, E_, R_ = 384, 448, 512, 576
                            S.op("dve", lambda e: e.tensor_tensor(out=dw[:, V_:V_ + 64], in0=pd[:, 0:64], in1=dtb_bc, op=ALU.add),
                                 [pd, prow], [dw])
                            yield
                            S.op("act", lambda e: e.activation(out=dw[:, A_:A_ + 64], in_=dw[:, V_:V_ + 64], func=AF.Abs), [dw], [dw])
                            yield
                            S.op("act", lambda e: e.activation(out=dw[:, E_:E_ + 64], in_=dw[:, A_:A_ + 64], func=AF.Exp, scale=-1.0),
                                 [dw], [dw])
                            yield
                            S.op("act", lambda e: e.activation(out=dw[:, E_:E_ + 64], in_=dw[:, E_:E_ + 64], func=AF.Ln, bias=1.0),
                                 [dw], [dw])
                            yield
                            S.op("dve", lambda e: e.tensor_scalar(out=dw[:, R_:R_ + 64], in0=dw[:, V_:V_ + 64], scalar1=0.0,
                                                                   scalar2=None, op0=ALU.max), [dw], [dw])
                            yield
                            S.op("dve", lambda e: e.tensor_tensor(out=dw[:, 0:64], in0=dw[:, R_:R_ + 64], in1=dw[:, E_:E_ + 64],
                                                                   op=ALU.add), [dw], [dw])
                            yield
                            S.op("dve", lambda e: e.tensor_tensor(out=dw[:, V_:V_ + 64], in0=dw[:, 0:64], in1=a_bc[:], op=ALU.mult),
                                 [dw, a_bc], [dw])
                            yield
                            S.op("pe", lambda e: e.matmul(pd[:, 64:96], lhsT=tri0[:], rhs=dw[:, V_:V_ + 32], start=True, stop=True),
                                 [tri0, dw], [pd])
                            yield
                            S.op("pe", lambda e: e.matmul(pd[:, 96:128], lhsT=tri1[:], rhs=dw[:, V_ + 32:V_ + 64], start=True, stop=True),
                                 [tri1, dw], [pd])
                            yield
                            S.op("pe", lambda e: e.matmul(pd[:, 128:192], lhsT=onesf[:], rhs=dw[:, V_:V_ + 64], start=True, stop=True),
                                 [onesf, dw], [pd])
                            yield
                            T_ = 640
                            S.op("dve", lambda e: e.tensor_copy(out=dw[:, 320:384], in_=pd[:, 64:128]), [pd], [dw])
                            yield
                            S.op("dve", lambda e: e.tensor_copy(out=dw[:, T_:T_ + 64], in_=pd[:, 128:192]), [pd], [dw])
                            yield
                            L_ = 704
                            S.op("act", lambda e: e.activation(out=dw[:, L_:L_ + 64], in_=dw[:, 0:64], func=AF.Ln), [dw], [dw])
                            yield
                            S.op("dve", lambda e: e.tensor_tensor(out=dw[:, 64:128], in0=dw[:, L_:L_ + 64], in1=dw[:, 320:384],
                                                                   op=ALU.subtract), [dw], [dw])
                            yield
                            S.op("act", lambda e: e.activation(out=dw[:, 128:192], in_=dw[:, 320:384], func=AF.Exp), [dw], [dw])
                            yield
                            S.op("act", lambda e: e.activation(out=dw[:, 256:320], in_=dw[:, T_:T_ + 64], func=AF.Exp), [dw], [dw])
                            yield
                            S.op("dve", lambda e: e.tensor_tensor(out=dw[:, A_:A_ + 64], in0=dw[:, T_:T_ + 64], in1=dw[:, 320:384],
                                                                   op=ALU.subtract), [dw], [dw])
                            yield
                            S.op("act", lambda e: e.activation(out=dw[:, A_:A_ + 64], in_=dw[:, A_:A_ + 64], func=AF.Exp), [dw], [dw])
                            yield
                            S.op("dve", lambda e: e.tensor_tensor(out=dw[:, 192:256], in0=dw[:, A_:A_ + 64], in1=dw[:, 0:64],
                                                                   op=ALU.mult), [dw], [dw])
                            yield
                            c3v = c3[:, :].rearrange("p (d j h) -> p d j h", d=2, j=3)
                            csv = dw[:, 320:384].rearrange("p (d h) -> p d h", d=2)
                            r1 = dw[:, E_:E_ + 64].rearrange("p (d h) -> p d h", d=2)
                            r2 = dw[:, R_:R_ + 64].rearrange("p (d h) -> p d h", d=2)
                            S.op("dve", lambda e: e.tensor_copy(out=c3v[:, :, 0, :], in_=csv), [dw], [c3])
                            yield
                            S.op("dve", lambda e: e.tensor_tensor(out=r1, in0=csv, in1=c3v[:, :, 0, :], op=ALU.subtract), [dw, c3], [dw])
                            yield
                            S.op("dve", lambda e: e.tensor_copy(out=c3v[:, :, 1, :], in_=r1), [dw], [c3])
                            yield
                            S.op("dve", lambda e: e.tensor_tensor(out=r2, in0=r1, in1=c3v[:, :, 1, :], op=ALU.subtract), [dw, c3], [dw])
                            yield
                            S.op("dve", lambda e: e.tensor_copy(out=c3v[:, :, 2, :], in_=r2), [dw], [c3])
                            yield
                            S.dma(DTS[r0:r0 + 128, :], dw[:, 0:384], dw, reads=[dw])
                            yield
                            S.dma(CS3[r0:r0 + 128, :], c3[:], c3, reads=[c3])
                            yield

                        NBD = 4
                        for b0 in range(0, HCH, NBD):
                            gens = [dt_chunk(c) for c in range(b0, min(b0 + NBD, HCH))]
                            while gens:
                                for g_ in list(gens):
                                    try:
                                        next(g_)
                                    except StopIteration:
                                        gens.remove(g_)
                        S.barrier()
                        S.release(wf_r.bufs + wb_r.bufs + xc_r.bufs + stg_r.bufs + dt_r.bufs + c3_r.bufs)

            for d in range(2):
                with contextlib.ExitStack() as st:
                    if d == 1:
                        wbs = sb(st, "wbs", [128, 16, 1024], BF16)
                        with contextlib.ExitStack() as stw:
                            wtmp_r = Ring([sb(stw, f"wtmp{i}", [128, 4, 1024], F32) for i in range(2)])
                            for q in range(4):
                                wt = wtmp_r.next()
                                S.dma(wt[:], w_bs[l, q * 512:(q + 1) * 512, :].rearrange("(k p) c -> p k c", p=128), wt, writes=[wt])
                                S.op("act", lambda e: e.activation(out=wbs[:, q * 4:(q + 1) * 4, :], in_=wt[:], func=AF.Copy), [wt], [wbs])
                            S.barrier()
                            S.release(wtmp_r.bufs)
                    xbw_r = Ring([sb(st, f"xbw{i}", [128, 32, 128], BF16) for i in range(2)])
                    dts_r = Ring([sb(st, f"dts{i}", [128, 384], F32) for i in range(2)])
                    cs3_r = Ring([sb(st, f"cs3{i}", [128, 192], BF16) for i in range(2)])
                    xs_r = Ring([sb(st, f"xs{i}", [128, 2048], BF16) for i in range(2)])
                    bt_r = Ring([sb(st, f"bt{i}", [128, 1024], BF16) for i in range(2)])
                    csT_r = Ring([sb(st, f"csT{i}", [96, 128], BF16) for i in range(2)])
                    scs_r = Ring([sb(st, f"scs{i}", [128, 1024], BF16) for i in range(2)])
                    xde_r = Ring([sb(st, f"xde{i}", [128, 2048], BF16) for i in range(2)])
                    dec_r = Ring([sb(st, f"dec{i}", [128, 4, 128], BF16) for i in range(3)])
                    M_r = Ring([sb(st, f"M{i}", [128, 4, 128], BF16) for i in range(3)])
                    tmp_r = Ring([sb(st, f"tmp{i}", [128, 256], F32) for i in range(3)])
                    yacc_r = Ring([sb(st, f"yacc{i}", [128, 2048], F32) for i in range(2 + d)])
                    for yb in yacc_r.bufs:
                        yb.views = [yb] + [Buf(yb.t, yb.name + f"_v{i}") for i in range(1, 8)]
                    Hf = sb(st, "Hf", [128, 2048], F32)
                    Hb = sb(st, "Hb", [128, 2048], BF16)
                    Hfv = [Hf] + [Buf(Hf.t, f"Hf_v{i}") for i in range(1, 8)]
                    Hbv = [Hb] + [Buf(Hb.t, f"Hb_v{i}") for i in range(1, 8)]
                    p_sc = ps(st, "p_sc", [128, 1024], F32)
                    p_cb_r = Ring([ps(st, f"p_cb{i}", [128, 512], F32) for i in range(2)])
                    pbk = [ps(st, f"pbk{i}", [128, 512], F32) for i in range(3)]
                    p_T = ps(st, "p_T", [128, 1024], BF16)
                    if d == 1:
                        zs_r = Ring([sb(st, f"zs{i}", [128, 2048], BF16) for i in range(3)])
                        g0_r = Ring([sb(st, f"g0{i}", [128, 1024], BF16) for i in range(3)])
                        ynb = sb(st, "ynb", [128, 2048], BF16)
                        ynT = sb(st, "ynT", [128, 16, 128], BF16)
                        gs = sb(st, "gs", [128, 32], F32)
                        m0_r = Ring([sb(st, f"m0{i}", [128, 1024], F32) for i in range(2)])
                    S.op("pool", lambda e: e.memset(Hf[:], 0.0), [], Hfv)
                    S.op("pool", lambda e: e.memset(Hb[:], 0.0), [], Hbv)

                    order = list(range(NCH)) if d == 0 else list(range(NCH - 1, -1, -1))
                    def sweep_loads(c):
                        r0 = c * 128
                        Bn = dict(xbw=xbw_r.next(), dts=dts_r.next(), cs3=cs3_r.next(), xs=xs_r.next(), bt=bt_r.next(), yacc=yacc_r.next())
                        S.dma(Bn["dts"][:], DTS[r0:r0 + 128, :], Bn["dts"], writes=[Bn["dts"]])
                        S.dma(Bn["cs3"][:], CS3[r0:r0 + 128, :], Bn["cs3"], writes=[Bn["cs3"]])
                        if d == 0:
                            S.dma(Bn["xbw"][:], XBC[c, :, :, :], Bn["xbw"], writes=[Bn["xbw"]])
                        else:
                            Bn["zs"] = zs_r.next(); Bn["g0"] = g0_r.next()
                            S.dma(Bn["xbw"][:, 16:32, :], XBC[c, :, 16:32, :], Bn["xbw"], writes=[Bn["xbw"]])
                            S.dma(Bn["xs"][:], XS[r0:r0 + 128, :], Bn["xs"], writes=[Bn["xs"]])
                            S.dma(Bn["bt"][:], BTM[r0:r0 + 128, :], Bn["bt"], writes=[Bn["bt"]])
                            S.dma(Bn["yacc"][:], Y0[r0:r0 + 128, :], Bn["yacc"], writes=Bn["yacc"].views)
                            S.dma(Bn["zs"][:], PJ[r0:r0 + 128, PJ_Z:PJ_Z + 2048], Bn["zs"], writes=[Bn["zs"]])
                            S.dma(Bn["g0"][:], PJ[r0:r0 + 128, PJ_MG:PJ_MG + 1024], Bn["g0"], writes=[Bn["g0"]])
                        return Bn

                    for db in dec_r.bufs:
                        db.views = [db] + [Buf(db.t, db.name + f"_v{i}") for i in range(1, 4)]

                    def prep_a(c, Bn):
                        r0 = c * 128
                        xbw = Bn["xbw"]; dts = Bn["dts"]; cs3 = Bn["cs3"]; xs = Bn["xs"]; bt = Bn["bt"]; yacc = Bn["yacc"]
                        Bn["csT"] = csT_r.next(); Bn["xde"] = xde_r.next()
                        csT = Bn["csT"]; xde = Bn["xde"]
                        if d == 0:
                            for half in range(3):
                                for j in range(8):
                                    S.op("pe", lambda e, j=j: e.transpose(out=p_T[:, j * 128:(j + 1) * 128],
                                                                          in_=xbw[:, half * 8 + j, :], identity=ident[:]),
                                         [xbw, ident], [p_T])
                                dstb = xs[:, half * 1024:(half + 1) * 1024] if half < 2 else bt[:]
                                dbuf = xs if half < 2 else bt
                                S.op("act", lambda e: e.activation(out=dstb, in_=p_T[:, :], func=AF.Copy), [p_T], [dbuf])
                            S.dma(XS[r0:r0 + 128, :], xs[:], xs, reads=[xs])
                            S.dma(BTM[r0:r0 + 128, :], bt[:], bt, reads=[bt])
                        S.op("pe", lambda e: e.transpose(out=p_T[0:96, 0:128], in_=cs3[:, d * 96:(d + 1) * 96], identity=ident[:]),
                             [cs3, ident], [p_T])
                        S.op("act", lambda e: e.activation(out=csT[:], in_=p_T[0:96, 0:128], func=AF.Copy), [p_T], [csT])
                        dtd = dts[:, d * 32:(d + 1) * 32]
                        wend = dts[:, 192 + d * 32:192 + (d + 1) * 32]
                        xs3 = xs[:, :].rearrange("p (h q) -> p h q", q=64)
                        S.op("dve", lambda e: e.tensor_tensor(out=xde[:, :].rearrange("p (h q) -> p h q", q=64), in0=xs3,
                                                               in1=bc3(wend, 64), op=ALU.mult), [xs, dts], [xde])
                        if d == 0:
                            S.op("dve", lambda e: e.tensor_tensor(out=yacc[:, :].rearrange("p (h q) -> p h q", q=64), in0=xs3,
                                                                   in1=bc3(dsk_bc, 64), op=ALU.mult), [xs, prow], yacc.views)

                    def prep_b(c, Bn):
                        xbw = Bn["xbw"]
                        Bn["scs"] = scs_r.next()
                        scs_ = Bn["scs"]
                        for g in range(8):
                            S.op("pe", lambda e, g=g: e.matmul(p_sc[:, g * 128:(g + 1) * 128], lhsT=xbw[:, 16 + g, :],
                                                               rhs=xbw[:, 24 + g, :], start=True, stop=True), [xbw], [p_sc])
                        S.op("act", lambda e: e.activation(out=scs_[:], in_=p_sc[:, :], func=AF.Copy), [p_sc], [scs_])

                    pending_post = []
                    pend = [sweep_loads(order[0])]
                    prep_a(order[0], pend[0])
                    prep_b(order[0], pend[0])
                    for ci, c in enumerate(order):
                        r0 = c * 128
                        Bn = pend.pop(0)
                        xbw = Bn["xbw"]; dts = Bn["dts"]; cs3 = Bn["cs3"]; xs = Bn["xs"]; bt = Bn["bt"]; yacc = Bn["yacc"]
                        csT = Bn["csT"]; xde = Bn["xde"]; scs = Bn["scs"]
                        if d == 1:
                            zs = Bn["zs"]; g0 = Bn["g0"]
                        GB = [None] * 8

                        def stage_a(g):
                            G_ = dict(p_cb=p_cb_r.next(), dec=dec_r.next(), M=M_r.next(), tmp=tmp_r.next())
                            GB[g] = G_
                            p_cb = G_["p_cb"]; dec = G_["dec"]; tmp = G_["tmp"]
                            bkA = pbk[g % 2]
                            p_yo = bkA.t[:, 0:256]; p_st = bkA.t[:, 256:512]
                            gc = slice(g * 256, (g + 1) * 256)
                            S.op("pe", lambda e: e.matmul(p_yo[:, :], lhsT=xbw[:, 24 + g, :], rhs=Hb[:, gc], start=True, stop=True),
                                 [xbw, Hbv[g]], [bkA])
                            S.op("pe", lambda e: e.matmul(p_st[:, :], lhsT=bt[:, g * 128:(g + 1) * 128], rhs=xde[:, gc], start=True, stop=True),
                                 [bt, xde], [bkA])
                            S.op("pe", lambda e: e.matmul(p_cb[:, :], lhsT=ident[:], rhs=negm[:, d * 512:(d + 1) * 512], start=True, stop=False),
                                 [ident, negm], [p_cb])
                            for r in range(4):
                                h = g * 4 + r
                                S.op("pe", lambda e, r=r, h=h: e.matmul(p_cb[:, r * 128:(r + 1) * 128], lhsT=esel[:, h * 128:(h + 1) * 128],
                                                                        rhs=csT[:], start=False, stop=(r == 3)), [esel, csT], [p_cb])
                            for r in range(4):
                                h = g * 4 + r
                                S.op("act", lambda e, r=r, h=h: e.activation(out=dec[:, r, :], in_=p_cb[:, r * 128:(r + 1) * 128],
                                                                             func=AF.Exp, bias=dts[:, 64 + d * 32 + h:64 + d * 32 + h + 1]),
                                     [p_cb, dts], [dec.views[r]])
                            cdec = dts[:, 256 + d * 32 + g * 4:256 + d * 32 + g * 4 + 4]
                            S.op("dve", lambda e: e.tensor_tensor(out=Hf[:, gc].rearrange("p (r q) -> p r q", q=64),
                                                                   in0=Hf[:, gc].rearrange("p (r q) -> p r q", q=64),
                                                                   in1=bc3(cdec, 64), op=ALU.mult), [Hfv[g], dts], [Hfv[g]])
                            ecs = dts[:, 128 + d * 32 + g * 4:128 + d * 32 + g * 4 + 4]
                            S.op("dve", lambda e: e.tensor_tensor(out=tmp[:, :].rearrange("p (r q) -> p r q", q=64),
                                                                   in0=p_yo[:, :].rearrange("p (r q) -> p r q", q=64),
                                                                   in1=bc3(ecs, 64), op=ALU.mult), [bkA, dts], [tmp])
                            S.op("dve", lambda e: e.tensor_tensor(out=Hf[:, gc], in0=p_st[:, :], in1=Hf[:, gc], op=ALU.add),
                                 [bkA, Hfv[g]], [Hfv[g]])
                            S.op("act", lambda e: e.activation(out=Hb[:, gc], in_=Hf[:, gc], func=AF.Copy), [Hfv[g]], [Hbv[g]])
                            S.op("dve", lambda e: e.tensor_tensor(out=yacc[:, gc], in0=yacc[:, gc], in1=tmp[:], op=ALU.add),
                                 [yacc.views[g], tmp], [yacc.views[g]])

                        def stage_b(g):
                            G_ = GB[g]
                            dec = G_["dec"]; M = G_["M"]
                            bky = pbk[2]
                            p_y = bky.t[:, (g % 2) * 256:(g % 2 + 1) * 256]
                            scg = scs[:, g * 128:(g + 1) * 128].unsqueeze(1).broadcast_to([128, 4, 128])
                            S.op("dve", lambda e: e.tensor_tensor(out=M[:], in0=dec[:], in1=scg, op=ALU.mult), dec.views + [scs], [M])
                            for r in range(4):
                                h = g * 4 + r
                                S.op("pe", lambda e, r=r, h=h: e.matmul(p_y[:, r * 64:(r + 1) * 64], lhsT=M[:, r, :],
                                                                        rhs=xs[:, h * 64:(h + 1) * 64], start=True, stop=True), [M, xs], [bky])

                        def stage_c(g):
                            bky = pbk[2]
                            p_y = bky.t[:, (g % 2) * 256:(g % 2 + 1) * 256]
                            gc = slice(g * 256, (g + 1) * 256)
                            S.op("dve", lambda e: e.tensor_tensor(out=yacc[:, gc], in0=p_y[:, :], in1=yacc[:, gc], op=ALU.add),
                                 [bky, yacc.views[g]], [yacc.views[g]])

                        for step in range(10):
                            if step < 8:
                                stage_a(step)
                            if 2 <= step <= 9:
                                stage_c(step - 2)
                            if 1 <= step <= 8:
                                stage_b(step - 1)
                            if step == 1 and ci + 1 < len(order):
                                pend.append(sweep_loads(order[ci + 1]))
                            if step >= 1:
                                for pg in list(pending_post):
                                    try:
                                        next(pg)
                                    except StopIteration:
                                        pending_post.remove(pg)
                            if step == 5 and ci + 1 < len(order):
                                prep_a(order[ci + 1], pend[0])
                        if d == 0:
                            S.dma(Y0[r0:r0 + 128, :], yacc[:], yacc, reads=yacc.views)
                        else:
                            def post(r0=r0, yacc=yacc, zs=zs, g0=g0):
                                m0 = m0_r.next()
                                yz = yacc
                                S.op("dve", lambda e: e.tensor_tensor(out=yz[:], in0=yacc[:], in1=zs[:], op=ALU.mult), yacc.views + [zs], yacc.views)
                                yield
                                for g in range(8):
                                    S.op("act", lambda e, g=g: e.activation(out=ynb[:, g * 256:(g + 1) * 256], in_=yz[:, g * 256:(g + 1) * 256],
                                                                            func=AF.Square, accum_out=gs[:, g:g + 1]), [yz], [ynb, gs])
                                yield
                                S.op("dve", lambda e: e.tensor_scalar(out=gs[:, 8:16], in0=gs[:, 0:8], scalar1=1.0 / 256, scalar2=EPS,
                                                                       op0=ALU.mult, op1=ALU.add), [gs], [gs])
                                S.op("act", lambda e: e.activation(out=gs[:, 16:24], in_=gs[:, 8:16], func=AF.Ln), [gs], [gs])
                                S.op("act", lambda e: e.activation(out=gs[:, 24:32], in_=gs[:, 16:24], func=AF.Exp, scale=-0.5), [gs], [gs])
                                yield
                                for g in range(8):
                                    S.op("dve", lambda e, g=g: e.scalar_tensor_tensor(
                                        out=ynb[:, g * 256:(g + 1) * 256], in0=yz[:, g * 256:(g + 1) * 256], scalar=gs[:, 24 + g:25 + g],
                                        in1=prow[:, G_SSD + g * 256:G_SSD + (g + 1) * 256], op0=ALU.mult, op1=ALU.mult), [yz, gs, prow], [ynb])
                                for half in range(2):
                                    yield
                                    for j in range(8):
                                        S.op("pe", lambda e, j=j: e.transpose(out=p_T[:, j * 128:(j + 1) * 128],
                                                                              in_=ynb[:, (half * 8 + j) * 128:(half * 8 + j + 1) * 128],
                                                                              identity=ident[:]), [ynb, ident], [p_T])
                                    S.op("act", lambda e: e.activation(out=ynT[:, half * 8:(half + 1) * 8, :],
                                                                        in_=p_T[:, :].rearrange("p (k t) -> p k t", t=128), func=AF.Copy),
                                         [p_T], [ynT])
                                for nb in range(2):
                                    yield
                                    for k in range(16):
                                        S.op("pe", lambda e, k=k: e.matmul(p_sc[:, nb * 512:(nb + 1) * 512], lhsT=ynT[:, k, :],
                                                                           rhs=wbs[:, k, nb * 512:(nb + 1) * 512], start=(k == 0), stop=(k == 15)),
                                             [ynT, wbs], [p_sc])
                                S.op("dve", lambda e: e.tensor_tensor(out=m0[:], in0=p_sc[:, :], in1=g0[:], op=ALU.mult), [p_sc, g0], [m0])
                                S.dma(M0[r0:r0 + 128, :], m0[:], m0, reads=[m0])
                            pending_post.append(post())
                        if ci + 1 < len(order):
                            for pg in list(pending_post):
                                if pg is not pending_post[-1] or d == 0:
                                    for _ in pg:
                                        pass
                                    pending_post.remove(pg)
                            prep_b(order[ci + 1], pend[0])
                    for pg in pending_post:
                        for _ in pg:
                            pass
                    S.barrier()
                    rel = xbw_r.bufs + dts_r.bufs + cs3_r.bufs + xs_r.bufs + bt_r.bufs + yacc_r.bufs
                    if d == 1:
                        rel += zs_r.bufs + g0_r.bufs + m0_r.bufs
                    S.release(rel)

            with contextlib.ExitStack() as st:
                wbg = sb(st, "wbg", [128, 8, 1024], BF16)
                wbx = sb(st, "wbx", [128, 8, 1024], BF16)
                wo = sb(st, "wo", [128, 8, 1024], BF16)
                kT = sb(st, "kT", [128, 8, 256], BF16)
                Vv = sb(st, "Vv", [128, 2, 1024], BF16)
                pA = ps(st, "pA", [128, 1024], F32)
                pB = ps(st, "pB", [128, 1024], F32)
                pC = ps(st, "pC", [128, 1024], F32)
                p_T = ps(st, "p5_T", [128, 1024], BF16)
                p_den = ps(st, "p_den", [128, 8], F32)
                with contextlib.ExitStack() as stw:
                    wtmp_r = Ring([sb(stw, f"w5tmp{i}", [128, 4, 1024], F32) for i in range(2)])
                    for (wsrc, wdst) in ((w_bg, wbg), (w_bx, wbx), (w_o, wo)):
                        for q in range(2):
                            wt = wtmp_r.next()
                            S.dma(wt[:], wsrc[l, q * 512:(q + 1) * 512, :].rearrange("(k p) c -> p k c", p=128), wt, writes=[wt])
                            S.op("act", lambda e, q=q, wdst=wdst, wt=wt: e.activation(out=wdst[:, q * 4:(q + 1) * 4, :], in_=wt[:], func=AF.Copy),
                                 [wt], [wdst])
                    memT = sb(stw, "memT", [128, 8, 256], BF16)
                    mx = sb(stw, "mx", [128, 1024], F32)
                    mjunk = sb(stw, "mjunk", [128, 1024], BF16)
                    mh = sb(stw, "mh", [128, 1024], BF16)
                    mss = sb(stw, "mss", [128, 4], F32)
                    wkb = sb(stw, "wkb", [128, 8, 2048], BF16)
                    for q in range(4):
                        wt = wtmp_r.next()
                        S.dma(wt[:, :, 0:512].rearrange("p k c -> p k c"),
                              w_kv[l, :, q * 512:(q + 1) * 512].rearrange("(k p) c -> p k c", p=128)[:, 0:4, :], wt, writes=[wt])
                        S.dma(wt[:, :, 512:1024],
                              w_kv[l, :, q * 512:(q + 1) * 512].rearrange("(k p) c -> p k c", p=128)[:, 4:8, :], wt, writes=[wt])
                        S.op("act", lambda e, q=q, wt=wt: e.activation(out=wkb[:, 0:4, q * 512:(q + 1) * 512], in_=wt[:, :, 0:512], func=AF.Copy),
                             [wt], [wkb])
                        S.op("dve", lambda e, q=q, wt=wt: e.tensor_copy(out=wkb[:, 4:8, q * 512:(q + 1) * 512], in_=wt[:, :, 512:1024]),
                             [wt], [wkb])
                    for mt in range(2):
                        S.dma(mx[:], mem_in[mt * 128:(mt + 1) * 128, :], mx, writes=[mx])
                        S.op("act", lambda e: e.activation(out=mjunk[:], in_=mx[:], func=AF.Square, accum_out=mss[:, 0:1]), [mx], [mjunk, mss])
                        S.op("dve", lambda e: e.tensor_scalar(out=mss[:, 1:2], in0=mss[:, 0:1], scalar1=1.0 / D, scalar2=EPS,
                                                               op0=ALU.mult, op1=ALU.add), [mss], [mss])
                        S.op("act", lambda e: e.activation(out=mss[:, 2:3], in_=mss[:, 1:2], func=AF.Ln), [mss], [mss])
                        S.op("act", lambda e: e.activation(out=mss[:, 3:4], in_=mss[:, 2:3], func=AF.Exp, scale=-0.5), [mss], [mss])
                        S.op("dve", lambda e: e.scalar_tensor_tensor(out=mh[:], in0=mx[:], scalar=mss[:, 3:4],
                                                                      in1=prow[:, G_MEM:G_MEM + 1024], op0=ALU.mult, op1=ALU.mult),
                             [mx, mss, prow], [mh])
                        for k in range(8):
                            S.op("pe", lambda e, k=k: e.transpose(out=p_T[:, k * 128:(k + 1) * 128], in_=mh[:, k * 128:(k + 1) * 128],
                                                                  identity=ident[:]), [mh, ident], [p_T])
                        S.op("act", lambda e, mt=mt: e.activation(out=memT[:, :, mt * 128:(mt + 1) * 128],
                                                                  in_=p_T[:, :].rearrange("p (k t) -> p k t", t=128), func=AF.Copy),
                             [p_T], [memT])
                    for j in range(8):
                        for k in range(8):
                            S.op("pe", lambda e, k=k, j=j: e.matmul(pA[:, 0:256], lhsT=wkb[:, k, j * 128:(j + 1) * 128], rhs=memT[:, k, :],
                                                                    start=(k == 0), stop=(k == 7)), [wkb, memT], [pA])
                        S.op("act", lambda e, j=j: e.activation(out=kT[:, j, :], in_=pA[:, 0:256], func=AF.Copy), [pA], [kT])
                    for mt in range(2):
                        for nb in range(2):
                            for k in range(8):
                                S.op("pe", lambda e, k=k, mt=mt, nb=nb: e.matmul(pB[:, 0:512], lhsT=memT[:, k, mt * 128:(mt + 1) * 128],
                                                                                 rhs=wkb[:, k, 1024 + nb * 512:1024 + (nb + 1) * 512],
                                                                                 start=(k == 0), stop=(k == 7)), [wkb, memT], [pB])
                            S.op("act", lambda e, mt=mt, nb=nb: e.activation(out=Vv[:, mt, nb * 512:(nb + 1) * 512], in_=pB[:, 0:512],
                                                                             func=AF.Copy), [pB], [Vv])
                    S.barrier()
                    S.release(wtmp_r.bufs + [mx])

                pj_r = Ring([sb(st, f"pj{i}", [128, PJ_W - 2048], BF16) for i in range(2)])
                xq_r = Ring([sb(st, f"xq{i}", [128, 8, 128], BF16) for i in range(2)])
                m0_r = Ring([sb(st, f"m05{i}", [128, 1024], F32) for i in range(2)])
                xi_r = Ring([sb(st, f"xi5{i}", [128, 1024], F32) for i in range(2)])
                xo_r = Ring([sb(st, f"xo5{i}", [128, 1024], F32) for i in range(2)])
                bst = sb(st, "bst", [128, 16], F32)
                vt = sb(st, "vt", [128, 1024], F32)
                vn = sb(st, "vn", [128, 1024], BF16)
                svt = sb(st, "svt", [128, 1024], F32)
                yg = sb(st, "yg", [128, 1024], BF16)
                tT = sb(st, "tT", [128, 8, 128], BF16)
                Eb = sb(st, "Eb", [128, 8, 128], BF16)
                rden = sb(st, "rden", [128, 8], F32)
                ot = sb(st, "ot", [128, 1024], F32)
                yx = sb(st, "yx", [128, 1024], BF16)
                macc = sb(st, "macc", [128, 1024], F32)
                mb = sb(st, "mb", [128, 1024], BF16)
                pjunk = sb(st, "pjunk", [128, 1024], BF16)
                pss = sb(st, "pss", [128, 4], F32)

                def transpose8(src, srcbuf):
                    for k in range(8):
                        S.op("pe", lambda e, k=k: e.transpose(out=p_T[:, k * 128:(k + 1) * 128], in_=src[:, k * 128:(k + 1) * 128],
                                                              identity=ident[:]), [srcbuf, ident], [p_T])
                    S.op("act", lambda e: e.activation(out=tT[:], in_=p_T[:, :].rearrange("p (k t) -> p k t", t=128), func=AF.Copy),
                         [p_T], [tT])

                def proj(pdst, wsb):
                    for nb in range(2):
                        for k in range(8):
                            S.op("pe", lambda e, k=k, nb=nb: e.matmul(pdst[:, nb * 512:(nb + 1) * 512], lhsT=tT[:, k, :],
                                                                      rhs=wsb[:, k, nb * 512:(nb + 1) * 512], start=(k == 0), stop=(k == 7)),
                                 [tT, wsb], [pdst])

                def p5_loads(c):
                    r0 = c * 128
                    Bn = dict(pj=pj_r.next(), xq=xq_r.next(), m0=m0_r.next(), xi=xi_r.next())
                    S.dma(Bn["pj"][:], PJ[r0:r0 + 128, 2048:PJ_W], Bn["pj"], writes=[Bn["pj"]])
                    S.dma(Bn["xq"][:], XQ[c, :, :, :], Bn["xq"], writes=[Bn["xq"]])
                    S.dma(Bn["m0"][:], M0[r0:r0 + 128, :], Bn["m0"], writes=[Bn["m0"]])
                    S.dma(Bn["xi"][:], Xsrc[r0:r0 + 128, :], Bn["xi"], writes=[Bn["xi"]])
                    return Bn

                def views(Bn):
                    pj = Bn["pj"]
                    o = -2048
                    return dict(pj=pj, xq=Bn["xq"], m0=Bn["m0"], xi=Bn["xi"],
                                sgg=pj[:, PJ_GG + o:PJ_GG + o + 1024], uu=pj[:, PJ_GUV + o:PJ_GUV + o + 1024],
                                vv=pj[:, PJ_GUV + o + 1024:PJ_GUV + o + 2048], sxg=pj[:, PJ_XG + o:PJ_XG + o + 1024],
                                g1=pj[:, PJ_MG + o + 1024:PJ_MG + o + 2048], g2=pj[:, PJ_MG + o + 2048:PJ_MG + o + 3072])

                def head(c, Bn):
                    V_ = views(Bn)
                    pj = V_["pj"]; xq = V_["xq"]; vv = V_["vv"]
                    for h in range(4):
                        for mt in range(2):
                            for dc in range(2):
                                S.op("pe", lambda e, h=h, mt=mt, dc=dc: e.matmul(
                                    pA[:, (h * 2 + mt) * 128:(h * 2 + mt + 1) * 128], lhsT=kT[:, h * 2 + dc, mt * 128:(mt + 1) * 128],
                                    rhs=xq[:, h * 2 + dc, :], start=(dc == 0), stop=(dc == 1)), [kT, xq], [pA])
                    yield
                    S.op("act", lambda e: e.activation(out=Eb[:], in_=pA[:, :].rearrange("p (j t) -> p j t", t=128), func=AF.Exp,
                                                        scale=1.0 / 16.0), [pA], [Eb])
                    yield
                    for i in range(2):
                        S.op("dve", lambda e, i=i: e.bn_stats(out=bst[:, i * 6:(i + 1) * 6], in_=vv[:, i * 512:(i + 1) * 512]), [pj], [bst])
                    S.op("dve", lambda e: e.bn_aggr(out=bst[:, 12:14], in_=bst[:, 0:12]), [bst], [bst])
                    S.op("dve", lambda e: e.tensor_scalar(out=bst[:, 14:15], in0=bst[:, 13:14], scalar1=EPS, scalar2=None, op0=ALU.add),
                         [bst], [bst])
                    S.op("act", lambda e: e.activation(out=bst[:, 14:15], in_=bst[:, 14:15], func=AF.Ln), [bst], [bst])
                    S.op("act", lambda e: e.activation(out=bst[:, 15:16], in_=bst[:, 14:15], func=AF.Exp, scale=-0.5), [bst], [bst])
                    yield
                    S.op("dve", lambda e: e.tensor_scalar(out=vt[:], in0=vv, scalar1=bst[:, 12:13], scalar2=bst[:, 15:16],
                                                           op0=ALU.subtract, op1=ALU.mult), [pj, bst], [vt])
                    S.op("dve", lambda e: e.tensor_tensor(out=vt[:], in0=vt[:], in1=prow[:, G_LNG:G_LNG + 1024], op=ALU.mult), [vt, prow], [vt])
                    S.op("dve", lambda e: e.tensor_tensor(out=vn[:], in0=vt[:], in1=prow[:, G_LNB:G_LNB + 1024], op=ALU.add), [vt, prow], [vn])
                    yield
                    for g in range(8):
                        S.op("pe", lambda e, g=g: e.matmul(pB[:, g * 128:(g + 1) * 128], lhsT=wsT[:, g * 128:(g + 1) * 128],
                                                           rhs=vn[:, g * 128:(g + 1) * 128], start=True, stop=True), [wsT, vn], [pB])

                def mid(c, Bn):
                    V_ = views(Bn)
                    pj = V_["pj"]; m0 = V_["m0"]
                    for h in range(4):
                        for mt in range(2):
                            S.op("pe", lambda e, h=h, mt=mt: e.matmul(pC[:, h * 256:(h + 1) * 256], lhsT=Eb[:, h * 2 + mt, :],
                                                                      rhs=Vv[:, mt, h * 256:(h + 1) * 256], start=(mt == 0), stop=(mt == 1)),
                                 [Eb, Vv], [pC])
                    for h in range(4):
                        for mt in range(2):
                            S.op("pe", lambda e, h=h, mt=mt: e.matmul(p_den[:, h * 2:h * 2 + 2], lhsT=Eb[:, h * 2 + mt, :], rhs=onesb[:],
                                                                      start=(mt == 0), stop=(mt == 1)), [Eb, onesb], [p_den])
                    yield
                    S.op("dve", lambda e: e.tensor_tensor(out=svt[:, :].rearrange("p (g q) -> p g q", q=128),
                                                           in0=pB[:, :].rearrange("p (g q) -> p g q", q=128),
                                                           in1=bc3(bsT[:, 0:8], 128), op=ALU.add), [pB, bsT], [svt])
                    S.op("dve", lambda e: e.tensor_tensor(out=svt[:], in0=svt[:], in1=V_["uu"], op=ALU.mult), [svt, pj], [svt])
                    S.op("dve", lambda e: e.tensor_tensor(out=yg[:], in0=svt[:], in1=V_["sgg"], op=ALU.mult), [svt, pj], [yg])
                    S.op("dve", lambda e: e.reciprocal(out=rden[:], in_=p_den[:, :]), [p_den], [rden])
                    rd4 = rden[:, :].rearrange("p (h two) -> p h two", two=2)[:, :, 0:1].broadcast_to([128, 4, 256])
                    S.op("dve", lambda e: e.tensor_tensor(out=ot[:, :].rearrange("p (h q) -> p h q", q=256),
                                                           in0=pC[:, :].rearrange("p (h q) -> p h q", q=256), in1=rd4, op=ALU.mult),
                         [pC, rden], [ot])
                    S.op("dve", lambda e: e.tensor_tensor(out=yx[:], in0=ot[:], in1=V_["sxg"], op=ALU.mult), [ot, pj], [yx])
                    yield
                    transpose8(yg, yg)
                    yield
                    proj(pA, wbg)
                    yield
                    S.op("dve", lambda e: e.tensor_tensor(out=macc[:], in0=pA[:, :], in1=V_["g1"], op=ALU.mult), [pA, pj], [macc])
                    S.op("dve", lambda e: e.tensor_tensor(out=macc[:], in0=macc[:], in1=m0[:], op=ALU.add), [macc, m0], [macc])
                    transpose8(yx, yx)
                    proj(pB, wbx)
                    S.op("dve", lambda e: e.tensor_tensor(out=ot[:], in0=pB[:, :], in1=V_["g2"], op=ALU.mult), [pB, pj], [ot])
                    S.op("dve", lambda e: e.tensor_tensor(out=mb[:], in0=ot[:], in1=macc[:], op=ALU.add), [ot, macc], [mb])

                def tail(c, Bn):
                    r0 = c * 128
                    xi = Bn["xi"]; xo = xo_r.next()
                    transpose8(mb, mb)
                    proj(pC, wo)
                    S.op("act", lambda e: e.activation(out=pjunk[:], in_=pC[:, :], func=AF.Square, accum_out=pss[:, 0:1]), [pC], [pjunk, pss])
                    S.op("dve", lambda e: e.tensor_scalar(out=pss[:, 1:2], in0=pss[:, 0:1], scalar1=1.0 / D, scalar2=EPS,
                                                           op0=ALU.mult, op1=ALU.add), [pss], [pss])
                    S.op("act", lambda e: e.activation(out=pss[:, 2:3], in_=pss[:, 1:2], func=AF.Ln), [pss], [pss])
                    S.op("act", lambda e: e.activation(out=pss[:, 3:4], in_=pss[:, 2:3], func=AF.Exp, scale=-0.5), [pss], [pss])
                    S.op("dve", lambda e: e.scalar_tensor_tensor(out=xo[:], in0=pC[:, :], scalar=pss[:, 3:4],
                                                                  in1=prow[:, G_POST:G_POST + 1024], op0=ALU.mult, op1=ALU.mult),
                         [pC, pss, prow], [xo])
                    S.op("dve", lambda e: e.tensor_tensor(out=xo[:], in0=xo[:], in1=xi[:], op=ALU.add), [xo, xi], [xo])
                    S.dma(Xdst[r0:r0 + 128, :], xo[:], xo, reads=[xo])

                def drain(gen):
                    for _ in gen:
                        pass

                cur = p5_loads(0)
                drain(head(0, cur))
                for c in range(NCH):
                    nxt = p5_loads(c + 1) if c + 1 < NCH else None
                    gm = mid(c, cur)
                    gh = head(c + 1, nxt) if nxt is not None else iter(())
                    for _ in range(4):
                        next(gm, None)
                        next(gh, None)
                    drain(gm)
                    drain(gh)
                    tail(c, cur)
                    cur = nxt
                S.barrier()
                S.release(pj_r.bufs + xq_r.bufs + m0_r.bufs + xi_r.bufs + xo_r.bufs)
        S.barrier()
    return nc


def host_consts():
    ident = np.eye(128, dtype=np.float32)
    k = np.arange(128)[:, None]
    t = np.arange(128)[None, :]
    tri0 = (k <= t).astype(np.float32)
    tri1 = (k >= t).astype(np.float32)
    esel = np.zeros((96, 32, 128), np.float32)
    for h in range(32):
        for j in range(3):
            esel[j * 32 + h, h, :] = 1.0
    nm0 = np.where(k > t, -30000.0, 0.0).astype(np.float32)
    nm1 = np.where(k < t, -30000.0, 0.0).astype(np.float32)
    c_nm = np.concatenate([np.tile(nm0, (1, 4)), np.tile(nm1, (1, 4))], axis=1)
    return {"c_ident": ident, "c_tri0": tri0, "c_tri1": tri1, "c_esel": esel.reshape(96, 32 * 128), "c_nm": c_nm}


def host_params(inp, depth):
    f = lambda a: np.asarray(a, dtype=np.float32)
    p_row = np.zeros((depth, 1, 9 * 1024), np.float32)
    p_row[:, 0, 0:1024] = f(inp["norm_pre_g"])[:depth]
    p_row[:, 0, 1024:3072] = f(inp["ssd_norm_g"])[:depth]
    p_row[:, 0, 3072:4096] = f(inp["gmlp_ln_g"])[:depth]
    p_row[:, 0, 4096:5120] = f(inp["gmlp_ln_b"])[:depth]
    p_row[:, 0, 5120:6144] = f(inp["mem_norm_g"])[:depth]
    p_row[:, 0, 6144:7168] = f(inp["norm_post_g"])[:depth]
    p_row[:, 0, 7168:7232] = f(inp["dt_bias"])[:depth].reshape(depth, 64)
    p_row[:, 0, 7232:7296] = f(inp["a_log"])[:depth].reshape(depth, 64)
    p_row[:, 0, 7296:7328] = f(inp["d_skip"])[:depth]
    cw = f(inp["conv_w"])[:depth]
    p_convw = np.ascontiguousarray(cw.reshape(depth, 5, 32, 128).transpose(0, 3, 2, 1)).reshape(depth, 128, 160)
    cb = f(inp["conv_b"])[:depth]
    p_convb = np.ascontiguousarray(cb.reshape(depth, 32, 128).transpose(0, 2, 1))
    ws = f(inp["w_spatial"])[:depth]
    p_wsT = np.ascontiguousarray(ws.transpose(0, 3, 1, 2)).reshape(depth, 128, 1024)
    bs = f(inp["b_spatial"])[:depth]
    p_bsT = np.ascontiguousarray(bs.transpose(0, 2, 1))
    return {"p_row": p_row, "p_convw": p_convw, "p_convb": p_convb, "p_wsT": p_wsT, "p_bsT": p_bsT}


_NC_CACHE = {}


def kernel(**inputs):
    x = np.asarray(inputs["x"], dtype=np.float32)
    B, L, _ = x.shape
    depth = inputs["w_in"].shape[0]
    key = (L, depth)
    if key not in _NC_CACHE:
        _NC_CACHE[key] = build(L, depth)
    nc = _NC_CACHE[key]
    shared = {}
    shared.update(host_consts())
    shared.update(host_params(inputs, depth))
    for n in ("w_in", "w_kv", "w_br_ssd", "w_br_gmlp", "w_br_xattn", "w_out"):
        shared[n] = np.ascontiguousarray(np.asarray(inputs[n], dtype=np.float32))
    mem = np.asarray(inputs["mem"], dtype=np.float32)
    in_maps = []
    for b in range(B):
        m = dict(shared)
        m["x"] = np.ascontiguousarray(x[b])
        m["mem"] = np.ascontiguousarray(mem[b])
        in_maps.append(m)
    res = run_bass_kernel_spmd(nc, in_maps, core_ids=list(range(B)))
    return np.stack([np.asarray(res.results[b]["out"], dtype=np.float32) for b in range(B)], axis=0)
```

```python
import contextlib
import numpy as np
import concourse.bass as bass
import concourse.mybir as mybir
from concourse.bass_utils import run_bass_kernel_spmd

F32 = mybir.dt.float32
BF16 = mybir.dt.bfloat16
AF = mybir.ActivationFunctionType
ALU = mybir.AluOpType

D = 1024
NCORES = 4
EPS = 1e-6
SAME_ENGINE_WAITS = True
DMA_CAST = True

C_Z, C_XBC, C_DT, C_GG, C_GUV, C_XQ, C_XG, C_MG = 0, 2048, 6144, 6208, 7232, 9280, 10304, 11328
PJ_Z, PJ_GG, PJ_GUV, PJ_XG, PJ_MG, PJ_W = 0, 2048, 3072, 5120, 6144, 9216


class Buf:
    def __init__(self, t, name):
        self.t = t
        self.name = name
        self.w = None
        self.r = {}
        self.dsem = None

    def __getitem__(self, idx):
        return self.t[idx]


class Sched:
    def __init__(self, nc, es):
        self.nc = nc
        self.es = es
        self.eng = {"pe": nc.tensor, "dve": nc.vector, "act": nc.scalar, "pool": nc.gpsimd, "sp": nc.sync}
        self.esem = {}
        self.cnt = {}
        self.seen = {e: {} for e in self.eng}
        self.nsem = 0
        for e in ("pe", "dve", "act", "pool"):
            self.esem[e] = self.new_sem("e_" + e)
            self.cnt[e] = 0
        self.free_dsems = []
        self.all_dsems = []

    def new_sem(self, name):
        self.nsem += 1
        return self.es.enter_context(self.nc.semaphore(name + str(self.nsem)))

    def get_dsem(self):
        if self.free_dsems:
            return self.free_dsems.pop()
        s = [self.new_sem("d"), 0]
        self.all_dsems.append(s)
        return s

    def release(self, bufs):
        for b in bufs:
            if b.dsem is not None:
                self.free_dsems.append(b.dsem)
                b.dsem = None

    def _wait(self, e, tok):
        sem, val, seng = tok
        if seng == e and (e == "pe" or not SAME_ENGINE_WAITS):
            return
        k = id(sem)
        if self.seen[e].get(k, 0) >= val:
            return
        self.eng[e].wait_ge(sem, val)
        self.seen[e][k] = val

    def _deps(self, e, reads, writes):
        for b in reads:
            if b.w is not None:
                self._wait(e, b.w)
        for b in writes:
            if b.w is not None:
                self._wait(e, b.w)
            for tok in b.r.values():
                self._wait(e, tok)

    def op(self, e, fn, reads=(), writes=(), sig=True):
        self._deps(e, reads, writes)
        if not sig:
            return fn(self.eng[e])
        if self.cnt[e] >= 30000:
            self.esem[e] = self.new_sem("e_" + e)
            self.cnt[e] = 0
        ins = fn(self.eng[e])
        self.cnt[e] += 1
        ins.then_inc(self.esem[e], 1)
        tok = (self.esem[e], self.cnt[e], e)
        for b in reads:
            b.r[e] = tok
        for b in writes:
            b.w = tok
            b.r = {}
        return ins

    def dma(self, out, in_, sb, reads=(), writes=(), q="sp"):
        self._deps(q, reads, writes)
        if sb.dsem is None:
            sb.dsem = self.get_dsem()
        ins = self.eng[q].dma_start(out=out, in_=in_)
        sb.dsem[1] += 16
        ins.then_inc(sb.dsem[0], 16)
        tok = (sb.dsem[0], sb.dsem[1], "dma")
        for b in reads:
            b.r[id(sb.dsem[0])] = tok
        for b in writes:
            b.w = tok
            b.r = {}

    def barrier(self):
        toks = [(self.esem[e], self.cnt[e], e) for e in self.esem if self.cnt[e] > 0]
        toks += [(s[0], s[1], "dma") for s in self.all_dsems if s[1] > 0]
        for e in self.eng:
            for tok in toks:
                if tok[2] == e:
                    continue
                self._wait(e, tok)


class Ring:
    def __init__(self, bufs):
        self.bufs = bufs
        self.i = 0

    def next(self):
        b = self.bufs[self.i % len(self.bufs)]
        self.i += 1
        return b


def bc3(ap, n):
    p, a = ap.shape
    return ap.unsqueeze(2).broadcast_to([p, a, n])


def build(L, depth, debug=False):
    NCH = L // 128
    HALF = min(L, 2048)
    NHALF = L // HALF
    HCH = HALF // 128
    NTB = HALF // 512
    nc = bass.Bass("TRN2", target_bir_lowering=False)

    def din(name, shape, dt=F32):
        return nc.dram_tensor(name, list(shape), dt, kind="ExternalInput").ap()

    skind = "ExternalOutput" if debug else "Internal"

    def dscr(name, shape, dt):
        return nc.dram_tensor(name, list(shape), dt, kind=skind).ap()

    x_in = din("x", [L, D])
    mem_in = din("mem", [256, D])
    w_in = din("w_in", [depth, D, 14400])
    w_kv = din("w_kv", [depth, D, 2048])
    w_bs = din("w_br_ssd", [depth, 2048, D])
    w_bg = din("w_br_gmlp", [depth, D, D])
    w_bx = din("w_br_xattn", [depth, D, D])
    w_o = din("w_out", [depth, D, D])
    p_row = din("p_row", [depth, 1, 9 * 1024])
    p_convw = din("p_convw", [depth, 128, 32 * 5])
    p_convb = din("p_convb", [depth, 128, 32])
    p_wsT = din("p_wsT", [depth, 128, 8 * 128])
    p_bsT = din("p_bsT", [depth, 128, 8])
    c_ident = din("c_ident", [128, 128])
    c_tri0 = din("c_tri0", [128, 128])
    c_tri1 = din("c_tri1", [128, 128])
    c_esel = din("c_esel", [96, 32 * 128])
    c_nm = din("c_nm", [128, 2 * 512])
    out_d = nc.dram_tensor("out", [L, D], F32, kind="ExternalOutput").ap()

    X1 = dscr("s_x1", [L, D], F32)
    PJ = dscr("s_pj", [L, PJ_W], BF16)
    XBC = dscr("s_xbc", [NCH, 128, 32, 128], BF16)
    XQ = dscr("s_xq", [NCH, 128, 8, 128], BF16)
    DTS = dscr("s_dts", [L, 384], F32)
    CS3 = dscr("s_cs3", [L, 192], BF16)
    XS = dscr("s_xs", [L, 2048], BF16)
    BTM = dscr("s_btm", [L, 1024], BF16)
    Y0 = dscr("s_y0", [L, 2048], F32)
    M0 = dscr("s_m0", [L, D], F32)

    with contextlib.ExitStack() as es:
        S = Sched(nc, es)

        uid = [0]

        def sb(st, name, shape, dt):
            uid[0] += 1
            return Buf(st.enter_context(nc.sbuf_tensor(f"{name}_u{uid[0]}", list(shape), dt)), name)

        def ps(st, name, shape, dt):
            uid[0] += 1
            return Buf(st.enter_context(nc.psum_tensor(f"{name}_u{uid[0]}", list(shape), dt)), name)

        ident = sb(es, "ident", [128, 128], BF16)
        tri0 = sb(es, "tri0", [128, 128], F32)
        tri1 = sb(es, "tri1", [128, 128], F32)
        onesf = sb(es, "onesf", [128, 128], F32)
        onesb = sb(es, "onesb", [128, 2], BF16)
        esel = sb(es, "esel", [96, 32 * 128], BF16)
        negm = sb(es, "negm", [128, 2 * 512], BF16)
        prow = sb(es, "prow", [128, 9 * 1024], F32)
        convw = sb(es, "convw", [128, 160], F32)
        convb = sb(es, "convb", [128, 32], F32)
        wsT = sb(es, "wsT", [128, 1024], BF16)
        bsT = sb(es, "bsT", [128, 8], F32)
        a_bc = sb(es, "a_bc", [128, 64], F32)
        G_PRE, G_SSD, G_LNG, G_LNB, G_MEM, G_POST, G_MISC = 0, 1024, 3072, 4096, 5120, 6144, 7168

        with contextlib.ExitStack() as st:
            tmpf = sb(st, "c_tmpf", [128, 32 * 128], F32)
            S.dma(tmpf[:, 0:128], c_ident[:, :], tmpf, writes=[tmpf])
            S.op("dve", lambda e: e.tensor_copy(out=ident[:], in_=tmpf[:, 0:128]), [tmpf], [ident])
            S.dma(tri0[:], c_tri0[:, :], tri0, writes=[tri0])
            S.dma(tri1[:], c_tri1[:, :], tri1, writes=[tri1])
            S.dma(tmpf[0:96, :], c_esel[:, :], tmpf, writes=[tmpf])
            S.op("dve", lambda e: e.tensor_copy(out=esel[:], in_=tmpf[0:96, :]), [tmpf], [esel])
            S.dma(tmpf[:, 0:1024], c_nm[:, :], tmpf, writes=[tmpf])
            S.op("dve", lambda e: e.tensor_copy(out=negm[:], in_=tmpf[:, 0:1024]), [tmpf], [negm])
            S.op("pool", lambda e: e.memset(onesf[:], 1.0), [], [onesf])
            S.op("pool", lambda e: e.memset(onesb[:], 1.0), [], [onesb])
            S.barrier()
            S.release([tmpf, tri0, tri1])

        for l in range(depth):
            Xsrc = x_in if l == 0 else X1
            Xdst = out_d if l == depth - 1 else X1
            if depth == 1:
                Xdst = out_d

            with contextlib.ExitStack() as st:
                tmpw = sb(st, "p_tmpw", [128, 1024], F32)
                S.dma(prow[:], p_row[l, 0:1, :].partition_broadcast(128), prow, writes=[prow])
                S.dma(convw[:], p_convw[l, :, :], convw, writes=[convw])
                S.dma(convb[:], p_convb[l, :, :], convb, writes=[convb])
                S.dma(bsT[:], p_bsT[l, :, :], bsT, writes=[bsT])
                S.dma(tmpw[:], p_wsT[l, :, :], tmpw, writes=[tmpw])
                S.op("dve", lambda e: e.tensor_copy(out=wsT[:], in_=tmpw[:]), [tmpw], [wsT])
                S.op("act", lambda e: e.activation(out=a_bc[:], in_=prow[:, G_MISC + 64:G_MISC + 128], func=AF.Exp),
                     [prow], [a_bc])
                S.op("dve", lambda e: e.tensor_scalar(out=a_bc[:], in0=a_bc[:], scalar1=-1.0, scalar2=None,
                                                       op0=ALU.mult), [a_bc], [a_bc])
                S.barrier()
                S.release([tmpw, prow, convw, convb, bsT])
            dtb_bc = prow[:, G_MISC:G_MISC + 64]
            dsk_bc = prow[:, G_MISC + 128:G_MISC + 160]

            for hf in range(NHALF):
                hs = hf * HALF
                with contextlib.ExitStack() as st:
                    hT = sb(st, "hT", [128, 8, HALF + 4], BF16)
                    with contextlib.ExitStack() as st1:
                        xin_r = Ring([sb(st1, f"xin{i}", [128, 1024], F32) for i in range(9)])
                        jk_r = Ring([sb(st1, f"jk{i}", [128, 1024], BF16) for i in range(4)])
                        junk = sb(st1, "junk", [128, 1024], BF16)
                        hb_r = Ring([sb(st1, f"hb{i}", [128, 1024], BF16) for i in range(5)])
                        ss_r = Ring([sb(st1, f"ss{i}", [128, 4], F32) for i in range(5)])
                        pT_r = Ring([ps(st1, f"pT{i}", [128, 1024], BF16) for i in range(2)])

                        def norm_load(r0, nr):
                            xin = xin_r.next()
                            S.dma(xin[0:nr, :], Xsrc[r0:r0 + nr, :], xin, writes=[xin])
                            return xin

                        def norm_rows(r0, nr, col0, xin=None):
                            hb = hb_r.next(); ss = ss_r.next(); pT = pT_r.next()
                            if xin is None:
                                xin = norm_load(r0, nr)
                            S.op("act", lambda e: e.activation(out=junk[0:nr, :], in_=xin[0:nr, :], func=AF.Square,
                                                                accum_out=ss[0:nr, 0:1]), [xin], [junk, ss])
                            S.op("dve", lambda e: e.tensor_scalar(out=ss[0:nr, 1:2], in0=ss[0:nr, 0:1], scalar1=1.0 / D,
                                                                   scalar2=EPS, op0=ALU.mult, op1=ALU.add), [ss], [ss])
                            S.op("act", lambda e: e.activation(out=ss[0:nr, 2:3], in_=ss[0:nr, 1:2], func=AF.Ln), [ss], [ss])
                            S.op("act", lambda e: e.activation(out=ss[0:nr, 3:4], in_=ss[0:nr, 2:3], func=AF.Exp, scale=-0.5), [ss], [ss])
                            S.op("dve", lambda e: e.scalar_tensor_tensor(out=hb[0:nr, :], in0=xin[0:nr, :], scalar=ss[0:nr, 3:4],
                                                                          in1=prow[0:nr, G_PRE:G_PRE + 1024], op0=ALU.mult,
                                                                          op1=ALU.mult), [xin, ss, prow], [hb])
                            for k in range(8):
                                S.op("pe", lambda e, k=k: e.transpose(out=pT[:, k * 128:k * 128 + nr],
                                                                      in_=hb[0:nr, k * 128:(k + 1) * 128],
                                                                      identity=ident[0:nr, 0:nr]), [hb, ident], [pT], sig=(k == 7))
                            src = pT[:, :].rearrange("p (k t) -> p k t", t=128)[:, :, 0:nr]
                            S.op("act", lambda e: e.activation(out=hT[:, :, col0:col0 + nr], in_=src, func=AF.Copy),
                                 [pT], [hT])

                        if hs - 2 >= 0:
                            norm_rows(hs - 2, 2, 0)
                        else:
                            S.op("pool", lambda e: e.memset(hT[:, :, 0:2], 0.0), [], [hT])
                        if hs + HALF + 2 <= L:
                            norm_rows(hs + HALF, 2, HALF + 2)
                        else:
                            S.op("pool", lambda e: e.memset(hT[:, :, HALF + 2:HALF + 4], 0.0), [], [hT])
                        NB1 = 4
                        loads = {}

                        def ensure_load(c):
                            if c < HCH and c not in loads:
                                loads[c] = norm_load(hs + c * 128, 128)

                        for c in range(min(NB1, HCH)):
                            ensure_load(c)
                        for b0 in range(0, HCH, NB1):
                            batch = list(range(b0, min(b0 + NB1, HCH)))
                            for c in batch:
                                ensure_load(c + NB1)
                            ctx = {c: dict(xin=loads.pop(c), hb=hb_r.next(), ss=ss_r.next(), jk=jk_r.next()) for c in batch}
                            for c in batch:
                                X = ctx[c]
                                S.op("act", lambda e, X=X: e.activation(out=X["jk"][:], in_=X["xin"][:], func=AF.Square,
                                                                        accum_out=X["ss"][:, 0:1]), [X["xin"]], [X["jk"], X["ss"]])
                            for c in batch:
                                X = ctx[c]
                                S.op("dve", lambda e, X=X: e.tensor_scalar(out=X["ss"][:, 1:2], in0=X["ss"][:, 0:1], scalar1=1.0 / D,
                                                                           scalar2=EPS, op0=ALU.mult, op1=ALU.add), [X["ss"]], [X["ss"]])
                            for c in batch:
                                X = ctx[c]
                                S.op("act", lambda e, X=X: e.activation(out=X["ss"][:, 2:3], in_=X["ss"][:, 1:2], func=AF.Ln), [X["ss"]], [X["ss"]])
                            for c in batch:
                                X = ctx[c]
                                S.op("act", lambda e, X=X: e.activation(out=X["ss"][:, 3:4], in_=X["ss"][:, 2:3], func=AF.Exp, scale=-0.5), [X["ss"]], [X["ss"]])
                            for c in batch:
                                X = ctx[c]
                                S.op("dve", lambda e, X=X: e.scalar_tensor_tensor(out=X["hb"][:], in0=X["xin"][:], scalar=X["ss"][:, 3:4],
                                                                                  in1=prow[:, G_PRE:G_PRE + 1024], op0=ALU.mult,
                                                                                  op1=ALU.mult), [X["xin"], X["ss"], prow], [X["hb"]])
                            for c in batch:
                                X = ctx[c]
                                pT = pT_r.next()
                                col0 = 2 + c * 128
                                for k in range(8):
                                    S.op("pe", lambda e, k=k, X=X, pT=pT: e.transpose(out=pT[:, k * 128:(k + 1) * 128],
                                                                                      in_=X["hb"][:, k * 128:(k + 1) * 128],
                                                                                      identity=ident[:]), [X["hb"], ident], [pT], sig=(k == 7))
                                S.op("act", lambda e, pT=pT, col0=col0: e.activation(out=hT[:, :, col0:col0 + 128],
                                                                                     in_=pT[:, :].rearrange("p (k t) -> p k t", t=128),
                                                                                     func=AF.Copy), [pT], [hT])
                        S.barrier()
                        S.release(xin_r.bufs)

                    with contextlib.ExitStack() as st2:
                        wf_r = Ring([sb(st2, f"wf{i}", [128, 8, 512], F32) for i in range(0 if DMA_CAST else 2)])
                        wb_r = Ring([sb(st2, f"wb{i}", [128, 8, 512], BF16) for i in range(3)])
                        pre_r = Ring([sb(st2, f"pre{i}", [128, HALF + 4], BF16) for i in range(2)])
                        acc_r = Ring([sb(st2, f"acc{i}", [128, HALF], F32) for i in range(2)])
                        xc_r = Ring([sb(st2, f"xc{i}", [128, HALF], BF16) for i in range(2)])
                        stg_r = Ring([sb(st2, f"stg{i}", [128, 512], BF16) for i in range(4)])
                        pm_r = Ring([ps(st2, f"pm{i}", [128, 512], F32) for i in range(4)])
                        pd_r = Ring([ps(st2, f"pd{i}", [128, 256], F32) for i in range(2)])
                        dt_r = Ring([sb(st2, f"dtw{i}", [128, 384 + 384], F32) for i in range(5)])
                        c3_r = Ring([sb(st2, f"c3w{i}", [128, 192], BF16) for i in range(5)])

                        def load_w(col0, ncol):
                            wb = wb_r.next()
                            src = w_in[l, :, col0:col0 + ncol].rearrange("(k p) c -> p k c", p=128)
                            if DMA_CAST:
                                S.dma(wb[:, :, 0:ncol], src, wb, writes=[wb], q="pool")
                            else:
                                wf = wf_r.next()
                                S.dma(wf[:, :, 0:ncol], src, wf, writes=[wf])
                                S.op("act", lambda e: e.activation(out=wb[:, :, 0:ncol], in_=wf[:, :, 0:ncol], func=AF.Copy), [wf], [wb])
                            return wb

                        segs = [(C_Z, PJ_Z, 2048, AF.Silu), (C_GG, PJ_GG, 1024, AF.Silu),
                                (C_GUV, PJ_GUV, 2048, AF.Gelu_apprx_tanh), (C_XG, PJ_XG, 1024, AF.Silu),
                                (C_MG, PJ_MG, 3072, AF.Sigmoid)]
                        wlist = ([(C_XBC + bi * 512, 512) for bi in range(8)] + [(C_XQ + bi * 512, 512) for bi in range(2)]
                                 + [(wc + b0, 512) for (wc, pc, width, fn) in segs for b0 in range(0, width, 512)] + [(C_DT, 64)])
                        wq = []
                        widx = [0]

                        def issue_w():
                            if widx[0] < len(wlist):
                                wq.append(load_w(*wlist[widx[0]]))
                                widx[0] += 1

                        def next_w():
                            if not wq:
                                issue_w()
                            wb_ = wq.pop(0)
                            issue_w()
                            return wb_

                        pending = []

                        def flush_pending():
                            while pending:
                                pending.pop(0)()

                        def store_tile(dst, xc, m):
                            for q in range(0, HCH, 8):
                                nq = min(8, HCH - q)
                                o = dst[hs // 128 + q:hs // 128 + q + nq, :, m, :].rearrange("c p t -> p c t")
                                i_ = xc[:, q * 128:(q + nq) * 128].rearrange("p (c t) -> p c t", t=128)
                                S.dma(o, i_, xc, reads=[xc])

                        def feat_block(col0, kind, mbase):
                            wb = next_w()
                            if kind == "xq":
                                flush_pending()
                            for mi in range(4):
                                m = mbase + mi
                                if kind == "xbc":
                                    pre = pre_r.next()
                                    blocks = [(i * 512, 512) for i in range(NTB)] + [(HALF, 4)]
                                else:
                                    pre = xc_r.next()
                                    blocks = [(i * 512, 512) for i in range(NTB)]
                                for (c0, n) in blocks:
                                    pm = pm_r.next()
                                    hc0 = c0 if kind == "xbc" else c0 + 2
                                    for k in range(8):
                                        S.op("pe", lambda e, k=k: e.matmul(pm[:, 0:n], lhsT=wb[:, k, mi * 128:(mi + 1) * 128],
                                                                           rhs=hT[:, k, hc0:hc0 + n], start=(k == 0), stop=(k == 7)),
                                             [wb, hT], [pm], sig=(k == 7))
                                    S.op("act", lambda e: e.activation(out=pre[:, c0:c0 + n], in_=pm[:, 0:n], func=AF.Copy),
                                         [pm], [pre])
                                if kind == "xbc":
                                    acc = acc_r.next(); xc = xc_r.next()
                                    S.op("act", lambda e: e.activation(out=acc[:], in_=pre[:, 0:HALF], func=AF.Copy,
                                                                        scale=convw[:, m * 5:m * 5 + 1]), [pre, convw], [acc])
                                    flush_pending()
                                    for k in range(1, 5):
                                        S.op("dve", lambda e, k=k: e.scalar_tensor_tensor(
                                            out=acc[:], in0=pre[:, k:k + HALF], scalar=convw[:, m * 5 + k:m * 5 + k + 1],
                                            in1=acc[:], op0=ALU.mult, op1=ALU.add), [pre, convw, acc], [acc])

                                    def fin(acc=acc, xc=xc, m=m):
                                        S.op("act", lambda e: e.activation(out=xc[:], in_=acc[:], func=AF.Silu,
                                                                            bias=convb[:, m:m + 1]), [acc, convb], [xc])
                                        store_tile(XBC, xc, m)
                                    pending.append(fin)
                                else:
                                    store_tile(XQ, pre, m)

                        for bi in range(8):
                            feat_block(C_XBC + bi * 512, "xbc", bi * 4)
                        for bi in range(2):
                            feat_block(C_XQ + bi * 512, "xq", bi * 4)
                        flush_pending()

                        for (wc, pc, width, fn) in segs:
                            for b0 in range(0, width, 512):
                                wb = next_w()
                                for c in range(HCH):
                                    pm = pm_r.next(); stg = stg_r.next()
                                    for k in range(8):
                                        S.op("pe", lambda e, k=k: e.matmul(pm[:, :], lhsT=hT[:, k, 2 + c * 128:2 + (c + 1) * 128],
                                                                           rhs=wb[:, k, :], start=(k == 0), stop=(k == 7)),
                                             [wb, hT], [pm], sig=(k == 7))
                                    S.op("act", lambda e: e.activation(out=stg[:], in_=pm[:, :], func=fn), [pm], [stg])
                                    r0 = hs + c * 128
                                    S.dma(PJ[r0:r0 + 128, pc + b0:pc + b0 + 512], stg[:], stg, reads=[stg])

                        wb = next_w()
                        pdq = Ring(pm_r.bufs + pd_r.bufs)

                        def dt_chunk(c):
                            pd = pdq.next(); dw = dt_r.next(); c3 = c3_r.next()
                            r0 = hs + c * 128
                            for k in range(8):
                                S.op("pe", lambda e, k=k: e.matmul(pd[:, 0:64], lhsT=hT[:, k, 2 + c * 128:2 + (c + 1) * 128],
                                                                   rhs=wb[:, k, 0:64], start=(k == 0), stop=(k == 7)), [wb, hT], [pd], sig=(k == 7))
                            V_, A_, E_, R_ = 384, 448, 512, 576
                            S.op("dve", lambda e: e.tensor_tensor(out=dw[:, V_:V_ + 64], in0=pd[:, 0:64], in1=dtb_bc, op=ALU.add),
                                 [pd, prow], [dw])
                            yield
                            S.op("act", lambda e: e.activation(out=dw[:, A_:A_ + 64], in_=dw[:, V_:V_ + 64], func=AF.Abs), [dw], [dw])
                            yield
                            S.op("act", lambda e: e.activation(out=dw[:, E_:E_ + 64], in_=dw[:, A_:A_ + 64], func=AF.Exp, scale=-1.0),
                                 [dw], [dw])
                            yield
                            S.op("act", lambda e: e.activation(out=dw[:, E_:E_ + 64], in_=dw[:, E_:E_ + 64], func=AF.Ln, bias=1.0),
                                 [dw], [dw])
                            yield
                            S.op("dve", lambda e: e.tensor_scalar(out=dw[:, R_:R_ + 64], in0=dw[:, V_:V_ + 64], scalar1=0.0,
                                                                   scalar2=None, op0=ALU.max), [dw], [dw])
                            yield
                            S.op("dve", lambda e: e.tensor_tensor(out=dw[:, 0:64], in0=dw[:, R_:R_ + 64], in1=dw[:, E_:E_ + 64],
                                                                   op=ALU.add), [dw], [dw])
                            yield
                            S.op("dve", lambda e: e.tensor_tensor(out=dw[:, V_:V_ + 64], in0=dw[:, 0:64], in1=a_bc[:], op=ALU.mult),
                                 [dw, a_bc], [dw])
                            yield
                            S.op("pe", lambda e: e.matmul(pd[:, 64:96], lhsT=tri0[:], rhs=dw[:, V_:V_ + 32], start=True, stop=True),
                                 [tri0, dw], [pd])
                            yield
                            S.op("pe", lambda e: e.matmul(pd[:, 96:128], lhsT=tri1[:], rhs=dw[:, V_ + 32:V_ + 64], start=True, stop=True),
                                 [tri1, dw], [pd])
                            yield
                            S.op("pe", lambda e: e.matmul(pd[:, 128:192], lhsT=onesf[:], rhs=dw[:, V_:V_ + 64], start=True, stop=True),
                                 [onesf, dw], [pd])
                            yield
                            T_ = 640
                            S.op("dve", lambda e: e.tensor_copy(out=dw[:, 320:384], in_=pd[:, 64:128]), [pd], [dw])
                            yield
                            S.op("dve", lambda e: e.tensor_copy(out=dw[:, T_:T_ + 64], in_=pd[:, 128:192]), [pd], [dw])
                            yield
                            L_ = 704
                            S.op("act", lambda e: e.activation(out=dw[:, L_:L_ + 64], in_=dw[:, 0:64], func=AF.Ln), [dw], [dw])
                            yield
                            S.op("dve", lambda e: e.tensor_tensor(out=dw[:, 64:128], in0=dw[:, L_:L_ + 64], in1=dw[:, 320:384],
                                                                   op=ALU.subtract), [dw], [dw])
                            yield
                            S.op("act", lambda e: e.activation(out=dw[:, 128:192], in_=dw[:, 320:384], func=AF.Exp), [dw], [dw])
                            yield
                            S.op("act", lambda e: e.activation(out=dw[:, 256:320], in_=dw[:, T_:T_ + 64], func=AF.Exp), [dw], [dw])
                            yield
                            S.op("dve", lambda e: e.tensor_tensor(out=dw[:, A_:A_ + 64], in0=dw[:, T_:T_ + 64], in1=dw[:, 320:384],
                                                                   op=ALU.subtract), [dw], [dw])
                            yield
                            S.op("act", lambda e: e.activation(out=dw[:, A_:A_ + 64], in_=dw[:, A_:A_ + 64], func=AF.Exp), [dw], [dw])
                            yield
                            S.op("dve", lambda e: e.tensor_tensor(out=dw[:, 192:256], in0=dw[:, A_:A_ + 64], in1=dw[:, 0:64],
                                                                   op=ALU.mult), [dw], [dw])
                            yield
                            c3v = c3[:, :].rearrange("p (d j h) -> p d j h", d=2, j=3)
                            csv = dw[:, 320:384].rearrange("p (d h) -> p d h", d=2)
                            r1 = dw[:, E_:E_ + 64].rearrange("p (d h) -> p d h", d=2)
                            r2 = dw[:, R_:R_ + 64].rearrange("p (d h) -> p d h", d=2)
                            S.op("dve", lambda e: e.tensor_copy(out=c3v[:, :, 0, :], in_=csv), [dw], [c3])
                            yield
                            S.op("dve", lambda e: e.tensor_tensor(out=r1, in0=csv, in1=c3v[:, :, 0, :], op=ALU.subtract), [dw, c3], [dw])
                            yield
                            S.op("dve", lambda e: e.tensor_copy(out=c3v[:, :, 1, :], in_=r1), [dw], [c3])
                            yield
                            S.op("dve", lambda e: e.tensor_tensor(out=r2, in0=r1, in1=c3v[:, :, 1, :], op=ALU.subtract), [dw, c3], [dw])
                            yield
                            S.op("dve", lambda e: e.tensor_copy(out=c3v[:, :, 2, :], in_=r2), [dw], [c3])
                            yield
                            S.dma(DTS[r0:r0 + 128, :], dw[:, 0:384], dw, reads=[dw])
                            yield
                            S.dma(CS3[r0:r0 + 128, :], c3[:], c3, reads=[c3])
                            yield

                        NBD = 4
                        for b0 in range(0, HCH, NBD):
                            gens = [dt_chunk(c) for c in range(b0, min(b0 + NBD, HCH))]
                            while gens:
                                for g_ in list(gens):
                                    try:
                                        next(g_)
                                    except StopIteration:
                                        gens.remove(g_)
                        S.barrier()
                        S.release(wf_r.bufs + wb_r.bufs + xc_r.bufs + stg_r.bufs + dt_r.bufs + c3_r.bufs)

            for d in range(2):
                with contextlib.ExitStack() as st:
                    if d == 1:
                        wbs = sb(st, "wbs", [128, 16, 1024], BF16)
                        with contextlib.ExitStack() as stw:
                            wtmp_r = Ring([sb(stw, f"wtmp{i}", [128, 4, 1024], F32) for i in range(2)])
                            for q in range(4):
                                wt = wtmp_r.next()
                                S.dma(wt[:], w_bs[l, q * 512:(q + 1) * 512, :].rearrange("(k p) c -> p k c", p=128), wt, writes=[wt])
                                S.op("act", lambda e: e.activation(out=wbs[:, q * 4:(q + 1) * 4, :], in_=wt[:], func=AF.Copy), [wt], [wbs])
                            S.barrier()
                            S.release(wtmp_r.bufs)
                    xbw_r = Ring([sb(st, f"xbw{i}", [128, 32, 128], BF16) for i in range(2)])
                    dts_r = Ring([sb(st, f"dts{i}", [128, 384], F32) for i in range(2)])
                    cs3_r = Ring([sb(st, f"cs3{i}", [128, 192], BF16) for i in range(2)])
                    xs_r = Ring([sb(st, f"xs{i}", [128, 2048], BF16) for i in range(2)])
                    bt_r = Ring([sb(st, f"bt{i}", [128, 1024], BF16) for i in range(2)])
                    csT_r = Ring([sb(st, f"csT{i}", [96, 128], BF16) for i in range(2)])
                    scs_r = Ring([sb(st, f"scs{i}", [128, 1024], BF16) for i in range(2)])
                    xde_r = Ring([sb(st, f"xde{i}", [128, 2048], BF16) for i in range(2)])
                    dec_r = Ring([sb(st, f"dec{i}", [128, 4, 128], BF16) for i in range(3)])
                    M_r = Ring([sb(st, f"M{i}", [128, 4, 128], BF16) for i in range(3)])
                    tmp_r = Ring([sb(st, f"tmp{i}", [128, 256], F32) for i in range(3)])
                    yacc_r = Ring([sb(st, f"yacc{i}", [128, 2048], F32) for i in range(2 + d)])
                    for yb in yacc_r.bufs:
                        yb.views = [yb] + [Buf(yb.t, yb.name + f"_v{i}") for i in range(1, 8)]
                    Hf = sb(st, "Hf", [128, 2048], F32)
                    Hb = sb(st, "Hb", [128, 2048], BF16)
                    Hfv = [Hf] + [Buf(Hf.t, f"Hf_v{i}") for i in range(1, 8)]
                    Hbv = [Hb] + [Buf(Hb.t, f"Hb_v{i}") for i in range(1, 8)]
                    p_sc = ps(st, "p_sc", [128, 1024], F32)
                    p_cb_r = Ring([ps(st, f"p_cb{i}", [128, 512], F32) for i in range(2)])
                    pbk = [ps(st, f"pbk{i}", [128, 512], F32) for i in range(3)]
                    p_T = ps(st, "p_T", [128, 1024], BF16)
                    if d == 1:
                        zs_r = Ring([sb(st, f"zs{i}", [128, 2048], BF16) for i in range(3)])
                        g0_r = Ring([sb(st, f"g0{i}", [128, 1024], BF16) for i in range(3)])
                        ynb = sb(st, "ynb", [128, 2048], BF16)
                        ynT = sb(st, "ynT", [128, 16, 128], BF16)
                        gs = sb(st, "gs", [128, 32], F32)
                        m0_r = Ring([sb(st, f"m0{i}", [128, 1024], F32) for i in range(2)])
                    S.op("pool", lambda e: e.memset(Hf[:], 0.0), [], Hfv)
                    S.op("pool", lambda e: e.memset(Hb[:], 0.0), [], Hbv)

                    order = list(range(NCH)) if d == 0 else list(range(NCH - 1, -1, -1))
                    def sweep_loads(c):
                        r0 = c * 128
                        Bn = dict(xbw=xbw_r.next(), dts=dts_r.next(), cs3=cs3_r.next(), xs=xs_r.next(), bt=bt_r.next(), yacc=yacc_r.next())
                        S.dma(Bn["dts"][:], DTS[r0:r0 + 128, :], Bn["dts"], writes=[Bn["dts"]])
                        S.dma(Bn["cs3"][:], CS3[r0:r0 + 128, :], Bn["cs3"], writes=[Bn["cs3"]])
                        if d == 0:
                            S.dma(Bn["xbw"][:], XBC[c, :, :, :], Bn["xbw"], writes=[Bn["xbw"]])
                        else:
                            Bn["zs"] = zs_r.next(); Bn["g0"] = g0_r.next()
                            S.dma(Bn["xbw"][:, 16:32, :], XBC[c, :, 16:32, :], Bn["xbw"], writes=[Bn["xbw"]])
                            S.dma(Bn["xs"][:], XS[r0:r0 + 128, :], Bn["xs"], writes=[Bn["xs"]])
                            S.dma(Bn["bt"][:], BTM[r0:r0 + 128, :], Bn["bt"], writes=[Bn["bt"]])
                            S.dma(Bn["yacc"][:], Y0[r0:r0 + 128, :], Bn["yacc"], writes=Bn["yacc"].views)
                            S.dma(Bn["zs"][:], PJ[r0:r0 + 128, PJ_Z:PJ_Z + 2048], Bn["zs"], writes=[Bn["zs"]])
                            S.dma(Bn["g0"][:], PJ[r0:r0 + 128, PJ_MG:PJ_MG + 1024], Bn["g0"], writes=[Bn["g0"]])
                        return Bn

                    for db in dec_r.bufs:
                        db.views = [db] + [Buf(db.t, db.name + f"_v{i}") for i in range(1, 4)]

                    def prep_a(c, Bn):
                        r0 = c * 128
                        xbw = Bn["xbw"]; dts = Bn["dts"]; cs3 = Bn["cs3"]; xs = Bn["xs"]; bt = Bn["bt"]; yacc = Bn["yacc"]
                        Bn["csT"] = csT_r.next(); Bn["xde"] = xde_r.next()
                        csT = Bn["csT"]; xde = Bn["xde"]
                        if d == 0:
                            for half in range(3):
                                for j in range(8):
                                    S.op("pe", lambda e, j=j: e.transpose(out=p_T[:, j * 128:(j + 1) * 128],
                                                                          in_=xbw[:, half * 8 + j, :], identity=ident[:]),
                                         [xbw, ident], [p_T], sig=(j == 7))
                                dstb = xs[:, half * 1024:(half + 1) * 1024] if half < 2 else bt[:]
                                dbuf = xs if half < 2 else bt
                                S.op("act", lambda e: e.activation(out=dstb, in_=p_T[:, :], func=AF.Copy), [p_T], [dbuf])
                            S.dma(XS[r0:r0 + 128, :], xs[:], xs, reads=[xs])
                            S.dma(BTM[r0:r0 + 128, :], bt[:], bt, reads=[bt])
                        S.op("pe", lambda e: e.transpose(out=p_T[0:96, 0:128], in_=cs3[:, d * 96:(d + 1) * 96], identity=ident[:]),
                             [cs3, ident], [p_T])
                        S.op("act", lambda e: e.activation(out=csT[:], in_=p_T[0:96, 0:128], func=AF.Copy), [p_T], [csT])
                        dtd = dts[:, d * 32:(d + 1) * 32]
                        wend = dts[:, 192 + d * 32:192 + (d + 1) * 32]
                        xs3 = xs[:, :].rearrange("p (h q) -> p h q", q=64)
                        S.op("dve", lambda e: e.tensor_tensor(out=xde[:, :].rearrange("p (h q) -> p h q", q=64), in0=xs3,
                                                               in1=bc3(wend, 64), op=ALU.mult), [xs, dts], [xde])
                        if d == 0:
                            S.op("dve", lambda e: e.tensor_tensor(out=yacc[:, :].rearrange("p (h q) -> p h q", q=64), in0=xs3,
                                                                   in1=bc3(dsk_bc, 64), op=ALU.mult), [xs, prow], yacc.views)

                    def prep_b(c, Bn):
                        xbw = Bn["xbw"]
                        Bn["scs"] = scs_r.next()
                        scs_ = Bn["scs"]
                        for g in range(8):
                            S.op("pe", lambda e, g=g: e.matmul(p_sc[:, g * 128:(g + 1) * 128], lhsT=xbw[:, 16 + g, :],
                                                               rhs=xbw[:, 24 + g, :], start=True, stop=True), [xbw], [p_sc], sig=(g == 7))
                        S.op("act", lambda e: e.activation(out=scs_[:], in_=p_sc[:, :], func=AF.Copy), [p_sc], [scs_])

                    pending_post = []
                    pend = [sweep_loads(order[0])]
                    prep_a(order[0], pend[0])
                    prep_b(order[0], pend[0])
                    for ci, c in enumerate(order):
                        r0 = c * 128
                        Bn = pend.pop(0)
                        xbw = Bn["xbw"]; dts = Bn["dts"]; cs3 = Bn["cs3"]; xs = Bn["xs"]; bt = Bn["bt"]; yacc = Bn["yacc"]
                        csT = Bn["csT"]; xde = Bn["xde"]; scs = Bn["scs"]
                        if d == 1:
                            zs = Bn["zs"]; g0 = Bn["g0"]
                        GB = [None] * 8

                        def stage_a(g):
                            G_ = dict(p_cb=p_cb_r.next(), dec=dec_r.next(), M=M_r.next(), tmp=tmp_r.next())
                            GB[g] = G_
                            p_cb = G_["p_cb"]; dec = G_["dec"]; tmp = G_["tmp"]
                            bkA = pbk[g % 2]
                            p_yo = bkA.t[:, 0:256]; p_st = bkA.t[:, 256:512]
                            gc = slice(g * 256, (g + 1) * 256)
                            S.op("pe", lambda e: e.matmul(p_yo[:, :], lhsT=xbw[:, 24 + g, :], rhs=Hb[:, gc], start=True, stop=True),
                                 [xbw, Hbv[g]], [bkA])
                            S.op("pe", lambda e: e.matmul(p_st[:, :], lhsT=bt[:, g * 128:(g + 1) * 128], rhs=xde[:, gc], start=True, stop=True),
                                 [bt, xde], [bkA])
                            S.op("pe", lambda e: e.matmul(p_cb[:, :], lhsT=ident[:], rhs=negm[:, d * 512:(d + 1) * 512], start=True, stop=False),
                                 [ident, negm], [p_cb], sig=False)
                            for r in range(4):
                                h = g * 4 + r
                                S.op("pe", lambda e, r=r, h=h: e.matmul(p_cb[:, r * 128:(r + 1) * 128], lhsT=esel[:, h * 128:(h + 1) * 128],
                                                                        rhs=csT[:], start=False, stop=(r == 3)), [esel, csT], [p_cb], sig=(r == 3))
                            for r in range(4):
                                h = g * 4 + r
                                S.op("act", lambda e, r=r, h=h: e.activation(out=dec[:, r, :], in_=p_cb[:, r * 128:(r + 1) * 128],
                                                                             func=AF.Exp, bias=dts[:, 64 + d * 32 + h:64 + d * 32 + h + 1]),
                                     [p_cb, dts], [dec.views[r]])
                            cdec = dts[:, 256 + d * 32 + g * 4:256 + d * 32 + g * 4 + 4]
                            S.op("dve", lambda e: e.tensor_tensor(out=Hf[:, gc].rearrange("p (r q) -> p r q", q=64),
                                                                   in0=Hf[:, gc].rearrange("p (r q) -> p r q", q=64),
                                                                   in1=bc3(cdec, 64), op=ALU.mult), [Hfv[g], dts], [Hfv[g]])
                            ecs = dts[:, 128 + d * 32 + g * 4:128 + d * 32 + g * 4 + 4]
                            S.op("dve", lambda e: e.tensor_tensor(out=tmp[:, :].rearrange("p (r q) -> p r q", q=64),
                                                                   in0=p_yo[:, :].rearrange("p (r q) -> p r q", q=64),
                                                                   in1=bc3(ecs, 64), op=ALU.mult), [bkA, dts], [tmp])
                            S.op("dve", lambda e: e.tensor_tensor(out=Hf[:, gc], in0=p_st[:, :], in1=Hf[:, gc], op=ALU.add),
                                 [bkA, Hfv[g]], [Hfv[g]])
                            S.op("act", lambda e: e.activation(out=Hb[:, gc], in_=Hf[:, gc], func=AF.Copy), [Hfv[g]], [Hbv[g]])
                            S.op("dve", lambda e: e.tensor_tensor(out=yacc[:, gc], in0=yacc[:, gc], in1=tmp[:], op=ALU.add),
                                 [yacc.views[g], tmp], [yacc.views[g]])

                        def stage_b(g):
                            G_ = GB[g]
                            dec = G_["dec"]; M = G_["M"]
                            bky = pbk[2]
                            p_y = bky.t[:, (g % 2) * 256:(g % 2 + 1) * 256]
                            scg = scs[:, g * 128:(g + 1) * 128].unsqueeze(1).broadcast_to([128, 4, 128])
                            S.op("dve", lambda e: e.tensor_tensor(out=M[:], in0=dec[:], in1=scg, op=ALU.mult), dec.views + [scs], [M])
                            for r in range(4):
                                h = g * 4 + r
                                S.op("pe", lambda e, r=r, h=h: e.matmul(p_y[:, r * 64:(r + 1) * 64], lhsT=M[:, r, :],
                                                                        rhs=xs[:, h * 64:(h + 1) * 64], start=True, stop=True), [M, xs], [bky], sig=(r == 3))

                        def stage_c(g):
                            bky = pbk[2]
                            p_y = bky.t[:, (g % 2) * 256:(g % 2 + 1) * 256]
                            gc = slice(g * 256, (g + 1) * 256)
                            S.op("dve", lambda e: e.tensor_tensor(out=yacc[:, gc], in0=p_y[:, :], in1=yacc[:, gc], op=ALU.add),
                                 [bky, yacc.views[g]], [yacc.views[g]])

                        for step in range(10):
                            if step < 8:
                                stage_a(step)
                            if 2 <= step <= 9:
                                stage_c(step - 2)
                            if 1 <= step <= 8:
                                stage_b(step - 1)
                            if step == 1 and ci + 1 < len(order):
                                pend.append(sweep_loads(order[ci + 1]))
                            if step >= 1:
                                for pg in list(pending_post):
                                    try:
                                        next(pg)
                                    except StopIteration:
                                        pending_post.remove(pg)
                            if step == 5 and ci + 1 < len(order):
                                prep_a(order[ci + 1], pend[0])
                        if d == 0:
                            S.dma(Y0[r0:r0 + 128, :], yacc[:], yacc, reads=yacc.views)
                        else:
                            def post(r0=r0, yacc=yacc, zs=zs, g0=g0):
                                m0 = m0_r.next()
                                yz = yacc
                                S.op("dve", lambda e: e.tensor_tensor(out=yz[:], in0=yacc[:], in1=zs[:], op=ALU.mult), yacc.views + [zs], yacc.views)
                                yield
                                for g in range(8):
                                    S.op("act", lambda e, g=g: e.activation(out=ynb[:, g * 256:(g + 1) * 256], in_=yz[:, g * 256:(g + 1) * 256],
                                                                            func=AF.Square, accum_out=gs[:, g:g + 1]), [yz], [ynb, gs])
                                yield
                                S.op("dve", lambda e: e.tensor_scalar(out=gs[:, 8:16], in0=gs[:, 0:8], scalar1=1.0 / 256, scalar2=EPS,
                                                                       op0=ALU.mult, op1=ALU.add), [gs], [gs])
                                S.op("act", lambda e: e.activation(out=gs[:, 16:24], in_=gs[:, 8:16], func=AF.Ln), [gs], [gs])
                                S.op("act", lambda e: e.activation(out=gs[:, 24:32], in_=gs[:, 16:24], func=AF.Exp, scale=-0.5), [gs], [gs])
                                yield
                                for g in range(8):
                                    S.op("dve", lambda e, g=g: e.scalar_tensor_tensor(
                                        out=ynb[:, g * 256:(g + 1) * 256], in0=yz[:, g * 256:(g + 1) * 256], scalar=gs[:, 24 + g:25 + g],
                                        in1=prow[:, G_SSD + g * 256:G_SSD + (g + 1) * 256], op0=ALU.mult, op1=ALU.mult), [yz, gs, prow], [ynb])
                                for half in range(2):
                                    yield
                                    for j in range(8):
                                        S.op("pe", lambda e, j=j: e.transpose(out=p_T[:, j * 128:(j + 1) * 128],
                                                                              in_=ynb[:, (half * 8 + j) * 128:(half * 8 + j + 1) * 128],
                                                                              identity=ident[:]), [ynb, ident], [p_T], sig=(j == 7))
                                    S.op("act", lambda e: e.activation(out=ynT[:, half * 8:(half + 1) * 8, :],
                                                                        in_=p_T[:, :].rearrange("p (k t) -> p k t", t=128), func=AF.Copy),
                                         [p_T], [ynT])
                                for nb in range(2):
                                    yield
                                    for k in range(16):
                                        S.op("pe", lambda e, k=k: e.matmul(p_sc[:, nb * 512:(nb + 1) * 512], lhsT=ynT[:, k, :],
                                                                           rhs=wbs[:, k, nb * 512:(nb + 1) * 512], start=(k == 0), stop=(k == 15)),
                                             [ynT, wbs], [p_sc], sig=(k == 15))
                                S.op("dve", lambda e: e.tensor_tensor(out=m0[:], in0=p_sc[:, :], in1=g0[:], op=ALU.mult), [p_sc, g0], [m0])
                                S.dma(M0[r0:r0 + 128, :], m0[:], m0, reads=[m0])
                            pending_post.append(post())
                        if ci + 1 < len(order):
                            for pg in list(pending_post):
                                if pg is not pending_post[-1] or d == 0:
                                    for _ in pg:
                                        pass
                                    pending_post.remove(pg)
                            prep_b(order[ci + 1], pend[0])
                    for pg in pending_post:
                        for _ in pg:
                            pass
                    S.barrier()
                    rel = xbw_r.bufs + dts_r.bufs + cs3_r.bufs + xs_r.bufs + bt_r.bufs + yacc_r.bufs
                    if d == 1:
                        rel += zs_r.bufs + g0_r.bufs + m0_r.bufs
                    S.release(rel)

            with contextlib.ExitStack() as st:
                wbg = sb(st, "wbg", [128, 8, 1024], BF16)
                wbx = sb(st, "wbx", [128, 8, 1024], BF16)
                wo = sb(st, "wo", [128, 8, 1024], BF16)
                kT = sb(st, "kT", [128, 8, 256], BF16)
                Vv = sb(st, "Vv", [128, 2, 1024], BF16)
                pA = ps(st, "pA", [128, 1024], F32)
                pB = ps(st, "pB", [128, 1024], F32)
                pC = ps(st, "pC", [128, 1024], F32)
                p_T = ps(st, "p5_T", [128, 1024], BF16)
                p_den = ps(st, "p_den", [128, 8], F32)
                with contextlib.ExitStack() as stw:
                    wtmp_r = Ring([sb(stw, f"w5tmp{i}", [128, 4, 1024], F32) for i in range(2)])
                    for (wsrc, wdst) in ((w_bg, wbg), (w_bx, wbx), (w_o, wo)):
                        for q in range(2):
                            wt = wtmp_r.next()
                            S.dma(wt[:], wsrc[l, q * 512:(q + 1) * 512, :].rearrange("(k p) c -> p k c", p=128), wt, writes=[wt])
                            S.op("act", lambda e, q=q, wdst=wdst, wt=wt: e.activation(out=wdst[:, q * 4:(q + 1) * 4, :], in_=wt[:], func=AF.Copy),
                                 [wt], [wdst])
                    memT = sb(stw, "memT", [128, 8, 256], BF16)
                    mx = sb(stw, "mx", [128, 1024], F32)
                    mjunk = sb(stw, "mjunk", [128, 1024], BF16)
                    mh = sb(stw, "mh", [128, 1024], BF16)
                    mss = sb(stw, "mss", [128, 4], F32)
                    wkb = sb(stw, "wkb", [128, 8, 2048], BF16)
                    for q in range(4):
                        wt = wtmp_r.next()
                        S.dma(wt[:, :, 0:512].rearrange("p k c -> p k c"),
                              w_kv[l, :, q * 512:(q + 1) * 512].rearrange("(k p) c -> p k c", p=128)[:, 0:4, :], wt, writes=[wt])
                        S.dma(wt[:, :, 512:1024],
                              w_kv[l, :, q * 512:(q + 1) * 512].rearrange("(k p) c -> p k c", p=128)[:, 4:8, :], wt, writes=[wt])
                        S.op("act", lambda e, q=q, wt=wt: e.activation(out=wkb[:, 0:4, q * 512:(q + 1) * 512], in_=wt[:, :, 0:512], func=AF.Copy),
                             [wt], [wkb])
                        S.op("dve", lambda e, q=q, wt=wt: e.tensor_copy(out=wkb[:, 4:8, q * 512:(q + 1) * 512], in_=wt[:, :, 512:1024]),
                             [wt], [wkb])
                    for mt in range(2):
                        S.dma(mx[:], mem_in[mt * 128:(mt + 1) * 128, :], mx, writes=[mx])
                        S.op("act", lambda e: e.activation(out=mjunk[:], in_=mx[:], func=AF.Square, accum_out=mss[:, 0:1]), [mx], [mjunk, mss])
                        S.op("dve", lambda e: e.tensor_scalar(out=mss[:, 1:2], in0=mss[:, 0:1], scalar1=1.0 / D, scalar2=EPS,
                                                               op0=ALU.mult, op1=ALU.add), [mss], [mss])
                        S.op("act", lambda e: e.activation(out=mss[:, 2:3], in_=mss[:, 1:2], func=AF.Ln), [mss], [mss])
                        S.op("act", lambda e: e.activation(out=mss[:, 3:4], in_=mss[:, 2:3], func=AF.Exp, scale=-0.5), [mss], [mss])
                        S.op("dve", lambda e: e.scalar_tensor_tensor(out=mh[:], in0=mx[:], scalar=mss[:, 3:4],
                                                                      in1=prow[:, G_MEM:G_MEM + 1024], op0=ALU.mult, op1=ALU.mult),
                             [mx, mss, prow], [mh])
                        for k in range(8):
                            S.op("pe", lambda e, k=k: e.transpose(out=p_T[:, k * 128:(k + 1) * 128], in_=mh[:, k * 128:(k + 1) * 128],
                                                                  identity=ident[:]), [mh, ident], [p_T], sig=(k == 7))
                        S.op("act", lambda e, mt=mt: e.activation(out=memT[:, :, mt * 128:(mt + 1) * 128],
                                                                  in_=p_T[:, :].rearrange("p (k t) -> p k t", t=128), func=AF.Copy),
                             [p_T], [memT])
                    for j in range(8):
                        for k in range(8):
                            S.op("pe", lambda e, k=k, j=j: e.matmul(pA[:, 0:256], lhsT=wkb[:, k, j * 128:(j + 1) * 128], rhs=memT[:, k, :],
                                                                    start=(k == 0), stop=(k == 7)), [wkb, memT], [pA], sig=(k == 7))
                        S.op("act", lambda e, j=j: e.activation(out=kT[:, j, :], in_=pA[:, 0:256], func=AF.Copy), [pA], [kT])
                    for mt in range(2):
                        for nb in range(2):
                            for k in range(8):
                                S.op("pe", lambda e, k=k, mt=mt, nb=nb: e.matmul(pB[:, 0:512], lhsT=memT[:, k, mt * 128:(mt + 1) * 128],
                                                                                 rhs=wkb[:, k, 1024 + nb * 512:1024 + (nb + 1) * 512],
                                                                                 start=(k == 0), stop=(k == 7)), [wkb, memT], [pB], sig=(k == 7))
                            S.op("act", lambda e, mt=mt, nb=nb: e.activation(out=Vv[:, mt, nb * 512:(nb + 1) * 512], in_=pB[:, 0:512],
                                                                             func=AF.Copy), [pB], [Vv])
                    S.barrier()
                    S.release(wtmp_r.bufs + [mx])

                pj_r = Ring([sb(st, f"pj{i}", [128, PJ_W - 2048], BF16) for i in range(2)])
                xq_r = Ring([sb(st, f"xq{i}", [128, 8, 128], BF16) for i in range(2)])
                m0_r = Ring([sb(st, f"m05{i}", [128, 1024], F32) for i in range(2)])
                xi_r = Ring([sb(st, f"xi5{i}", [128, 1024], F32) for i in range(2)])
                xo_r = Ring([sb(st, f"xo5{i}", [128, 1024], F32) for i in range(2)])
                bst = sb(st, "bst", [128, 16], F32)
                vt = sb(st, "vt", [128, 1024], F32)
                vn = sb(st, "vn", [128, 1024], BF16)
                svt = sb(st, "svt", [128, 1024], F32)
                yg = sb(st, "yg", [128, 1024], BF16)
                tT = sb(st, "tT", [128, 8, 128], BF16)
                Eb = sb(st, "Eb", [128, 8, 128], BF16)
                rden = sb(st, "rden", [128, 8], F32)
                ot = sb(st, "ot", [128, 1024], F32)
                yx = sb(st, "yx", [128, 1024], BF16)
                macc = sb(st, "macc", [128, 1024], F32)
                mb = sb(st, "mb", [128, 1024], BF16)
                pjunk = sb(st, "pjunk", [128, 1024], BF16)
                pss = sb(st, "pss", [128, 4], F32)

                def transpose8(src, srcbuf):
                    for k in range(8):
                        S.op("pe", lambda e, k=k: e.transpose(out=p_T[:, k * 128:(k + 1) * 128], in_=src[:, k * 128:(k + 1) * 128],
                                                              identity=ident[:]), [srcbuf, ident], [p_T], sig=(k == 7))
                    S.op("act", lambda e: e.activation(out=tT[:], in_=p_T[:, :].rearrange("p (k t) -> p k t", t=128), func=AF.Copy),
                         [p_T], [tT])

                def proj(pdst, wsb):
                    for nb in range(2):
                        for k in range(8):
                            S.op("pe", lambda e, k=k, nb=nb: e.matmul(pdst[:, nb * 512:(nb + 1) * 512], lhsT=tT[:, k, :],
                                                                      rhs=wsb[:, k, nb * 512:(nb + 1) * 512], start=(k == 0), stop=(k == 7)),
                                 [tT, wsb], [pdst], sig=(k == 7))

                def p5_loads(c):
                    r0 = c * 128
                    Bn = dict(pj=pj_r.next(), xq=xq_r.next(), m0=m0_r.next(), xi=xi_r.next())
                    S.dma(Bn["pj"][:], PJ[r0:r0 + 128, 2048:PJ_W], Bn["pj"], writes=[Bn["pj"]])
                    S.dma(Bn["xq"][:], XQ[c, :, :, :], Bn["xq"], writes=[Bn["xq"]])
                    S.dma(Bn["m0"][:], M0[r0:r0 + 128, :], Bn["m0"], writes=[Bn["m0"]])
                    S.dma(Bn["xi"][:], Xsrc[r0:r0 + 128, :], Bn["xi"], writes=[Bn["xi"]])
                    return Bn

                def views(Bn):
                    pj = Bn["pj"]
                    o = -2048
                    return dict(pj=pj, xq=Bn["xq"], m0=Bn["m0"], xi=Bn["xi"],
                                sgg=pj[:, PJ_GG + o:PJ_GG + o + 1024], uu=pj[:, PJ_GUV + o:PJ_GUV + o + 1024],
                                vv=pj[:, PJ_GUV + o + 1024:PJ_GUV + o + 2048], sxg=pj[:, PJ_XG + o:PJ_XG + o + 1024],
                                g1=pj[:, PJ_MG + o + 1024:PJ_MG + o + 2048], g2=pj[:, PJ_MG + o + 2048:PJ_MG + o + 3072])

                def head(c, Bn):
                    V_ = views(Bn)
                    pj = V_["pj"]; xq = V_["xq"]; vv = V_["vv"]
                    for h in range(4):
                        for mt in range(2):
                            for dc in range(2):
                                S.op("pe", lambda e, h=h, mt=mt, dc=dc: e.matmul(
                                    pA[:, (h * 2 + mt) * 128:(h * 2 + mt + 1) * 128], lhsT=kT[:, h * 2 + dc, mt * 128:(mt + 1) * 128],
                                    rhs=xq[:, h * 2 + dc, :], start=(dc == 0), stop=(dc == 1)), [kT, xq], [pA], sig=(h == 3 and mt == 1 and dc == 1))
                    yield
                    S.op("act", lambda e: e.activation(out=Eb[:], in_=pA[:, :].rearrange("p (j t) -> p j t", t=128), func=AF.Exp,
                                                        scale=1.0 / 16.0), [pA], [Eb])
                    yield
                    for i in range(2):
                        S.op("dve", lambda e, i=i: e.bn_stats(out=bst[:, i * 6:(i + 1) * 6], in_=vv[:, i * 512:(i + 1) * 512]), [pj], [bst])
                    S.op("dve", lambda e: e.bn_aggr(out=bst[:, 12:14], in_=bst[:, 0:12]), [bst], [bst])
                    S.op("dve", lambda e: e.tensor_scalar(out=bst[:, 14:15], in0=bst[:, 13:14], scalar1=EPS, scalar2=None, op0=ALU.add),
                         [bst], [bst])
                    S.op("act", lambda e: e.activation(out=bst[:, 14:15], in_=bst[:, 14:15], func=AF.Ln), [bst], [bst])
                    S.op("act", lambda e: e.activation(out=bst[:, 15:16], in_=bst[:, 14:15], func=AF.Exp, scale=-0.5), [bst], [bst])
                    yield
                    S.op("dve", lambda e: e.tensor_scalar(out=vt[:], in0=vv, scalar1=bst[:, 12:13], scalar2=bst[:, 15:16],
                                                           op0=ALU.subtract, op1=ALU.mult), [pj, bst], [vt])
                    S.op("dve", lambda e: e.tensor_tensor(out=vt[:], in0=vt[:], in1=prow[:, G_LNG:G_LNG + 1024], op=ALU.mult), [vt, prow], [vt])
                    S.op("dve", lambda e: e.tensor_tensor(out=vn[:], in0=vt[:], in1=prow[:, G_LNB:G_LNB + 1024], op=ALU.add), [vt, prow], [vn])
                    yield
                    for g in range(8):
                        S.op("pe", lambda e, g=g: e.matmul(pB[:, g * 128:(g + 1) * 128], lhsT=wsT[:, g * 128:(g + 1) * 128],
                                                           rhs=vn[:, g * 128:(g + 1) * 128], start=True, stop=True), [wsT, vn], [pB], sig=(g == 7))

                def mid(c, Bn):
                    V_ = views(Bn)
                    pj = V_["pj"]; m0 = V_["m0"]
                    for h in range(4):
                        for mt in range(2):
                            S.op("pe", lambda e, h=h, mt=mt: e.matmul(pC[:, h * 256:(h + 1) * 256], lhsT=Eb[:, h * 2 + mt, :],
                                                                      rhs=Vv[:, mt, h * 256:(h + 1) * 256], start=(mt == 0), stop=(mt == 1)),
                                 [Eb, Vv], [pC], sig=(h == 3 and mt == 1))
                    for h in range(4):
                        for mt in range(2):
                            S.op("pe", lambda e, h=h, mt=mt: e.matmul(p_den[:, h * 2:h * 2 + 2], lhsT=Eb[:, h * 2 + mt, :], rhs=onesb[:],
                                                                      start=(mt == 0), stop=(mt == 1)), [Eb, onesb], [p_den], sig=(h == 3 and mt == 1))
                    yield
                    S.op("dve", lambda e: e.tensor_tensor(out=svt[:, :].rearrange("p (g q) -> p g q", q=128),
                                                           in0=pB[:, :].rearrange("p (g q) -> p g q", q=128),
                                                           in1=bc3(bsT[:, 0:8], 128), op=ALU.add), [pB, bsT], [svt])
                    S.op("dve", lambda e: e.tensor_tensor(out=svt[:], in0=svt[:], in1=V_["uu"], op=ALU.mult), [svt, pj], [svt])
                    S.op("dve", lambda e: e.tensor_tensor(out=yg[:], in0=svt[:], in1=V_["sgg"], op=ALU.mult), [svt, pj], [yg])
                    S.op("dve", lambda e: e.reciprocal(out=rden[:], in_=p_den[:, :]), [p_den], [rden])
                    rd4 = rden[:, :].rearrange("p (h two) -> p h two", two=2)[:, :, 0:1].broadcast_to([128, 4, 256])
                    S.op("dve", lambda e: e.tensor_tensor(out=ot[:, :].rearrange("p (h q) -> p h q", q=256),
                                                           in0=pC[:, :].rearrange("p (h q) -> p h q", q=256), in1=rd4, op=ALU.mult),
                         [pC, rden], [ot])
                    S.op("dve", lambda e: e.tensor_tensor(out=yx[:], in0=ot[:], in1=V_["sxg"], op=ALU.mult), [ot, pj], [yx])
                    yield
                    transpose8(yg, yg)
                    yield
                    proj(pA, wbg)
                    yield
                    S.op("dve", lambda e: e.tensor_tensor(out=macc[:], in0=pA[:, :], in1=V_["g1"], op=ALU.mult), [pA, pj], [macc])
                    S.op("dve", lambda e: e.tensor_tensor(out=macc[:], in0=macc[:], in1=m0[:], op=ALU.add), [macc, m0], [macc])
                    transpose8(yx, yx)
                    proj(pB, wbx)
                    S.op("dve", lambda e: e.tensor_tensor(out=ot[:], in0=pB[:, :], in1=V_["g2"], op=ALU.mult), [pB, pj], [ot])
                    S.op("dve", lambda e: e.tensor_tensor(out=mb[:], in0=ot[:], in1=macc[:], op=ALU.add), [ot, macc], [mb])

                def tail(c, Bn):
                    r0 = c * 128
                    xi = Bn["xi"]; xo = xo_r.next()
                    transpose8(mb, mb)
                    proj(pC, wo)
                    S.op("act", lambda e: e.activation(out=pjunk[:], in_=pC[:, :], func=AF.Square, accum_out=pss[:, 0:1]), [pC], [pjunk, pss])
                    S.op("dve", lambda e: e.tensor_scalar(out=pss[:, 1:2], in0=pss[:, 0:1], scalar1=1.0 / D, scalar2=EPS,
                                                           op0=ALU.mult, op1=ALU.add), [pss], [pss])
                    S.op("act", lambda e: e.activation(out=pss[:, 2:3], in_=pss[:, 1:2], func=AF.Ln), [pss], [pss])
                    S.op("act", lambda e: e.activation(out=pss[:, 3:4], in_=pss[:, 2:3], func=AF.Exp, scale=-0.5), [pss], [pss])
                    S.op("dve", lambda e: e.scalar_tensor_tensor(out=xo[:], in0=pC[:, :], scalar=pss[:, 3:4],
                                                                  in1=prow[:, G_POST:G_POST + 1024], op0=ALU.mult, op1=ALU.mult),
                         [pC, pss, prow], [xo])
                    S.op("dve", lambda e: e.tensor_tensor(out=xo[:], in0=xo[:], in1=xi[:], op=ALU.add), [xo, xi], [xo])
                    S.dma(Xdst[r0:r0 + 128, :], xo[:], xo, reads=[xo])

                def drain(gen):
                    for _ in gen:
                        pass

                cur = p5_loads(0)
                drain(head(0, cur))
                for c in range(NCH):
                    nxt = p5_loads(c + 1) if c + 1 < NCH else None
                    gm = mid(c, cur)
                    gh = head(c + 1, nxt) if nxt is not None else iter(())
                    for _ in range(4):
                        next(gm, None)
                        next(gh, None)
                    drain(gm)
                    drain(gh)
                    tail(c, cur)
                    cur = nxt
                S.barrier()
                S.release(pj_r.bufs + xq_r.bufs + m0_r.bufs + xi_r.bufs + xo_r.bufs)
        S.barrier()
    return nc


def host_consts():
    ident = np.eye(128, dtype=np.float32)
    k = np.arange(128)[:, None]
    t = np.arange(128)[None, :]
    tri0 = (k <= t).astype(np.float32)
    tri1 = (k >= t).astype(np.float32)
    esel = np.zeros((96, 32, 128), np.float32)
    for h in range(32):
        for j in range(3):
            esel[j * 32 + h, h, :] = 1.0
    nm0 = np.where(k > t, -30000.0, 0.0).astype(np.float32)
    nm1 = np.where(k < t, -30000.0, 0.0).astype(np.float32)
    c_nm = np.concatenate([np.tile(nm0, (1, 4)), np.tile(nm1, (1, 4))], axis=1)
    return {"c_ident": ident, "c_tri0": tri0, "c_tri1": tri1, "c_esel": esel.reshape(96, 32 * 128), "c_nm": c_nm}


def host_params(inp, depth):
    f = lambda a: np.asarray(a, dtype=np.float32)
    p_row = np.zeros((depth, 1, 9 * 1024), np.float32)
    p_row[:, 0, 0:1024] = f(inp["norm_pre_g"])[:depth]
    p_row[:, 0, 1024:3072] = f(inp["ssd_norm_g"])[:depth]
    p_row[:, 0, 3072:4096] = f(inp["gmlp_ln_g"])[:depth]
    p_row[:, 0, 4096:5120] = f(inp["gmlp_ln_b"])[:depth]
    p_row[:, 0, 5120:6144] = f(inp["mem_norm_g"])[:depth]
    p_row[:, 0, 6144:7168] = f(inp["norm_post_g"])[:depth]
    p_row[:, 0, 7168:7232] = f(inp["dt_bias"])[:depth].reshape(depth, 64)
    p_row[:, 0, 7232:7296] = f(inp["a_log"])[:depth].reshape(depth, 64)
    p_row[:, 0, 7296:7328] = f(inp["d_skip"])[:depth]
    cw = f(inp["conv_w"])[:depth]
    p_convw = np.ascontiguousarray(cw.reshape(depth, 5, 32, 128).transpose(0, 3, 2, 1)).reshape(depth, 128, 160)
    cb = f(inp["conv_b"])[:depth]
    p_convb = np.ascontiguousarray(cb.reshape(depth, 32, 128).transpose(0, 2, 1))
    ws = f(inp["w_spatial"])[:depth]
    p_wsT = np.ascontiguousarray(ws.transpose(0, 3, 1, 2)).reshape(depth, 128, 1024)
    bs = f(inp["b_spatial"])[:depth]
    p_bsT = np.ascontiguousarray(bs.transpose(0, 2, 1))
    return {"p_row": p_row, "p_convw": p_convw, "p_convb": p_convb, "p_wsT": p_wsT, "p_bsT": p_bsT}


_NC_CACHE = {}


def kernel(**inputs):
    x = np.asarray(inputs["x"], dtype=np.float32)
    B, L, _ = x.shape
    depth = inputs["w_in"].shape[0]
    key = (L, depth)
    if key not in _NC_CACHE:
        _NC_CACHE[key] = build(L, depth)
    nc = _NC_CACHE[key]
    shared = {}
    shared.update(host_consts())
    shared.update(host_params(inputs, depth))
    for n in ("w_in", "w_kv", "w_br_ssd", "w_br_gmlp", "w_br_xattn", "w_out"):
        shared[n] = np.ascontiguousarray(np.asarray(inputs[n], dtype=np.float32))
    mem = np.asarray(inputs["mem"], dtype=np.float32)
    in_maps = []
    for b in range(B):
        m = dict(shared)
        m["x"] = np.ascontiguousarray(x[b])
        m["mem"] = np.ascontiguousarray(mem[b])
        in_maps.append(m)
    res = run_bass_kernel_spmd(nc, in_maps, core_ids=list(range(B)))
    return np.stack([np.asarray(res.results[b]["out"], dtype=np.float32) for b in range(B)], axis=0)
```
